# Optimizing a Trainium2 kernel written in Bass

```python
import jax, jax.numpy as jnp
from jax import lax
import numpy as np

D_MODEL = 2048
BATCH = 8
SEQ = 4096
DEPTH = 2
DEC_BATCH = 4
DEC_SEQ = 4096
PAST_LEN = 128

MLA_HEADS = 8
QK_NOPE_DIM = 128
QK_ROPE_DIM = 64
V_HEAD_DIM = 128
Q_LORA_RANK = 512
KV_LORA_RANK = 256
ROPE_THETA = 10000.0
Q_BLOCK = 128
MLA_DIM = MLA_HEADS * V_HEAD_DIM
RWKV_HEADS = 16
RWKV_HEAD_DIM = 64
RWKV_DIM = RWKV_HEADS * RWKV_HEAD_DIM
DECAY_LORA = 64
ICLR_LORA = 64
GATE_LORA = 160
N_DIR = 2
RWKV_GN_EPS = 64e-5
D_FF = 5632
CONV_WIDTH = 3
N_BRANCH = 2
N_MOD = 6
NORM_EPS = 1e-6

MLA_IN = Q_LORA_RANK + KV_LORA_RANK + QK_ROPE_DIM
RWKV_IN = 3 * RWKV_DIM + N_DIR * DECAY_LORA + N_DIR * ICLR_LORA + GATE_LORA
D_IN = MLA_IN + RWKV_IN + N_BRANCH * D_MODEL
IN_OFFSETS = (Q_LORA_RANK, Q_LORA_RANK + KV_LORA_RANK, MLA_IN, MLA_IN + RWKV_IN)
RW_OFFSETS = (RWKV_DIM, 2 * RWKV_DIM, 3 * RWKV_DIM, 3 * RWKV_DIM + N_DIR * DECAY_LORA,
              3 * RWKV_DIM + N_DIR * (DECAY_LORA + ICLR_LORA))

kernel_name = "hybrid_mla_rwkv7_bidir_encoder"


def _rmsnorm(x, g):
    xf = x.astype(jnp.float32)
    y = xf * lax.rsqrt(jnp.mean(xf * xf, axis=-1, keepdims=True) + NORM_EPS)
    return (y * g.astype(jnp.float32)).astype(x.dtype)


def _rope_tables(seq_len, dtype):
    inv = 1.0 / (ROPE_THETA ** (jnp.arange(0, QK_ROPE_DIM, 2, dtype=jnp.float32) / QK_ROPE_DIM))
    ang = jnp.arange(seq_len, dtype=jnp.float32)[:, None] * inv[None, :]
    return jnp.cos(ang).astype(dtype), jnp.sin(ang).astype(dtype)


def _rope(x, cos, sin):
    x1, x2 = jnp.split(x, 2, axis=-1)
    return jnp.concatenate([x1 * cos - x2 * sin, x2 * cos + x1 * sin], axis=-1)


def _shift_centered(x):
    xp = jnp.pad(x, ((0, 0), (1, 1), (0, 0)))
    return xp[:, :-2], xp[:, 2:]


def _mla_branch(q_down, kv_down, k_rope, cos, sin, q_norm_g, w_uq, kv_norm_g, w_ukv, w_o_att):
    B, S, _ = q_down.shape
    q = (_rmsnorm(q_down, q_norm_g) @ w_uq).reshape(B, S, MLA_HEADS, QK_NOPE_DIM + QK_ROPE_DIM)
    q_nope, q_rope = q[..., :QK_NOPE_DIM], q[..., QK_NOPE_DIM:]
    q_rope = _rope(q_rope, cos[:, None, :], sin[:, None, :])
    kv = (_rmsnorm(kv_down, kv_norm_g) @ w_ukv).reshape(B, S, MLA_HEADS, QK_NOPE_DIM + V_HEAD_DIM)
    k_nope, v = kv[..., :QK_NOPE_DIM], kv[..., QK_NOPE_DIM:]
    k_rope = _rope(k_rope, cos, sin)
    scale = (QK_NOPE_DIM + QK_ROPE_DIM) ** -0.5
    nb = S // Q_BLOCK
    qn_blocks = q_nope.reshape(B, nb, Q_BLOCK, MLA_HEADS, QK_NOPE_DIM).swapaxes(0, 1)
    qr_blocks = q_rope.reshape(B, nb, Q_BLOCK, MLA_HEADS, QK_ROPE_DIM).swapaxes(0, 1)

    def attend(blk):
        qn, qr = blk
        s = (jnp.einsum('bqhd,bkhd->bhqk', qn, k_nope, preferred_element_type=jnp.float32)
             + jnp.einsum('bqhr,bkr->bhqk', qr, k_rope, preferred_element_type=jnp.float32))
        p = jax.nn.softmax(s * scale, axis=-1).astype(v.dtype)
        return jnp.einsum('bhqk,bkhd->bqhd', p, v)

    o = lax.map(attend, (qn_blocks, qr_blocks))
    o = o.swapaxes(0, 1).reshape(B, S, MLA_DIM)
    return o @ w_o_att


def _orient(t):
    return jnp.concatenate([t[:1], jnp.flip(t[1:], axis=2)], axis=0)


def _rwkv7_step(state, inp):
    r_t, w_t, k_t, v_t, kk_t, a_t = inp
    s_kk = jnp.einsum('dbhvk,dbhk->dbhv', state, kk_t)
    state = (state * w_t[..., None, :]
             - s_kk[..., None] * (kk_t * a_t)[..., None, :]
             + v_t[..., None] * k_t[..., None, :])
    return state, jnp.einsum('dbhvk,dbhk->dbhv', state, r_t)


def _rwkv7_branch(p, mu, w0, w_decay_up, a0, w_iclr_up, w_gate_up, k_k, k_a, r_k, gn_g, gn_b, w_o_rwkv):
    f32 = jnp.float32
    B, S, _ = p.shape
    H, N = RWKV_HEADS, RWKV_HEAD_DIM
    p_prev, p_next = _shift_centered(p)
    p = p + mu[0] * (p_prev - p) + mu[1] * (p_next - p)
    r, k, v, wd, ad, gd = jnp.split(p.astype(f32), RW_OFFSETS, axis=-1)
    wd = wd.reshape(B, S, N_DIR, DECAY_LORA)
    ad = ad.reshape(B, S, N_DIR, ICLR_LORA)
    w_raw = w0[:, None, None, :] + jnp.einsum('bsdl,dlc->dbsc', jnp.tanh(wd), w_decay_up.astype(f32))
    decay = jnp.exp(-jnp.exp(-jax.nn.softplus(-w_raw) - 0.5))
    a = jax.nn.sigmoid(a0[:, None, None, :] + jnp.einsum('bsdl,dlc->dbsc', ad, w_iclr_up.astype(f32)))
    g = jax.nn.sigmoid(gd) @ w_gate_up.astype(f32)
    kk = (k * k_k).reshape(B, S, H, N)
    kk = kk / jnp.maximum(jnp.linalg.norm(kk, axis=-1, keepdims=True), 1e-12)
    k_dir = k[None] * (1.0 + (a - 1.0) * k_a)
    rh, vh = r.reshape(B, S, H, N), v.reshape(B, S, H, N)
    heads = lambda t: t.reshape(N_DIR, B, S, H, N)
    both = lambda t: jnp.stack([t, t], axis=0)
    xs = tuple(jnp.moveaxis(_orient(t), 2, 0) for t in
               (both(rh), heads(decay), heads(k_dir), both(vh), both(kk), heads(a)))
    state0 = jnp.zeros((N_DIR, B, H, N, N), f32)
    _, o = lax.scan(_rwkv7_step, state0, xs)
    o = _orient(jnp.moveaxis(o, 0, 2))
    o = o[0] + o[1]
    mean = jnp.mean(o, axis=-1, keepdims=True)
    var = jnp.mean(jnp.square(o - mean), axis=-1, keepdims=True)
    o = ((o - mean) * lax.rsqrt(var + RWKV_GN_EPS)).reshape(B, S, RWKV_DIM) * gn_g + gn_b
    bonus = jnp.einsum('bshn,dbshn,dhn->bsh', rh, heads(k_dir), r_k.astype(f32))[..., None] * vh
    o = (o + bonus.reshape(B, S, RWKV_DIM)) * g
    return o.astype(p.dtype) @ w_o_rwkv


def _conv_ffn(h, w_up, conv_w, conv_b, w_down):
    a, b = jnp.split(h @ w_up, 2, axis=-1)
    a_prev, a_next = _shift_centered(a)
    a = a_prev * conv_w[0] + a * conv_w[1] + a_next * conv_w[2] + conv_b
    return (jax.nn.silu(a) * b) @ w_down


def _layer(x, c, cos, sin, l, P):
    mod = jax.nn.silu(c) @ P['ada_w'][l] + P['ada_b'][l]
    sh1, sc1, gt1, sh2, sc2, gt2 = jnp.split(mod[:, None, :], N_MOD, axis=-1)
    h = _rmsnorm(x, P['norm_mix_g'][l]) * (1.0 + sc1) + sh1
    proj = h @ P['w_in'][l]
    q_down, kv_down, k_rope, rw, gates = jnp.split(proj, IN_OFFSETS, axis=-1)
    att = _mla_branch(q_down, kv_down, k_rope, cos, sin, P['q_norm_g'][l], P['w_uq'][l],
                      P['kv_norm_g'][l], P['w_ukv'][l], P['w_o_att'][l])
    rwk = _rwkv7_branch(rw, P['rwkv_mu'][l], P['rwkv_w0'][l], P['rwkv_w_decay_up'][l], P['rwkv_a0'][l],
                        P['rwkv_w_iclr_up'][l], P['rwkv_w_gate_up'][l], P['rwkv_k_k'][l], P['rwkv_k_a'][l],
                        P['rwkv_r_k'][l], P['rwkv_gn_g'][l], P['rwkv_gn_b'][l], P['w_o_rwkv'][l])
    g_att, g_rwk = jnp.split(jax.nn.sigmoid(gates), N_BRANCH, axis=-1)
    mixed = (g_att * att + g_rwk * rwk) @ P['w_out'][l]
    x = x + gt1 * mixed
    h = _rmsnorm(x, P['norm_ffn_g'][l]) * (1.0 + sc2) + sh2
    x = x + gt2 * _conv_ffn(h, P['w_ffn_up'][l], P['conv_w'][l], P['conv_b'][l], P['w_ffn_down'][l])
    return x


def _trunk(x, c, P):
    cos, sin = _rope_tables(x.shape[1], x.dtype)
    for l in range(DEPTH):
        x = _layer(x, c, cos, sin, l, P)
    return _rmsnorm(x, P['final_norm_g'])


def setup_inputs(seed: int = 0) -> dict:
    key = jax.random.key(seed)
    ks = jax.random.split(key, 32)
    nrm = lambda k, shape, s: jax.random.normal(k, shape, jnp.float32) * s
    gain = lambda k, shape: 1.0 + 0.02 * jax.random.normal(k, shape, jnp.float32)
    L = DEPTH
    conv_w = nrm(ks[27], (L, CONV_WIDTH, D_FF), 0.3).at[:, 1].add(1.0)
    return {
        'x_prompt': nrm(ks[0], (BATCH, SEQ, D_MODEL), 1.0),
        'x_sample': nrm(ks[1], (DEC_BATCH, DEC_SEQ, D_MODEL), 1.0),
        'c_prompt': nrm(ks[2], (BATCH, D_MODEL), 1.0),
        'c_sample': nrm(ks[3], (DEC_BATCH, D_MODEL), 1.0),
        'ada_w': nrm(ks[4], (L, D_MODEL, N_MOD * D_MODEL), 0.5 * D_MODEL ** -0.5),
        'ada_b': nrm(ks[5], (L, N_MOD * D_MODEL), 0.02),
        'norm_mix_g': gain(ks[6], (L, D_MODEL)),
        'w_in': nrm(ks[7], (L, D_MODEL, D_IN), D_MODEL ** -0.5),
        'q_norm_g': gain(ks[8], (L, Q_LORA_RANK)),
        'w_uq': nrm(ks[9], (L, Q_LORA_RANK, MLA_HEADS * (QK_NOPE_DIM + QK_ROPE_DIM)), Q_LORA_RANK ** -0.5),
        'kv_norm_g': gain(ks[10], (L, KV_LORA_RANK)),
        'w_ukv': nrm(ks[11], (L, KV_LORA_RANK, MLA_HEADS * (QK_NOPE_DIM + V_HEAD_DIM)), KV_LORA_RANK ** -0.5),
        'w_o_att': nrm(ks[12], (L, MLA_DIM, D_MODEL), MLA_DIM ** -0.5),
        'rwkv_mu': jax.random.uniform(ks[13], (L, 2, RWKV_IN), jnp.float32, 0.0, 0.5),
        'rwkv_w0': jax.random.uniform(ks[14], (L, N_DIR, RWKV_DIM), jnp.float32, -6.0, -1.0),
        'rwkv_w_decay_up': nrm(ks[15], (L, N_DIR, DECAY_LORA, RWKV_DIM), 0.1 * DECAY_LORA ** -0.5),
        'rwkv_a0': nrm(ks[16], (L, N_DIR, RWKV_DIM), 0.5),
        'rwkv_w_iclr_up': nrm(ks[17], (L, N_DIR, ICLR_LORA, RWKV_DIM), ICLR_LORA ** -0.5),
        'rwkv_w_gate_up': nrm(ks[18], (L, GATE_LORA, RWKV_DIM), GATE_LORA ** -0.5),
        'rwkv_k_k': 0.85 + nrm(ks[19], (L, RWKV_DIM), 0.05),
        'rwkv_k_a': gain(ks[20], (L, RWKV_DIM)),
        'rwkv_r_k': nrm(ks[21], (L, N_DIR, RWKV_HEADS, RWKV_HEAD_DIM), 0.1),
        'rwkv_gn_g': gain(ks[22], (L, RWKV_DIM)),
        'rwkv_gn_b': nrm(ks[23], (L, RWKV_DIM), 0.02),
        'w_o_rwkv': nrm(ks[24], (L, RWKV_DIM, D_MODEL), RWKV_DIM ** -0.5),
        'w_out': nrm(ks[25], (L, D_MODEL, D_MODEL), D_MODEL ** -0.5),
        'norm_ffn_g': gain(ks[26], (L, D_MODEL)),
        'w_ffn_up': nrm(ks[28], (L, D_MODEL, 2 * D_FF), D_MODEL ** -0.5),
        'conv_w': conv_w,
        'conv_b': nrm(ks[29], (L, D_FF), 0.02),
        'w_ffn_down': nrm(ks[30], (L, D_FF, D_MODEL), D_FF ** -0.5),
        'final_norm_g': gain(ks[31], (D_MODEL,)),
    }


def reference(x_prompt, x_sample, c_prompt, c_sample, ada_w, ada_b, norm_mix_g, w_in, q_norm_g, w_uq,
              kv_norm_g, w_ukv, w_o_att, rwkv_mu, rwkv_w0, rwkv_w_decay_up, rwkv_a0, rwkv_w_iclr_up,
              rwkv_w_gate_up, rwkv_k_k, rwkv_k_a, rwkv_r_k, rwkv_gn_g, rwkv_gn_b, w_o_rwkv, w_out,
              norm_ffn_g, w_ffn_up, conv_w, conv_b, w_ffn_down, final_norm_g):
    P = dict(ada_w=ada_w, ada_b=ada_b, norm_mix_g=norm_mix_g, w_in=w_in, q_norm_g=q_norm_g, w_uq=w_uq,
             kv_norm_g=kv_norm_g, w_ukv=w_ukv, w_o_att=w_o_att, rwkv_mu=rwkv_mu, rwkv_w0=rwkv_w0,
             rwkv_w_decay_up=rwkv_w_decay_up, rwkv_a0=rwkv_a0, rwkv_w_iclr_up=rwkv_w_iclr_up,
             rwkv_w_gate_up=rwkv_w_gate_up, rwkv_k_k=rwkv_k_k, rwkv_k_a=rwkv_k_a, rwkv_r_k=rwkv_r_k,
             rwkv_gn_g=rwkv_gn_g, rwkv_gn_b=rwkv_gn_b, w_o_rwkv=w_o_rwkv, w_out=w_out,
             norm_ffn_g=norm_ffn_g, w_ffn_up=w_ffn_up, conv_w=conv_w, conv_b=conv_b,
             w_ffn_down=w_ffn_down, final_norm_g=final_norm_g)
    y_prompt = _trunk(x_prompt, c_prompt, P)
    y_sample = _trunk(x_sample, c_sample, P)
    return (y_prompt, y_sample)
```

```python
import numpy as np
from contextlib import ExitStack
import concourse.bass as bass
import concourse.mybir as mybir
from concourse.bass_utils import run_bass_kernel_spmd

F32 = mybir.dt.float32
BF16 = mybir.dt.bfloat16
AF = mybir.ActivationFunctionType
ALU = mybir.AluOpType
AX = mybir.AxisListType

D = 2048
KD = 16
DFF = 5632
NH = 8
DIN = 8416
C = 128


class Buf:
    __slots__ = ("w", "r")

    def __init__(self):
        self.w = None
        self.r = {}


class TT:
    __slots__ = ("t", "b")

    def __init__(self, t):
        self.t = t
        self.b = Buf()


class Sched:
    def __init__(self, nc, es, ndma=12):
        self.nc = nc
        self.eng = {'pe': nc.tensor, 'act': nc.scalar, 'dve': nc.vector, 'pool': nc.gpsimd, 'sp': nc.sync}
        self.sem = {}
        self.cnt = {}
        self.known = {k: {} for k in self.eng}
        for k in ['pe', 'act', 'dve', 'pool']:
            self.sem[k] = es.enter_context(nc.semaphore("s_" + k))
            self.cnt[k] = 0
        self.dpool, self.dcnt, self.drr = {}, {}, {}
        for q in ['sp', 'pool']:
            self.dpool[q] = [es.enter_context(nc.semaphore(f"d_{q}_{i}")) for i in range(ndma)]
            self.dcnt[q] = [0] * ndma
            self.drr[q] = 0
        self.rots = {}
        self.stopped = False
        _SREF[0] = self

    def rot(self, name, items):
        i = self.rots.get(name, 0)
        self.rots[name] = i + 1
        return items[i % len(items)]

    def _wait(self, e, deps):
        eng = self.eng[e]
        kn = self.known[e]
        for (sk, val) in deps:
            if kn.get(sk, 0) >= val:
                continue
            if sk[0] == 'c':
                if sk[1] == e and e == 'pe':
                    continue
                s = self.sem[sk[1]]
            else:
                s = self.dpool[sk[1]][sk[2]]
            eng.wait_ge(s, val)
            kn[sk] = val

    @staticmethod
    def _deps(reads, writes):
        deps = []
        for b in reads:
            if b.w is not None:
                deps.append(b.w)
        for b in writes:
            if b.w is not None:
                deps.append(b.w)
            deps.extend(b.r.items())
        return deps

    @staticmethod
    def _commit(ev, reads, writes):
        for b in reads:
            if b.r.get(ev[0], 0) < ev[1]:
                b.r[ev[0]] = ev[1]
        for b in writes:
            b.w = ev
            b.r = {}

    def op(self, e, fn, reads=(), writes=()):
        if self.stopped:
            return
        reads = [x.b if isinstance(x, TT) else x for x in reads]
        writes = [x.b if isinstance(x, TT) else x for x in writes]
        self._wait(e, self._deps(reads, writes))
        ins = fn()
        self.cnt[e] += 1
        ins.then_inc(self.sem[e], 1)
        ev = (('c', e), self.cnt[e])
        self._commit(ev, reads, writes)

    def dma(self, q, out, in_, reads=(), writes=(), **kw):
        if self.stopped:
            return
        reads = [x.b if isinstance(x, TT) else x for x in reads]
        writes = [x.b if isinstance(x, TT) else x for x in writes]
        i = self.drr[q]
        self.drr[q] = (i + 1) % len(self.dpool[q])
        sk = ('d', q, i)
        deps = self._deps(reads, writes)
        if self.dcnt[q][i] > 0:
            deps.append((sk, self.dcnt[q][i]))
        self._wait(q, deps)
        ins = self.eng[q].dma_start(out=out, in_=in_, **kw)
        self.dcnt[q][i] += 16
        ins.then_inc(self.dpool[q][i], 16)
        self._commit((sk, self.dcnt[q][i]), reads, writes)

    def barrier(self):
        if self.stopped:
            return
        evs = [(('c', e), self.cnt[e]) for e in self.cnt if self.cnt[e] > 0]
        for q in self.dpool:
            for i in range(len(self.dpool[q])):
                if self.dcnt[q][i] > 0:
                    evs.append((('d', q, i), self.dcnt[q][i]))
        for e in self.eng:
            self._wait(e, evs)

    def finish(self):
        for q in self.dpool:
            for i, s in enumerate(self.dpool[q]):
                if self.dcnt[q][i] > 0:
                    self.nc.sync.wait_ge(s, self.dcnt[q][i])

    def mm(self, out, pairs, reads, writes, start=True, stop=True):
        nc = self.nc

        def fn():
            ins = None
            n = len(pairs)
            for i, (l, r) in enumerate(pairs):
                ins = nc.tensor.matmul(out, lhsT=l, rhs=r, start=(start and i == 0), stop=(stop and i == n - 1))
            return ins
        self.op('pe', fn, reads, writes)

    def tr(self, out, in_, ident, reads, writes):
        nc = self.nc
        self.op('pe', lambda: nc.tensor.transpose(out, in_, ident), reads, writes)

    def act(self, out, in_, func, reads, writes, **kw):
        nc = self.nc
        self.op('act', lambda: nc.scalar.activation(out=out, in_=in_, func=func, **kw), reads, writes)

    def tt(self, out, in0, in1, op, reads, writes, e='dve'):
        eng = self.eng[e]
        self.op(e, lambda: eng.tensor_tensor(out=out, in0=in0, in1=in1, op=op), reads, writes)

    def ts(self, out, in0, s1, s2, op0, op1, reads, writes):
        nc = self.nc
        if s2 is None:
            self.op('dve', lambda: nc.vector.tensor_scalar(out=out, in0=in0, scalar1=s1, scalar2=None, op0=op0), reads, writes)
        else:
            self.op('dve', lambda: nc.vector.tensor_scalar(out=out, in0=in0, scalar1=s1, scalar2=s2, op0=op0, op1=op1), reads, writes)

    def stt(self, out, in0, scalar, in1, op0, op1, reads, writes):
        nc = self.nc
        self.op('dve', lambda: nc.vector.scalar_tensor_tensor(out=out, in0=in0, scalar=scalar, in1=in1, op0=op0, op1=op1), reads, writes)

    def cp(self, out, in_, reads, writes, e=None):
        nc = self.nc
        if e is None:
            e = self.rot('cp', ['act', 'dve'])
        if e == 'act':
            self.op('act', lambda: nc.scalar.copy(out=out, in_=in_), reads, writes)
        else:
            self.op('dve', lambda: nc.vector.tensor_copy(out=out, in_=in_), reads, writes)

    def rsqrt(self, out, in_, scale, eps_ap, reads, writes):
        self.act(out, in_, AF.Sqrt, reads, writes, scale=scale, bias=eps_ap)
        nc = self.nc
        self.op('dve', lambda: nc.vector.reciprocal(out=out, in_=out), writes, writes)


def _col_layout():
    items = [("norm_mix_g", 16), ("norm_ffn_g", 16), ("final_norm_g", 16), ("ada_b", 96), ("q_norm_g", 4),
             ("kv_norm_g", 2), ("mu0", 28), ("mu1", 28), ("w0_0", 8), ("w0_1", 8), ("a0_0", 8), ("a0_1", 8),
             ("k_k", 8), ("k_a", 8), ("r_k_0", 8), ("r_k_1", 8), ("cw0", 44), ("cw1", 44), ("cw2", 44), ("cb", 44)]
    co, off = {}, 0
    for n, w in items:
        co[n] = off
        off += w
    return co, off


CO, NCOL = _col_layout()
CST = {"ident": 0, "bd": 128, "lt": 256, "le": 384, "gt": 512, "ge": 640, "hsel": 768, "swap": 770}
NCST = 770 + 64


def _in_chunks():
    ch = []
    for j in range(4):
        ch.append(("q", j * 128, 128, j))
    for j in range(2):
        ch.append(("kv", 512 + j * 128, 128, j))
    ch.append(("rope", 768, 64, 0))
    for j in range(27):
        ch.append(("rw", 832 + j * 128, 128, j))
    ch.append(("rw", 832 + 27 * 128, 32, 27))
    for j in range(32):
        ch.append(("gate", 4320 + j * 128, 128, j))
    slabs, cur = [], []
    for c in ch:
        if cur and ((c[1] + c[2] - cur[0][1] > 512) or (c[0] != cur[-1][0] and c[0] in ("rw", "gate"))):
            slabs.append(cur)
            cur = []
        cur.append(c)
    slabs.append(cur)
    return slabs


class _Stop(Exception):
    pass


_KSTOP = [99]


_SREF = [None]


def _ck(n):
    if _KSTOP[0] <= n:
        _SREF[0].stopped = True


def build(S_, NSEQ, L):
    NB = S_ // 512
    NT = S_ // 128
    NCH = S_ // C
    nc = bass.Bass("TRN2", target_bir_lowering=False)

    def din(name, shape, dt=F32):
        return nc.dram_tensor(name, shape, dt, kind="ExternalInput").ap()

    def dscr(name, shape, dt=F32):
        return nc.dram_tensor(name, shape, dt, kind="Internal").ap()

    xT = din("xT", [NSEQ, D, S_])
    c_fm = din("c_fm", [128, KD, NSEQ])
    ada_w = din("ada_w", [L, D, 6 * D])
    w_in = din("w_in", [L, D, DIN])
    w_uq = din("w_uq", [L, 512, 1536])
    w_ukv = din("w_ukv", [L, 256, 2048])
    w_o_att = din("w_o_att", [L, 1024, D])
    wdu = din("wdu", [L, 128, 1024])
    wiu = din("wiu", [L, 128, 1024])
    wgu = din("wgu", [L, 160, 1024])
    w_o_rwkv = din("w_o_rwkv", [L, 1024, D])
    w_out = din("w_out", [L, D, D])
    w_up = din("w_up", [L, D, 2 * DFF])
    w_down = din("w_down", [L, DFF, D])
    gn_g = din("gn_g", [L, 1, 1024])
    gn_b = din("gn_b", [L, 1, 1024])
    cols_d = din("cols", [L, 128, NCOL])
    cst_d = din("cst", [128, NCST])
    ropec = din("ropec", [64, S_])
    ropes = din("ropes", [64, S_])
    yT = nc.dram_tensor("yT", [NSEQ, D, S_], F32, kind="ExternalOutput").ap()

    rw_d = [dscr(f"rw{s}", [3584, S_]) for s in range(NSEQ)]
    rwm_d = [dscr(f"rwm{s}", [3584, S_]) for s in range(NSEQ)]
    gate_d = [dscr(f"gate{s}", [4096, S_]) for s in range(NSEQ)]
    ga_d = [dscr(f"ga{s}", [D, S_]) for s in range(NSEQ)]
    od_d = [[dscr(f"od{s}_{d}", [S_, 1040]) for d in range(2)] for s in range(NSEQ)]
    vtm_d = [dscr(f"vtm{s}", [S_, 1024]) for s in range(NSEQ)]
    x1_d = [dscr(f"x1_{s}", [D, S_]) for s in range(NSEQ)]
    x2_d = [dscr(f"x2_{s}", [D, S_]) for s in range(NSEQ)]
    B_rw = [Buf() for _ in range(NSEQ)]
    B_rwm = [Buf() for _ in range(NSEQ)]
    B_gate = [Buf() for _ in range(NSEQ)]
    B_ga = [Buf() for _ in range(NSEQ)]
    B_od = [[Buf() for _ in range(2)] for _ in range(NSEQ)]
    B_vtm = [Buf() for _ in range(NSEQ)]
    B_x1 = [Buf() for _ in range(NSEQ)]
    B_x2 = [Buf() for _ in range(NSEQ)]
    B_y = Buf()
    B_in = Buf()

    with ExitStack() as es:
        S = Sched(nc, es)

        _tc = [0]

        def tile(st, name, shape, dt=F32):
            _tc[0] += 1
            return TT(st.enter_context(nc.sbuf_tensor(f"{name}_{_tc[0]}", shape, dt)))

        PS = [TT(es.enter_context(nc.psum_tensor(f"ps{i}", [128, 512], F32))) for i in range(7)]
        PSB = TT(es.enter_context(nc.psum_tensor("psb", [128, 1024], BF16)))

        cst = tile(es, "cst", [128, NCST])
        S.dma('sp', cst.t[:], cst_d, [B_in], [cst])
        ident = cst.t[:, 0:128]
        bd_ones = cst.t[:, 128:256]
        hsel = cst.t[:, 768:770]
        swapm = cst.t[0:64, 770:834]
        identb = tile(es, "identb", [128, 128], BF16)
        S.cp(identb.t[:], ident, [cst], [identb], e='dve')
        onesb = tile(es, "onesb", [128, 128], BF16)
        S.op('dve', lambda: nc.vector.memset(onesb.t[:], 1.0), [], [onesb])
        onesf = tile(es, "onesf", [128, 128])
        S.op('dve', lambda: nc.vector.memset(onesf.t[:], 1.0), [], [onesf])
        epsc = tile(es, "epsc", [128, 2])
        S.op('dve', lambda: nc.vector.memset(epsc.t[:, 0:1], 1e-6), [], [epsc])
        S.op('dve', lambda: nc.vector.memset(epsc.t[:, 1:2], 64e-5), [], [epsc])
        zero_c = tile(es, "zero_c", [128, 1])
        S.op('dve', lambda: nc.vector.memset(zero_c.t[:], 0.0), [], [zero_c])
        mA = [tile(es, f"mA{d}", [128, 256]) for d in range(2)]
        mN = [tile(es, f"mN{d}", [128, 512]) for d in range(2)]
        for d in range(2):
            s_, i_, n_ = (("lt", "le", "gt") if d == 0 else ("gt", "ge", "lt"))
            S.cp(mA[d].t[:, 0:128], cst.t[:, CST[s_]:CST[s_] + 128], [cst], [mA[d]], e='dve')
            S.cp(mA[d].t[:, 128:256], cst.t[:, CST[i_]:CST[i_] + 128], [cst], [mA[d]], e='dve')
            for q in range(4):
                S.cp(mN[d].t[:, q * 128:(q + 1) * 128], cst.t[:, CST[n_]:CST[n_] + 128], [cst], [mN[d]], e='dve')

        cur_d, cur_B = [xT[s] for s in range(NSEQ)], [B_in for _ in range(NSEQ)]

        for l in range(L):
          try:
            with ExitStack() as ls:
                S.barrier()
                cols = tile(ls, "cols", [128, NCOL])
                S.dma('sp', cols.t[:], cols_d[l], [B_in], [cols])

                def col(name, k=0, n=1):
                    return cols.t[:, CO[name] + k:CO[name] + k + n]
                modT = tile(ls, "modT", [128, NSEQ, 96])
                der = tile(ls, "der", [128, NSEQ, 2, KD])
                omka = tile(ls, "omka", [128, 8])
                S.ts(omka.t[:], col("k_a", 0, 8), -1.0, 1.0, ALU.mult, ALU.add, [cols], [omka])

                with ExitStack() as ph:
                    S.barrier()
                    csil = tile(ph, "csil", [128, KD, NSEQ])
                    S.dma('sp', csil.t[:], c_fm, [B_in], [csil])
                    S.act(csil.t[:], csil.t[:], AF.Silu, [csil], [csil])
                    slabs = [tile(ph, f"mslab{i}", [128, KD, 512]) for i in range(2)]
                    psm = PS[0]
                    for sb in range(24):
                        sl = slabs[sb % 2]
                        S.dma('sp', sl.t[:], ada_w[l, :, sb * 512:(sb + 1) * 512].rearrange("(k p) c -> p k c", p=128), [B_in], [sl])
                        for jj in range(4):
                            j = sb * 4 + jj
                            S.mm(psm.t[:, j * NSEQ:(j + 1) * NSEQ],
                                 [(sl.t[:, k, jj * 128:(jj + 1) * 128], csil.t[:, k, :]) for k in range(KD)],
                                 [sl, csil], [psm])
                    pv = psm.t[:, 0:96 * NSEQ].rearrange("p (j s) -> p j s", s=NSEQ)
                    for s in range(NSEQ):
                        S.tt(modT.t[:, s, :], pv[:, :, s], col("ada_b", 0, 96), ALU.add, [psm, cols], [modT])
                        S.stt(der.t[:, s, 0, :], modT.t[:, s, 16:32], 1.0, col("norm_mix_g", 0, 16), ALU.add, ALU.mult, [modT, cols], [der])
                        S.stt(der.t[:, s, 1, :], modT.t[:, s, 64:80], 1.0, col("norm_ffn_g", 0, 16), ALU.add, ALU.mult, [modT, cols], [der])

                _ck(1)
                for s in range(NSEQ):
                    sh1 = lambda k: modT.t[:, s, 0 + k:1 + k]
                    gt1 = lambda k: modT.t[:, s, 32 + k:33 + k]
                    sh2 = lambda k: modT.t[:, s, 48 + k:49 + k]
                    gt2 = lambda k: modT.t[:, s, 80 + k:81 + k]
                    A1 = lambda k: der.t[:, s, 0, k:k + 1]
                    A2 = lambda k: der.t[:, s, 1, k:k + 1]
                    xin_d, xin_B = cur_d[s], cur_B[s]

                    def norm_block(ph_tiles, src_d, src_B, t0, n, Acol, shcol):
                        xt, hT, sq, rstd, tmp = ph_tiles
                        S.dma('sp', xt.t[:, :, 0:n], src_d[:, t0:t0 + n].rearrange("(k p) t -> p k t", p=128), [src_B], [xt])
                        pss = S.rot('nps', [PS[5], PS[6]])
                        for k in range(KD):
                            sqk = S.rot('sq', sq)
                            S.act(sqk.t[:, 0:n], xt.t[:, k, 0:n], AF.Square, [xt], [sqk])
                            S.mm(pss.t[:, 0:n], [(onesb.t[:], sqk.t[:, 0:n])], [onesb, sqk], [pss], start=(k == 0), stop=(k == KD - 1))
                        S.rsqrt(rstd.t[:, 0:n], pss.t[:, 0:n], 1.0 / D, epsc.t[:, 0:1], [pss, epsc], [rstd])
                        for k in range(KD):
                            tk = S.rot('ntmp', tmp)
                            S.tt(tk.t[:, 0:n], xt.t[:, k, 0:n], rstd.t[:, 0:n], ALU.mult, [xt, rstd], [tk])
                            S.act(hT.t[:, k, 0:n], tk.t[:, 0:n], AF.Identity, [tk, modT, der], [hT], scale=Acol(k), bias=shcol(k))

                    with ExitStack() as mla:
                        S.barrier()
                        qdnT = tile(mla, "qdnT", [128, 4, S_], BF16)
                        kvnT = tile(mla, "kvnT", [128, 2, S_], BF16)
                        krT = tile(mla, "krT", [64, S_], BF16)
                        with ExitStack() as ph:
                            S.barrier()
                            xt = tile(ph, "a_xt", [128, KD, 512])
                            hT = tile(ph, "a_hT", [128, KD, 512], BF16)
                            sq = [tile(ph, f"a_sq{i}", [128, 512], BF16) for i in range(2)]
                            rstd = tile(ph, "a_rstd", [128, 512])
                            tmp = [tile(ph, f"a_tmp{i}", [128, 512]) for i in range(2)]
                            wsl = [tile(ph, f"a_w{i}", [128, KD, 512], BF16) for i in range(2)]
                            stg = [tile(ph, f"a_stg{i}", [128, 4, 512]) for i in range(2)]
                            qd = tile(ph, "a_qd", [128, 6, 512])
                            sqf = [tile(ph, f"a_sqf{i}", [128, 512]) for i in range(2)]
                            kr = tile(ph, "a_kr", [64, 512])
                            krs = tile(ph, "a_krs", [64, 512])
                            cs = tile(ph, "a_cs", [64, 2, 512])
                            rq = tile(ph, "a_rq", [128, 2, 512])
                            slabsA = _in_chunks()
                            for blk in range(NB):
                                t0 = blk * 512
                                norm_block((xt, hT, sq, rstd, tmp), xin_d, xin_B, t0, 512, A1, sh1)
                                S.dma('sp', cs.t[:, 0, :], ropec[:, t0:t0 + 512], [B_in], [cs])
                                S.dma('sp', cs.t[:, 1, :], ropes[:, t0:t0 + 512], [B_in], [cs])
                                for si, slab in enumerate(slabsA):
                                    c0 = slab[0][1]
                                    c1 = slab[-1][1] + slab[-1][2]
                                    w = wsl[si % 2]
                                    S.dma('pool', w.t[:, :, 0:c1 - c0], w_in[l, :, c0:c1].rearrange("(k p) c -> p k c", p=128), [B_in], [w])
                                    st = None
                                    nst = 0
                                    for (kind, cc, cw, idx) in slab:
                                        ps = S.rot('aps', PS[0:4])
                                        S.mm(ps.t[0:cw, :], [(w.t[:, k, cc - c0:cc - c0 + cw], hT.t[:, k, :]) for k in range(KD)], [w, hT], [ps])
                                        if kind == "q":
                                            S.cp(qd.t[:, idx, :], ps.t[:, :], [ps], [qd])
                                        elif kind == "kv":
                                            S.cp(qd.t[:, 4 + idx, :], ps.t[:, :], [ps], [qd])
                                        elif kind == "rope":
                                            S.cp(kr.t[:, :], ps.t[0:64, :], [ps], [kr], e='act')
                                            ps2 = S.rot('aps', PS[0:4])
                                            S.mm(ps2.t[0:64, :], [(swapm, kr.t[:, :])], [cst, kr], [ps2])
                                            S.tt(krs.t[:], ps2.t[0:64, :], cs.t[:, 1, :], ALU.mult, [ps2, cs], [krs])
                                            S.tt(kr.t[:], kr.t[:], cs.t[:, 0, :], ALU.mult, [kr, cs], [kr])
                                            S.tt(krT.t[:, t0:t0 + 512], kr.t[:], krs.t[:], ALU.add, [kr, krs], [krT])
                                        else:
                                            if st is None:
                                                st = S.rot('astg', stg)
                                                nst = 0
                                                st_first = (kind, idx)
                                            if kind == "gate":
                                                S.act(st.t[0:cw, nst, :], ps.t[0:cw, :], AF.Sigmoid, [ps], [st])
                                            else:
                                                S.cp(st.t[0:cw, nst, :], ps.t[0:cw, :], [ps], [st])
                                            nst += 1
                                    if st is not None:
                                        kind0, i0 = st_first
                                        dd, dB = (rw_d[s], B_rw[s]) if kind0 == "rw" else (gate_d[s], B_gate[s])
                                        nfull = nst - 1 if slab[-1][2] == 32 else nst
                                        if nfull > 0:
                                            S.dma('sp', dd[i0 * 128:(i0 + nfull) * 128, t0:t0 + 512].rearrange("(a p) t -> p a t", p=128),
                                                  st.t[:, 0:nfull, :], [st], [dB])
                                        if nfull < nst:
                                            S.dma('sp', dd[(i0 + nfull) * 128:(i0 + nfull) * 128 + 32, t0:t0 + 512], st.t[0:32, nfull, :], [st], [dB])
                                for (j0, nj, gname, dst, ps) in ((0, 4, "q_norm_g", qdnT, PS[4]), (4, 2, "kv_norm_g", kvnT, PS[4])):
                                    for j in range(nj):
                                        sqk = S.rot('sqf', sqf)
                                        S.act(sqk.t[:], qd.t[:, j0 + j, :], AF.Square, [qd], [sqk])
                                        S.mm(ps.t[:, :], [(onesf.t[:], sqk.t[:])], [onesf, sqk], [ps], start=(j == 0), stop=(j == nj - 1))
                                    rr = rq.t[:, 0, :]
                                    S.rsqrt(rr, ps.t[:, :], 1.0 / (128 * nj), epsc.t[:, 0:1], [ps, epsc], [rq])
                                    for j in range(nj):
                                        S.stt(dst.t[:, j, t0:t0 + 512], qd.t[:, j0 + j, :], col(gname, j), rr, ALU.mult, ALU.mult, [qd, cols, rq], [dst])

                        _ck(2)
                        with ExitStack() as ph:
                            S.barrier()
                            oT = tile(ph, "b_oT", [128, NH, S_], BF16)
                            with ExitStack() as ph2:
                                S.barrier()
                                wuq = tile(ph2, "b_wuq", [128, 4, 1536], BF16)
                                wukv = tile(ph2, "b_wukv", [128, 2, 2048], BF16)
                                S.dma('pool', wuq.t[:], w_uq[l].rearrange("(k p) c -> p k c", p=128), [B_in], [wuq])
                                S.dma('pool', wukv.t[:], w_ukv[l].rearrange("(k p) c -> p k c", p=128), [B_in], [wukv])
                                knT = tile(ph2, "b_knT", [128, S_], BF16)
                                Vt = tile(ph2, "b_Vt", [128, NT, 128], BF16)
                                qnT = tile(ph2, "b_qnT", [128, S_], BF16)
                                qrT = tile(ph2, "b_qrT", [64, S_], BF16)
                                qr = tile(ph2, "b_qr", [64, 512])
                                qrs = tile(ph2, "b_qrs", [64, 512])
                                cs = tile(ph2, "b_cs", [64, 2, 512])
                                pT = [tile(ph2, f"b_pT{i}", [128, 512], BF16) for i in range(3)]
                                rec = tile(ph2, "b_rec", [128, 512])
                                scale = 192.0 ** -0.5
                                for h in range(NH):
                                    for blk in range(NB):
                                        t0 = blk * 512
                                        ps = S.rot('bps', PS[0:3])
                                        S.mm(ps.t[:, :], [(wukv.t[:, r, h * 256:h * 256 + 128], kvnT.t[:, r, t0:t0 + 512]) for r in range(2)], [wukv, kvnT], [ps])
                                        S.cp(knT.t[:, t0:t0 + 512], ps.t[:, :], [ps], [knT])
                                        ps = S.rot('bps', PS[0:3])
                                        for q in range(4):
                                            tt_ = blk * 4 + q
                                            S.mm(ps.t[:, q * 128:(q + 1) * 128], [(kvnT.t[:, r, tt_ * 128:(tt_ + 1) * 128], wukv.t[:, r, h * 256 + 128:h * 256 + 256]) for r in range(2)], [wukv, kvnT], [ps])
                                        S.cp(Vt.t[:, blk * 4:blk * 4 + 4, :], ps.t[:, :].rearrange("p (a b) -> p a b", b=128), [ps], [Vt])
                                        ps = S.rot('bps', PS[0:3])
                                        S.mm(ps.t[:, :], [(wuq.t[:, r, h * 192:h * 192 + 128], qdnT.t[:, r, t0:t0 + 512]) for r in range(4)], [wuq, qdnT], [ps])
                                        S.cp(qnT.t[:, t0:t0 + 512], ps.t[:, :], [ps], [qnT])
                                        ps = S.rot('bps', PS[0:3])
                                        S.mm(ps.t[0:64, :], [(wuq.t[:, r, h * 192 + 128:h * 192 + 192], qdnT.t[:, r, t0:t0 + 512]) for r in range(4)], [wuq, qdnT], [ps])
                                        S.cp(qr.t[:, :], ps.t[0:64, :], [ps], [qr], e='act')
                                        S.dma('sp', cs.t[:, 0, :], ropec[:, t0:t0 + 512], [B_in], [cs])
                                        S.dma('sp', cs.t[:, 1, :], ropes[:, t0:t0 + 512], [B_in], [cs])
                                        ps2 = S.rot('bps', PS[0:3])
                                        S.mm(ps2.t[0:64, :], [(swapm, qr.t[:, :])], [cst, qr], [ps2])
                                        S.tt(qrs.t[:], ps2.t[0:64, :], cs.t[:, 1, :], ALU.mult, [ps2, cs], [qrs])
                                        S.tt(qr.t[:], qr.t[:], cs.t[:, 0, :], ALU.mult, [qr, cs], [qr])
                                        S.tt(qrT.t[:, t0:t0 + 512], qr.t[:], qrs.t[:], ALU.add, [qr, qrs], [qrT])
                                    for qb in range(NB):
                                        q0 = qb * 512
                                        psO, psD = PS[3], PS[4]
                                        for kt in range(NT):
                                            k0 = kt * 128
                                            ps = S.rot('sps', [PS[5], PS[6], PS[0]])
                                            S.mm(ps.t[:, :], [(knT.t[:, k0:k0 + 128], qnT.t[:, q0:q0 + 512]), (krT.t[:, k0:k0 + 128], qrT.t[:, q0:q0 + 512])], [knT, qnT, krT, qrT], [ps])
                                            p = S.rot('pT', pT)
                                            S.act(p.t[:], ps.t[:, :], AF.Exp, [ps], [p], scale=scale)
                                            S.mm(psO.t[:, :], [(Vt.t[:, kt, :], p.t[:])], [Vt, p], [psO], start=(kt == 0), stop=(kt == NT - 1))
                                            S.mm(psD.t[:, :], [(onesb.t[:], p.t[:])], [onesb, p], [psD], start=(kt == 0), stop=(kt == NT - 1))
                                        S.op('dve', lambda: nc.vector.reciprocal(out=rec.t[:], in_=psD.t[:, :]), [psD], [rec])
                                        S.tt(oT.t[:, h, q0:q0 + 512], psO.t[:, :], rec.t[:], ALU.mult, [psO, rec], [oT])
                            _ck(3)
                            with ExitStack() as ph2:
                                S.barrier()
                                woa = tile(ph2, "b_woa", [128, NH, D], BF16)
                                S.dma('pool', woa.t[:], w_o_att[l].rearrange("(k p) c -> p k c", p=128), [B_in], [woa])
                                gts = [tile(ph2, f"b_g{i}", [128, 4, 512]) for i in range(2)]
                                ost = [tile(ph2, f"b_o{i}", [128, 4, 512]) for i in range(2)]
                                for blk in range(NB):
                                    t0 = blk * 512
                                    for dg in range(4):
                                        g = S.rot('bg', gts)
                                        o = S.rot('bo', ost)
                                        S.dma('sp', g.t[:], gate_d[s][dg * 512:(dg + 1) * 512, t0:t0 + 512].rearrange("(a p) t -> p a t", p=128), [B_gate[s]], [g])
                                        for a in range(4):
                                            dc = dg * 4 + a
                                            ps = S.rot('bps', PS[0:3])
                                            S.mm(ps.t[:, :], [(woa.t[:, h, dc * 128:(dc + 1) * 128], oT.t[:, h, t0:t0 + 512]) for h in range(NH)], [woa, oT], [ps])
                                            S.tt(o.t[:, a, :], ps.t[:, :], g.t[:, a, :], ALU.mult, [ps, g], [o])
                                        S.dma('sp', ga_d[s][dg * 512:(dg + 1) * 512, t0:t0 + 512].rearrange("(a p) t -> p a t", p=128), o.t[:], [o], [B_ga[s]])

                    _ck(4)
                    with ExitStack() as ph:
                        S.barrier()
                        pin = [tile(ph, f"c0_in{i}", [128, S_ + 2]) for i in range(2)]
                        pout = [tile(ph, f"c0_out{i}", [128, S_]) for i in range(2)]
                        cm = tile(ph, "c0_cm", [128, 28])
                        S.tt(cm.t[:], col("mu0", 0, 28), col("mu1", 0, 28), ALU.add, [cols], [cm])
                        S.ts(cm.t[:], cm.t[:], -1.0, 1.0, ALU.mult, ALU.add, [cm], [cm])
                        for i in range(2):
                            S.op('dve', lambda: nc.vector.memset(pin[i].t[:, 0:1], 0.0), [], [pin[i]])
                            S.op('dve', lambda: nc.vector.memset(pin[i].t[:, S_ + 1:S_ + 2], 0.0), [], [pin[i]])
                        for j in range(28):
                            np_ = 128 if j < 27 else 32
                            a, o = pin[j % 2], pout[j % 2]
                            S.dma('sp', a.t[0:np_, 1:S_ + 1], rw_d[s][j * 128:j * 128 + np_, :], [B_rw[s]], [a])
                            S.ts(o.t[0:np_, :], a.t[0:np_, 1:S_ + 1], cm.t[0:np_, j:j + 1], None, ALU.mult, None, [a, cm], [o])
                            S.stt(o.t[0:np_, :], a.t[0:np_, 0:S_], col("mu0", j)[0:np_], o.t[0:np_, :], ALU.mult, ALU.add, [a, cols, o], [o])
                            S.stt(o.t[0:np_, :], a.t[0:np_, 2:S_ + 2], col("mu1", j)[0:np_], o.t[0:np_, :], ALU.mult, ALU.add, [a, cols, o], [o])
                            S.dma('sp', rwm_d[s][j * 128:j * 128 + np_, :], o.t[0:np_, :], [o], [B_rwm[s]])

                    _ck(5)
                    with ExitStack() as ph:
                        S.barrier()
                        wdu_t = tile(ph, "c_wdu", [128, 1024])
                        wiu_t = tile(ph, "c_wiu", [128, 1024])
                        S.dma('sp', wdu_t.t[:], wdu[l], [B_in], [wdu_t])
                        S.dma('sp', wiu_t.t[:], wiu[l], [B_in], [wiu_t])
                        X = [tile(ph, f"c_x{i}", [128, 8, 128]) for i in range(14)]
                        AR8 = tile(ph, "c_AR8", [128, 8, 2, 128])
                        AM = tile(ph, "c_AM", [128, 16, 384])
                        NP = [tile(ph, f"c_NP{i}", [128, 16, 128]) for i in range(2)]
                        GP = [tile(ph, f"c_GP{i}", [128, 16, 128]) for i in range(2)]
                        Y = tile(ph, "c_Y", [128, 16, 128])
                        WTin = tile(ph, "c_WTin", [128, 16, 64])
                        WT8 = tile(ph, "c_WT8", [128, 8, 128])
                        MT = tile(ph, "c_MT", [128, 16, 64])
                        Vtm = tile(ph, "c_Vtm", [128, 8, 128])
                        BHp = tile(ph, "c_BHp", [128, 16, 128])
                        KHp = tile(ph, "c_KHp", [128, 16, 128])
                        OT = tile(ph, "c_OT", [128, 1040])
                        wda = tile(ph, "c_wda", [128, 2, 128])
                        twd = tile(ph, "c_twd", [128, 128])
                        adm = tile(ph, "c_adm", [128, 128])
                        stm = [tile(ph, f"c_stm{i}", [128, 8, 64]) for i in range(2)]
                        PCt = tile(ph, "c_PC", [128, 8])
                        ST = [tile(ph, f"c_ST{d}", [128, 8, 64]) for d in range(2)]
                        stmp = tile(ph, "c_stmp", [128, 8, 64])
                        gB = {n: [Buf() for _ in range(4)] for n in ("AM", "N0", "N1", "G0", "G1", "Y")}
                        S.op('dve', lambda: nc.vector.memset(BHp.t[:], 0.0), [], [BHp])
                        S.op('dve', lambda: nc.vector.memset(KHp.t[:], 0.0), [], [KHp])
                        for d in range(2):
                            S.op('dve', lambda: nc.vector.memset(ST[d].t[:], 0.0), [], [ST[d]])
                        (r8, k8, v8, lw8, a8, L8, eLx, eLi, eNL, eCL, kk8, t8, b8, u8) = X
                        bc8 = lambda ap: ap.unsqueeze(2).broadcast_to([128, 8, 128])

                        def chunk_step(d, c):
                            t0 = c * C
                            rows = lambda base: rwm_d[s][base:base + 1024, t0:t0 + C].rearrange("(a p) t -> p a t", p=128)
                            S.dma('sp', r8.t[:], rows(0), [B_rwm[s]], [r8])
                            S.dma('sp', k8.t[:], rows(1024), [B_rwm[s]], [k8])
                            S.dma('sp', v8.t[:], rows(2048), [B_rwm[s]], [v8])
                            S.dma('sp', wda.t[:], rwm_d[s][3072:3328, t0:t0 + C].rearrange("(a p) t -> p a t", p=128), [B_rwm[s]], [wda])
                            S.act(twd.t[:], wda.t[:, 0, :], AF.Tanh, [wda], [twd])
                            S.ts(twd.t[:], twd.t[:], hsel[:, d:d + 1], None, ALU.mult, None, [twd, cst], [twd])
                            S.ts(adm.t[:], wda.t[:, 1, :], hsel[:, d:d + 1], None, ALU.mult, None, [wda, cst], [adm])
                            for (wt, src, bname, dst) in ((wdu_t, twd.t, f"w0_{d}", lw8), (wiu_t, None, f"a0_{d}", a8)):
                                for g2 in range(2):
                                    ps = S.rot('cps', PS[0:4])
                                    for q in range(4):
                                        hp = g2 * 4 + q
                                        rhs = twd.t[:, :] if src is not None else adm.t[:, :]
                                        S.mm(ps.t[:, q * 128:(q + 1) * 128], [(wt.t[:, hp * 128:(hp + 1) * 128], rhs)], [wt, twd, adm], [ps])
                                    for q in range(4):
                                        hp = g2 * 4 + q
                                        S.act(dst.t[:, hp, :], ps.t[:, q * 128:(q + 1) * 128], AF.Sigmoid, [ps, cols], [dst], bias=col(bname, hp))
                            _ck(5.1)
                            S.ts(lw8.t[:], lw8.t[:], -0.6065306597126334, None, ALU.mult, None, [lw8], [lw8])
                            for hp in range(8):
                                S.op('dve', lambda: nc.vector.tensor_tensor_scan(out=L8.t[:, hp, :], data0=lw8.t[:, hp, :], data1=lw8.t[:, hp, :], initial=0.0, op0=ALU.add, op1=ALU.bypass), [lw8], [L8])
                            LCb = L8.t[:, :, 127:128].broadcast_to([128, 8, 128])
                            S.act(PCt.t[:], L8.t[:, :, 127], AF.Exp, [L8], [PCt])
                            if d == 0:
                                S.tt(eLx.t[:], L8.t[:], lw8.t[:], ALU.subtract, [L8, lw8], [eLx])
                                S.tt(eCL.t[:], LCb, L8.t[:], ALU.subtract, [L8], [eCL])
                                S.act(eLi.t[:], L8.t[:], AF.Exp, [L8], [eLi])
                                S.act(eNL.t[:], L8.t[:], AF.Exp, [L8], [eNL], scale=-1.0)
                            else:
                                S.tt(eLx.t[:], LCb, L8.t[:], ALU.subtract, [L8], [eLx])
                                S.tt(eLi.t[:], eLx.t[:], lw8.t[:], ALU.add, [eLx, lw8], [eLi])
                                S.tt(eCL.t[:], LCb, eLi.t[:], ALU.subtract, [L8, eLi], [eCL])
                                S.act(eNL.t[:], eLi.t[:], AF.Exp, [eLi], [eNL], scale=-1.0)
                                S.act(eLi.t[:], eLi.t[:], AF.Exp, [eLi], [eLi])
                            S.act(eLx.t[:], eLx.t[:], AF.Exp, [eLx], [eLx])
                            S.act(eCL.t[:], eCL.t[:], AF.Exp, [eCL], [eCL])
                            _ck(5.2)
                            S.tt(kk8.t[:], k8.t[:], bc8(col("k_k", 0, 8)), ALU.mult, [k8, cols], [kk8])
                            S.tt(u8.t[:], kk8.t[:], kk8.t[:], ALU.mult, [kk8], [u8])
                            for g2 in range(2):
                                ps = S.rot('cps', PS[0:4])
                                for q in range(4):
                                    S.mm(ps.t[:, q * 128:(q + 1) * 128], [(bd_ones, u8.t[:, g2 * 4 + q, :])], [cst, u8], [ps])
                                S.ts(L8.t[:, g2 * 4:g2 * 4 + 4, :], ps.t[:, :].rearrange("p (a b) -> p a b", b=128), 1e-24, None, ALU.max, None, [ps], [L8])
                            S.act(L8.t[:], L8.t[:], AF.Sqrt, [L8], [L8])
                            S.op('dve', lambda: nc.vector.reciprocal(out=L8.t[:], in_=L8.t[:]), [L8], [L8])
                            S.tt(kk8.t[:], kk8.t[:], L8.t[:], ALU.mult, [kk8, L8], [kk8])
                            S.tt(t8.t[:], a8.t[:], bc8(col("k_a", 0, 8)), ALU.mult, [a8, cols], [t8])
                            S.tt(t8.t[:], t8.t[:], bc8(omka.t[:, 0:8]), ALU.add, [t8, omka], [t8])
                            S.tt(t8.t[:], t8.t[:], k8.t[:], ALU.mult, [t8, k8], [t8])
                            S.tt(b8.t[:], kk8.t[:], a8.t[:], ALU.mult, [kk8, a8], [b8])
                            S.stt(AR8.t[:, :, 0, :], kk8.t[:], -1.0, eLx.t[:], ALU.mult, ALU.mult, [kk8, eLx], [AR8])
                            S.tt(AR8.t[:, :, 1, :], r8.t[:], eLi.t[:], ALU.mult, [r8, eLi], [AR8])
                            S.tt(u8.t[:], r8.t[:], t8.t[:], ALU.mult, [r8, t8], [u8])
                            S.tt(u8.t[:], u8.t[:], bc8(col(f"r_k_{d}", 0, 8)), ALU.mult, [u8, cols], [u8])
                            psB = S.rot('cps', PS[0:4])
                            for hp in range(8):
                                S.mm(psB.t[:, hp * 2:hp * 2 + 2], [(u8.t[:, hp, :], hsel)], [u8, cst], [psB])
                            S.cp(OT.t[:, 1024:1040], psB.t[:, 0:16], [psB], [OT], e='act')
                            bt8, kt8, bh8, kh8 = a8, lw8, eLx, eLi
                            S.tt(bh8.t[:], b8.t[:], eCL.t[:], ALU.mult, [b8, eCL], [bh8])
                            S.tt(kh8.t[:], t8.t[:], eCL.t[:], ALU.mult, [t8, eCL], [kh8])
                            S.tt(bt8.t[:], b8.t[:], eNL.t[:], ALU.mult, [b8, eNL], [bt8])
                            S.tt(kt8.t[:], t8.t[:], eNL.t[:], ALU.mult, [t8, eNL], [kt8])
                            _ck(5.3)
                            for g2 in range(2):
                                ps = S.rot('cps', PS[0:4])
                                for q in range(4):
                                    S.tr(ps.t[:, q * 128:(q + 1) * 128], v8.t[:, g2 * 4 + q, :], ident, [v8, cst], [ps])
                                S.cp(Vtm.t[:, g2 * 4:g2 * 4 + 4, :], ps.t[:, :].rearrange("p (a b) -> p a b", b=128), [ps], [Vtm])
                                ps = S.rot('cps', PS[0:4])
                                for q in range(4):
                                    S.tr(ps.t[:, q * 128:(q + 1) * 128], AR8.t[:, g2 * 4 + q, 0, :], ident, [AR8, cst], [ps])
                                S.cp(Y.t[:, g2 * 8:g2 * 8 + 8, 0:64], ps.t[:, :].rearrange("p (a b) -> p a b", b=64), [ps], [gB["Y"][2 * g2], gB["Y"][2 * g2 + 1]])
                                for (src, dstp) in ((bh8, BHp), (kh8, KHp)):
                                    ps = S.rot('cps', PS[0:4])
                                    for q in range(4):
                                        S.tr(ps.t[:, q * 128:(q + 1) * 128], src.t[:, g2 * 4 + q, :], ident, [src, cst], [ps])
                                    pv4 = ps.t[:, :].rearrange("p (a h b) -> p a h b", h=2, b=64)
                                    dv4 = dstp.t[:, g2 * 8:g2 * 8 + 8, :].rearrange("p (a h) c -> p a h c", h=2)
                                    S.cp(dv4[:, :, 0, 0:64], pv4[:, :, 0, :], [ps], [dstp])
                                    S.cp(dv4[:, :, 1, 64:128], pv4[:, :, 1, :], [ps], [dstp])
                            if d == 0:
                                S.dma('sp', vtm_d[s][t0:t0 + C, :], Vtm.t[:].rearrange("p a b -> p (a b)"), [Vtm], [B_vtm[s]])
                            _ck(5.4)
                            am = [r8, k8]
                            rm = [L8, eNL]
                            for hh in range(2):
                                S.ts(am[hh].t[:], AR8.t[:, :, 0, :], hsel[:, hh:hh + 1], None, ALU.mult, None, [AR8, cst], [am[hh]])
                                S.ts(rm[hh].t[:], AR8.t[:, :, 1, :], hsel[:, hh:hh + 1], None, ALU.mult, None, [AR8, cst], [rm[hh]])
                            Aak = GP[1]
                            for h in range(16):
                                hp, hh = h // 2, h % 2
                                sl = slice(64 * hh, 64 * hh + 64)
                                g4 = h // 4
                                ps = S.rot('cps', PS[0:4])
                                S.mm(ps.t[:, 0:128], [(bt8.t[:, hp, :], am[hh].t[:, hp, :])], [bt8, am[hh]], [ps])
                                S.mm(ps.t[:, 128:256], [(bt8.t[:, hp, :], rm[hh].t[:, hp, :])], [bt8, rm[hh]], [ps])
                                S.mm(ps.t[:, 256:384], [(kt8.t[:, hp, :], am[hh].t[:, hp, :])], [kt8, am[hh]], [ps])
                                S.mm(ps.t[:, 384:512], [(kt8.t[:, hp, :], rm[hh].t[:, hp, :])], [kt8, rm[hh]], [ps])
                                S.tt(AM.t[:, h, 0:256], ps.t[:, 0:256], mA[d].t[:], ALU.mult, [ps, mA[d]], [gB["AM"][g4]])
                                S.tt(Aak.t[:, h, :], ps.t[:, 256:384], mA[d].t[:, 0:128], ALU.mult, [ps, mA[d]], [gB["G1"][g4]])
                                S.tt(AM.t[:, h, 256:384], ps.t[:, 384:512], mA[d].t[:, 128:256], ALU.mult, [ps, mA[d]], [gB["AM"][g4]])
                            for g4 in range(4):
                                ps = S.rot('cps', PS[0:4])
                                for q in range(4):
                                    h = g4 * 4 + q
                                    hp, hh = h // 2, h % 2
                                    sl = slice(64 * hh, 64 * hh + 64)
                                    S.mm(ps.t[:, q * 128:(q + 1) * 128], [(am[hh].t[:, hp, :], bt8.t[:, hp, :])], [am[hh], bt8], [ps])
                                S.tt(NP[0].t[:, g4 * 4:g4 * 4 + 4, :], ps.t[:, :].rearrange("p (a b) -> p a b", b=128), mN[d].t[:].rearrange("p (a b) -> p a b", b=128), ALU.mult, [ps, mN[d]], [gB["N0"][g4]])
                            for g8 in range(2):
                                ps = S.rot('cps', PS[0:4])
                                for q in range(8):
                                    h = g8 * 8 + q
                                    hp, hh = h // 2, h % 2
                                    S.mm(ps.t[:, q * 64:(q + 1) * 64], [(Aak.t[:, h, :], Vtm.t[:, hp, 64 * hh:64 * hh + 64])], [gB["G1"][h // 4], Vtm], [ps])
                                S.cp(Y.t[:, g8 * 8:g8 * 8 + 8, 64:128], ps.t[:, :].rearrange("p (a b) -> p a b", b=64), [ps], [gB["Y"][2 * g8], gB["Y"][2 * g8 + 1]])
                            _ck(5.5)
                            for p_ in range(7):
                                if p_ == 0:
                                    Gt, Gb = (lambda h: AM.t[:, h, 0:128]), gB["AM"]
                                else:
                                    gi = (p_ - 1) % 2
                                    Gt, Gb = (lambda h, gi=gi: GP[gi].t[:, h, :]), gB[f"G{gi}"]
                                ni = p_ % 2
                                Nt, Nb = (lambda h, ni=ni: NP[ni].t[:, h, :]), gB[f"N{ni}"]
                                for g4 in range(4):
                                    ps = S.rot('cps', PS[0:4])
                                    for q in range(4):
                                        h = g4 * 4 + q
                                        S.mm(ps.t[:, q * 128:(q + 1) * 128], [(Gt(h), Y.t[:, h, :])], [Gb[g4], gB["Y"][g4]], [ps])
                                    S.tt(Y.t[:, g4 * 4:g4 * 4 + 4, :], ps.t[:, :].rearrange("p (a b) -> p a b", b=128), Y.t[:, g4 * 4:g4 * 4 + 4, :], ALU.add, [ps, gB["Y"][g4]], [gB["Y"][g4]])
                                if p_ == 6:
                                    break
                                go = p_ % 2
                                no = (p_ + 1) % 2
                                for g4 in range(4):
                                    ps = S.rot('cps', PS[0:4])
                                    for q in range(4):
                                        h = g4 * 4 + q
                                        S.mm(ps.t[:, q * 128:(q + 1) * 128], [(Nt(h), Gt(h))], [Gb[g4], Nb[g4]], [ps])
                                    S.cp(GP[go].t[:, g4 * 4:g4 * 4 + 4, :], ps.t[:, :].rearrange("p (a b) -> p a b", b=128), [ps], [gB[f"G{go}"][g4]])
                                    if p_ < 5:
                                        ps = S.rot('cps', PS[0:4])
                                        for q in range(4):
                                            h = g4 * 4 + q
                                            S.mm(ps.t[:, q * 128:(q + 1) * 128], [(Gt(h), Nt(h))], [Gb[g4], Nb[g4]], [ps])
                                        S.cp(NP[no].t[:, g4 * 4:g4 * 4 + 4, :], ps.t[:, :].rearrange("p (a b) -> p a b", b=128), [ps], [gB[f"N{no}"][g4]])
                            _ck(5.6)
                            S.cp(WTin.t[:], Y.t[:, :, 0:64], gB["Y"], [WTin], e='dve')
                            for g2 in range(2):
                                ps = S.rot('cps', PS[0:4])
                                for q in range(4):
                                    hp = g2 * 4 + q
                                    S.tr(ps.t[:, q * 128:(q + 1) * 128], WTin.t[:, 2 * hp:2 * hp + 2, :].rearrange("p a b -> p (a b)"), ident, [WTin, cst], [ps])
                                S.cp(WT8.t[:, g2 * 4:g2 * 4 + 4, :], ps.t[:, :].rearrange("p (a b) -> p a b", b=128), [ps], [WT8])
                            _ck(5.7)
                            st = ST[d]
                            for hh in range(2):
                                S.ts(stm[hh].t[:], st.t[:], hsel[:, hh:hh + 1], None, ALU.mult, None, [st, cst], [stm[hh]])
                            for g8 in range(2):
                                ps = PS[4 + g8]
                                for q in range(8):
                                    h = g8 * 8 + q
                                    hp, hh = h // 2, h % 2
                                    sl = slice(64 * hh, 64 * hh + 64)
                                    S.mm(ps.t[:, q * 64:(q + 1) * 64], [(WT8.t[:, hp, :], stm[hh].t[:, hp, :])], [WT8, stm[hh]], [ps])
                                S.tt(MT.t[:, g8 * 8:g8 * 8 + 8, :], ps.t[:, :].rearrange("p (a b) -> p a b", b=64), Y.t[:, g8 * 8:g8 * 8 + 8, 64:128], ALU.add, [ps, gB["Y"][2 * g8], gB["Y"][2 * g8 + 1]], [MT])
                            for g8 in range(2):
                                ps = PS[4 + g8]
                                for q in range(8):
                                    h = g8 * 8 + q
                                    hp, hh = h // 2, h % 2
                                    sl = slice(64 * hh, 64 * hh + 64)
                                    S.mm(ps.t[:, q * 64:(q + 1) * 64],
                                         [(rm[hh].t[:, hp, :], st.t[:, hp, :]), (AM.t[:, h, 128:256], MT.t[:, h, :]), (AM.t[:, h, 256:384], Vtm.t[:, hp, 64 * hh:64 * hh + 64])],
                                         [rm[hh], st, gB["AM"][h // 4], MT, Vtm], [ps])
                                S.cp(OT.t[:, g8 * 512:(g8 + 1) * 512], ps.t[:, :], [ps], [OT], e='act')
                            S.dma('sp', od_d[s][d][t0:t0 + C, :], OT.t[:], [OT], [B_od[s][d]])
                            ps = PS[6]
                            for hp in range(8):
                                S.mm(ps.t[:, hp * 64:(hp + 1) * 64],
                                     [(BHp.t[:, 2 * hp, :], MT.t[:, 2 * hp, :]), (KHp.t[:, 2 * hp, :], Vtm.t[:, hp, 0:64]),
                                      (BHp.t[:, 2 * hp + 1, :], MT.t[:, 2 * hp + 1, :]), (KHp.t[:, 2 * hp + 1, :], Vtm.t[:, hp, 64:128])],
                                     [BHp, KHp, MT, Vtm], [ps])
                            S.tt(stmp.t[:], st.t[:], PCt.t[:].unsqueeze(2).broadcast_to([128, 8, 64]), ALU.mult, [st, PCt], [stmp])
                            S.tt(st.t[:], stmp.t[:], ps.t[:, :].rearrange("p (a b) -> p a b", b=64), ALU.add, [stmp, ps], [st])

                        for i in range(NCH):
                            chunk_step(0, i)
                            chunk_step(1, NCH - 1 - i)

                    _ck(6)
                    with ExitStack() as ph:
                        S.barrier()
                        worw = tile(ph, "d_worw", [128, 8, D], BF16)
                        wout = tile(ph, "d_wout", [128, KD, D], BF16)
                        S.dma('pool', worw.t[:], w_o_rwkv[l].rearrange("(k p) c -> p k c", p=128), [B_in], [worw])
                        S.dma('pool', wout.t[:], w_out[l].rearrange("(k p) c -> p k c", p=128), [B_in], [wout])
                        wgA = tile(ph, "d_wgA", [128, 1024])
                        wgB = tile(ph, "d_wgB", [32, 1024])
                        S.dma('sp', wgA.t[:], wgu[l, 0:128, :], [B_in], [wgA])
                        S.dma('sp', wgB.t[:], wgu[l, 128:160, :], [B_in], [wgB])
                        gng = tile(ph, "d_gng", [128, 1024])
                        gnb = tile(ph, "d_gnb", [128, 1024])
                        S.dma('sp', gng.t[:], gn_g[l].partition_broadcast(128), [B_in], [gng])
                        S.dma('sp', gnb.t[:], gn_b[l].partition_broadcast(128), [B_in], [gnb])
                        of = [tile(ph, f"d_of{i}", [128, 1040]) for i in range(2)]
                        ob = [tile(ph, f"d_ob{i}", [128, 1040]) for i in range(2)]
                        vt = [tile(ph, f"d_vt{i}", [128, 1024]) for i in range(2)]
                        gdA = tile(ph, "d_gdA", [128, 128])
                        gdB = tile(ph, "d_gdB", [32, 128])
                        o3 = tile(ph, "d_o3", [128, 16, 64])
                        sq3 = tile(ph, "d_sq3", [128, 16, 64])
                        st16 = tile(ph, "d_st16", [128, 4, 16])
                        ofin = tile(ph, "d_ofin", [128, 1024], BF16)
                        ofT = tile(ph, "d_ofT", [128, 8, 512], BF16)
                        miT = tile(ph, "d_miT", [128, KD, 512], BF16)
                        gl = [tile(ph, f"d_gl{i}", [128, 512]) for i in range(3)]
                        tm = [tile(ph, f"d_tm{i}", [128, 512]) for i in range(2)]
                        xo = [tile(ph, f"d_xo{i}", [128, 512]) for i in range(2)]
                        v3 = lambda t: t.t[:, 0:1024].rearrange("p (a b) -> p a b", b=64)
                        b16 = lambda ap: ap.unsqueeze(2).broadcast_to([128, 16, 64])
                        for blk in range(NB):
                            t0 = blk * 512
                            for q4 in range(4):
                                tk = t0 + q4 * 128
                                f_, b_, v_ = of[q4 % 2], ob[q4 % 2], vt[q4 % 2]
                                S.dma('sp', f_.t[:], od_d[s][0][tk:tk + 128, :], [B_od[s][0]], [f_])
                                S.dma('sp', b_.t[:], od_d[s][1][tk:tk + 128, :], [B_od[s][1]], [b_])
                                S.dma('sp', v_.t[:], vtm_d[s][tk:tk + 128, :], [B_vtm[s]], [v_])
                                S.dma('sp', gdA.t[:], rwm_d[s][3328:3456, tk:tk + 128], [B_rwm[s]], [gdA])
                                S.dma('sp', gdB.t[:], rwm_d[s][3456:3488, tk:tk + 128], [B_rwm[s]], [gdB])
                                S.act(gdA.t[:], gdA.t[:], AF.Sigmoid, [gdA], [gdA])
                                S.act(gdB.t[:], gdB.t[:], AF.Sigmoid, [gdB], [gdB])
                                S.tt(f_.t[:], f_.t[:], b_.t[:], ALU.add, [f_, b_], [f_])
                                S.op('dve', lambda: nc.vector.reduce_sum(out=st16.t[:, 0, :], in_=v3(f_), axis=AX.X), [f_], [st16])
                                S.ts(st16.t[:, 0, :], st16.t[:, 0, :], 1.0 / 64, None, ALU.mult, None, [st16], [st16])
                                S.tt(o3.t[:], v3(f_), b16(st16.t[:, 0, :]), ALU.subtract, [f_, st16], [o3])
                                S.tt(sq3.t[:], o3.t[:], o3.t[:], ALU.mult, [o3], [sq3])
                                S.op('dve', lambda: nc.vector.reduce_sum(out=st16.t[:, 1, :], in_=sq3.t[:], axis=AX.X), [sq3], [st16])
                                S.rsqrt(st16.t[:, 2, :], st16.t[:, 1, :], 1.0 / 64, epsc.t[:, 1:2], [st16, epsc], [st16])
                                S.tt(o3.t[:], o3.t[:], b16(st16.t[:, 2, :]), ALU.mult, [o3, st16], [o3])
                                o3f = o3.t[:].rearrange("p a b -> p (a b)")
                                S.tt(o3f, o3f, gng.t[:], ALU.mult, [o3, gng], [o3])
                                S.tt(o3f, o3f, gnb.t[:], ALU.add, [o3, gnb], [o3])
                                S.tt(sq3.t[:], v3(v_), b16(f_.t[:, 1024:1040]), ALU.mult, [v_, f_], [sq3])
                                S.tt(o3.t[:], o3.t[:], sq3.t[:], ALU.add, [o3, sq3], [o3])
                                for hf in range(2):
                                    ps = S.rot('dps', PS[0:4])
                                    S.mm(ps.t[:, :], [(gdA.t[:], wgA.t[:, hf * 512:(hf + 1) * 512]), (gdB.t[:], wgB.t[:, hf * 512:(hf + 1) * 512])], [gdA, gdB, wgA, wgB], [ps])
                                    S.tt(ofin.t[:, hf * 512:(hf + 1) * 512], o3f[:, hf * 512:(hf + 1) * 512], ps.t[:, :], ALU.mult, [o3, ps], [ofin])
                                for cc in range(8):
                                    S.tr(PSB.t[:, cc * 128:(cc + 1) * 128], ofin.t[:, cc * 128:(cc + 1) * 128], identb.t[:], [ofin, identb], [PSB])
                                S.cp(ofT.t[:, :, q4 * 128:(q4 + 1) * 128], PSB.t[:, :].rearrange("p (a b) -> p a b", b=128), [PSB], [ofT])
                            for dc in range(KD):
                                g1, g2_ = S.rot('dgl', gl), S.rot('dgl', gl)
                                S.dma('sp', g1.t[:], gate_d[s][2048 + dc * 128:2048 + (dc + 1) * 128, t0:t0 + 512], [B_gate[s]], [g1])
                                S.dma('sp', g2_.t[:], ga_d[s][dc * 128:(dc + 1) * 128, t0:t0 + 512], [B_ga[s]], [g2_])
                                ps = S.rot('dps', PS[0:4])
                                S.mm(ps.t[:, :], [(worw.t[:, cc, dc * 128:(dc + 1) * 128], ofT.t[:, cc, :]) for cc in range(8)], [worw, ofT], [ps])
                                t_ = S.rot('dtm', tm)
                                S.tt(t_.t[:], ps.t[:, :], g1.t[:], ALU.mult, [ps, g1], [t_])
                                S.tt(miT.t[:, dc, :], t_.t[:], g2_.t[:], ALU.add, [t_, g2_], [miT])
                            for dc in range(KD):
                                xi = S.rot('dgl', gl)
                                S.dma('sp', xi.t[:], xin_d[dc * 128:(dc + 1) * 128, t0:t0 + 512], [xin_B], [xi])
                                ps = S.rot('dps', PS[0:4])
                                S.mm(ps.t[:, :], [(wout.t[:, k, dc * 128:(dc + 1) * 128], miT.t[:, k, :]) for k in range(KD)], [wout, miT], [ps])
                                xo_ = S.rot('dxo', xo)
                                S.stt(xo_.t[:], ps.t[:, :], gt1(dc), xi.t[:], ALU.mult, ALU.add, [ps, modT, xi], [xo_])
                                S.dma('sp', x1_d[s][dc * 128:(dc + 1) * 128, t0:t0 + 512], xo_.t[:], [xo_], [B_x1[s]])

                    _ck(7)
                    with ExitStack() as ph:
                        S.barrier()
                        xt = tile(ph, "f_xt", [128, KD, 512])
                        hT = tile(ph, "f_hT", [128, KD, 512], BF16)
                        xh = tile(ph, "f_xh", [128, KD, 2])
                        hTh = tile(ph, "f_hTh", [128, KD, 2], BF16)
                        sq = [tile(ph, f"f_sq{i}", [128, 512], BF16) for i in range(3)]
                        rstd = tile(ph, "f_rstd", [128, 512])
                        rstdh = tile(ph, "f_rstdh", [128, 2])
                        tmp = [tile(ph, f"f_tmp{i}", [128, 512]) for i in range(3)]
                        wsl = [tile(ph, f"f_w{i}", [128, KD, 512], BF16) for i in range(3)]
                        wds = [tile(ph, f"f_wd{i}", [128, 11, 512], BF16) for i in range(2)]
                        uT = tile(ph, "f_uT", [128, 44, 512], BF16)
                        acc = [tile(ph, f"f_acc{i}", [128, 512]) for i in range(2)]
                        hal = [tile(ph, f"f_hal{i}", [128, 2]) for i in range(2)]
                        xo = [tile(ph, f"f_xo{i}", [128, 512]) for i in range(2)]
                        last = (l == L - 1)
                        for blk in range(NB):
                            t0 = blk * 512
                            norm_block((xt, hT, sq, rstd, tmp), x1_d[s], B_x1[s], t0, 512, A2, sh2)
                            tl, tr_ = max(t0 - 1, 0), min(t0 + 512, S_ - 1)
                            S.dma('sp', xh.t[:, :, 0:1], x1_d[s][:, tl:tl + 1].rearrange("(k p) t -> p k t", p=128), [B_x1[s]], [xh], allow_slow_non_contiguous=True)
                            S.dma('sp', xh.t[:, :, 1:2], x1_d[s][:, tr_:tr_ + 1].rearrange("(k p) t -> p k t", p=128), [B_x1[s]], [xh], allow_slow_non_contiguous=True)
                            pss = PS[6]
                            for k in range(KD):
                                sqk = S.rot('sq', sq)
                                S.act(sqk.t[:, 0:2], xh.t[:, k, :], AF.Square, [xh], [sqk])
                                S.mm(pss.t[:, 0:2], [(onesb.t[:], sqk.t[:, 0:2])], [onesb, sqk], [pss], start=(k == 0), stop=(k == KD - 1))
                            S.rsqrt(rstdh.t[:], pss.t[:, 0:2], 1.0 / D, epsc.t[:, 0:1], [pss, epsc], [rstdh])
                            for k in range(KD):
                                tk = S.rot('ntmp', tmp)
                                S.tt(tk.t[:, 0:2], xh.t[:, k, :], rstdh.t[:], ALU.mult, [xh, rstdh], [tk])
                                S.act(hTh.t[:, k, :], tk.t[:, 0:2], AF.Identity, [tk, modT, der], [hTh], scale=A2(k), bias=sh2(k))
                            if blk == 0:
                                S.op('dve', lambda: nc.vector.memset(hTh.t[:, :, 0:1], 0.0), [], [hTh])
                            if blk == NB - 1:
                                S.op('dve', lambda: nc.vector.memset(hTh.t[:, :, 1:2], 0.0), [], [hTh])
                            for sj in range(11):
                                wa, wb = S.rot('fw', wsl), S.rot('fw', wsl)
                                S.dma('pool', wa.t[:], w_up[l, :, sj * 512:(sj + 1) * 512].rearrange("(k p) c -> p k c", p=128), [B_in], [wa])
                                S.dma('pool', wb.t[:], w_up[l, :, DFF + sj * 512:DFF + (sj + 1) * 512].rearrange("(k p) c -> p k c", p=128), [B_in], [wb])
                                for a in range(4):
                                    j = sj * 4 + a
                                    psa = S.rot('fpa', PS[0:2])
                                    psh = PS[5]
                                    psb_ = S.rot('fpb', PS[2:4])
                                    S.mm(psa.t[:, :], [(wa.t[:, k, a * 128:(a + 1) * 128], hT.t[:, k, :]) for k in range(KD)], [wa, hT], [psa])
                                    S.mm(psh.t[:, 0:2], [(wa.t[:, k, a * 128:(a + 1) * 128], hTh.t[:, k, :]) for k in range(KD)], [wa, hTh], [psh])
                                    S.mm(psb_.t[:, :], [(wb.t[:, k, a * 128:(a + 1) * 128], hT.t[:, k, :]) for k in range(KD)], [wb, hT], [psb_])
                                    ac = S.rot('facc', acc)
                                    hl = S.rot('fhal', hal)
                                    S.cp(hl.t[:], psh.t[:, 0:2], [psh], [hl], e='act')
                                    S.act(ac.t[:], psa.t[:, :], AF.Identity, [psa, cols], [ac], scale=col("cw1", j), bias=col("cb", j))
                                    S.stt(ac.t[:, 1:512], psa.t[:, 0:511], col("cw0", j), ac.t[:, 1:512], ALU.mult, ALU.add, [psa, cols, ac], [ac])
                                    S.stt(ac.t[:, 0:511], psa.t[:, 1:512], col("cw2", j), ac.t[:, 0:511], ALU.mult, ALU.add, [psa, cols, ac], [ac])
                                    S.stt(ac.t[:, 0:1], hl.t[:, 0:1], col("cw0", j), ac.t[:, 0:1], ALU.mult, ALU.add, [hl, cols, ac], [ac])
                                    S.stt(ac.t[:, 511:512], hl.t[:, 1:2], col("cw2", j), ac.t[:, 511:512], ALU.mult, ALU.add, [hl, cols, ac], [ac])
                                    S.act(ac.t[:], ac.t[:], AF.Silu, [ac], [ac])
                                    S.tt(uT.t[:, j, :], ac.t[:], psb_.t[:, :], ALU.mult, [ac, psb_], [uT])
                            for dg in range(4):
                                pso = PS[0:4]
                                for rs in range(4):
                                    wd_ = S.rot('fwd', wds)
                                    S.dma('pool', wd_.t[:], w_down[l, rs * 1408:(rs + 1) * 1408, dg * 512:(dg + 1) * 512].rearrange("(a p) c -> p a c", p=128), [B_in], [wd_])
                                    for a in range(11):
                                        for d4 in range(4):
                                            S.mm(pso[d4].t[:, :], [(wd_.t[:, a, d4 * 128:(d4 + 1) * 128], uT.t[:, rs * 11 + a, :])], [wd_, uT], [pso[d4]],
                                                 start=(rs == 0 and a == 0), stop=(rs == 3 and a == 10))
                                for d4 in range(4):
                                    dc = dg * 4 + d4
                                    xo_ = S.rot('fxo', xo)
                                    S.stt(xo_.t[:], pso[d4].t[:, :], gt2(dc), xt.t[:, dc, :], ALU.mult, ALU.add, [pso[d4], modT, xt], [xo_])
                                    S.dma('sp', x2_d[s][dc * 128:(dc + 1) * 128, t0:t0 + 512], xo_.t[:], [xo_], [B_x2[s]])
                        if last:
                            for blk in range(NB):
                                t0 = blk * 512
                                S.dma('sp', xt.t[:], x2_d[s][:, t0:t0 + 512].rearrange("(k p) t -> p k t", p=128), [B_x2[s]], [xt])
                                pss = S.rot('nps', [PS[5], PS[6]])
                                for k in range(KD):
                                    sqk = S.rot('sq', sq)
                                    S.act(sqk.t[:], xt.t[:, k, :], AF.Square, [xt], [sqk])
                                    S.mm(pss.t[:, :], [(onesb.t[:], sqk.t[:])], [onesb, sqk], [pss], start=(k == 0), stop=(k == KD - 1))
                                S.rsqrt(rstd.t[:], pss.t[:, :], 1.0 / D, epsc.t[:, 0:1], [pss, epsc], [rstd])
                                for k in range(KD):
                                    xo_ = S.rot('fxo', xo)
                                    S.stt(xo_.t[:], xt.t[:, k, :], col("final_norm_g", k), rstd.t[:], ALU.mult, ALU.mult, [xt, cols, rstd], [xo_])
                                    S.dma('sp', yT[s, k * 128:(k + 1) * 128, t0:t0 + 512], xo_.t[:], [xo_], [B_y])
                    cur_d[s], cur_B[s] = x2_d[s], B_x2[s]
          except _Stop:
            break
        S.finish()
    return nc


def _fm(v):
    v = np.asarray(v, np.float32).reshape(-1)
    n = (v.size + 127) // 128
    p = np.zeros(n * 128, np.float32)
    p[:v.size] = v
    return np.ascontiguousarray(p.reshape(n, 128).T)


def _host_consts(S_):
    cst = np.zeros((128, NCST), np.float32)
    cst[:, 0:128] = np.eye(128)
    bd = np.zeros((128, 128), np.float32)
    bd[:64, :64] = 1
    bd[64:, 64:] = 1
    cst[:, 128:256] = bd
    p = np.arange(128)[:, None]
    f = np.arange(128)[None, :]
    cst[:, 256:384] = (p < f)
    cst[:, 384:512] = (p <= f)
    cst[:, 512:640] = (p > f)
    cst[:, 640:768] = (p >= f)
    cst[:64, 768] = 1
    cst[64:, 769] = 1
    sw = np.zeros((64, 64), np.float32)
    for m in range(32):
        sw[m + 32, m] = -1.0
        sw[m, m + 32] = 1.0
    cst[:64, 770:834] = sw
    inv = (1.0 / (np.float32(10000.0) ** (np.arange(0, 64, 2, dtype=np.float32) / np.float32(64)))).astype(np.float32)
    ang = np.arange(S_, dtype=np.float32)[:, None] * inv[None, :]
    cos = np.cos(ang).astype(np.float32).T
    sin = np.sin(ang).astype(np.float32).T
    return cst, np.ascontiguousarray(np.concatenate([cos, cos], 0)), np.ascontiguousarray(np.concatenate([sin, sin], 0))


def _pack_cols(I, L):
    out = np.zeros((L, 128, NCOL), np.float32)
    for l in range(L):
        def put(name, v):
            a = _fm(v)
            out[l, :, CO[name]:CO[name] + a.shape[1]] = a
        put("norm_mix_g", I["norm_mix_g"][l])
        put("norm_ffn_g", I["norm_ffn_g"][l])
        put("final_norm_g", I["final_norm_g"])
        put("ada_b", I["ada_b"][l])
        put("q_norm_g", I["q_norm_g"][l])
        put("kv_norm_g", I["kv_norm_g"][l])
        put("mu0", I["rwkv_mu"][l, 0])
        put("mu1", I["rwkv_mu"][l, 1])
        for d in range(2):
            put(f"w0_{d}", I["rwkv_w0"][l, d])
            put(f"a0_{d}", I["rwkv_a0"][l, d])
            put(f"r_k_{d}", I["rwkv_r_k"][l, d])
        put("k_k", I["rwkv_k_k"][l])
        put("k_a", I["rwkv_k_a"][l])
        for i in range(3):
            put(f"cw{i}", I["conv_w"][l, i])
        put("cb", I["conv_b"][l])
    return out


_NC_CACHE = {}


def run_groups(I, xs, cs, S_, NSEQ, L, n_cores):
    key = (S_, NSEQ, L)
    if key not in _NC_CACHE:
        _NC_CACHE[key] = build(S_, NSEQ, L)
    nc = _NC_CACHE[key]
    cst, rc, rs = _host_consts(S_)
    f32 = lambda a: np.ascontiguousarray(np.asarray(a, np.float32))
    shared = {
        "ada_w": f32(I["ada_w"]), "w_in": f32(I["w_in"]), "w_uq": f32(I["w_uq"]), "w_ukv": f32(I["w_ukv"]),
        "w_o_att": f32(I["w_o_att"]), "wdu": f32(I["rwkv_w_decay_up"]).reshape(L, 128, 1024),
        "wiu": f32(I["rwkv_w_iclr_up"]).reshape(L, 128, 1024), "wgu": f32(I["rwkv_w_gate_up"]),
        "w_o_rwkv": f32(I["w_o_rwkv"]), "w_out": f32(I["w_out"]), "w_up": f32(I["w_ffn_up"]), "w_down": f32(I["w_ffn_down"]),
        "gn_g": f32(I["rwkv_gn_g"]).reshape(L, 1, 1024), "gn_b": f32(I["rwkv_gn_b"]).reshape(L, 1, 1024),
        "cols": _pack_cols(I, L), "cst": cst, "ropec": rc, "ropes": rs,
    }
    in_maps = []
    for x, c in zip(xs, cs):
        m = dict(shared)
        m["xT"] = np.ascontiguousarray(np.transpose(x, (0, 2, 1)))
        m["c_fm"] = np.ascontiguousarray(np.transpose(np.asarray(c, np.float32).reshape(NSEQ, KD, 128), (2, 1, 0)))
        in_maps.append(m)
    res = run_bass_kernel_spmd(nc, in_maps, core_ids=list(range(n_cores)))
    return [np.ascontiguousarray(np.transpose(r["yT"], (0, 2, 1))) for r in res.results]


def kernel(**I):
    xp, xs_ = np.asarray(I["x_prompt"], np.float32), np.asarray(I["x_sample"], np.float32)
    cp, cs_ = np.asarray(I["c_prompt"], np.float32), np.asarray(I["c_sample"], np.float32)
    S_ = xp.shape[1]
    L = I["ada_w"].shape[0]
    allx = np.concatenate([xp, xs_], 0)
    allc = np.concatenate([cp, cs_], 0)
    n = allx.shape[0]
    NSEQ = 2
    assign = [[0, 1], [2, 3], [4, 5], [6, 7], [8, 8], [9, 9], [10, 10], [11, 11]]
    xs = [allx[a] for a in assign]
    cs = [allc[a] for a in assign]
    outs = run_groups(I, xs, cs, S_, NSEQ, L, 8)
    y = np.zeros_like(allx)
    for a, o in zip(assign, outs):
        for j, si in enumerate(a):
            y[si] = o[j]
    return (y[:xp.shape[0]], y[xp.shape[0]:])
```

```python
import numpy as np
from contextlib import ExitStack
import concourse.bass as bass
import concourse.mybir as mybir
from concourse.bass_utils import run_bass_kernel_spmd

F32 = mybir.dt.float32
BF16 = mybir.dt.bfloat16
AF = mybir.ActivationFunctionType
ALU = mybir.AluOpType
AX = mybir.AxisListType

D = 2048
KD = 16
DFF = 5632
NH = 8
DIN = 8416
C = 128


class Buf:
    __slots__ = ("w", "r")

    def __init__(self):
        self.w = None
        self.r = {}


class TT:
    __slots__ = ("t", "b")

    def __init__(self, t):
        self.t = t
        self.b = Buf()


class Sched:
    def __init__(self, nc, es, ndma=12):
        self.nc = nc
        self.eng = {'pe': nc.tensor, 'act': nc.scalar, 'dve': nc.vector, 'pool': nc.gpsimd, 'sp': nc.sync}
        self.sem = {}
        self.cnt = {}
        self.known = {k: {} for k in self.eng}
        for k in ['pe', 'act', 'dve', 'pool']:
            self.sem[k] = es.enter_context(nc.semaphore("s_" + k))
            self.cnt[k] = 0
        self.dpool, self.dcnt, self.drr = {}, {}, {}
        for q in ['sp', 'pool']:
            self.dpool[q] = [es.enter_context(nc.semaphore(f"d_{q}_{i}")) for i in range(ndma)]
            self.dcnt[q] = [0] * ndma
            self.drr[q] = 0
        self.rots = {}
        self.stopped = False
        _SREF[0] = self

    def rot(self, name, items):
        i = self.rots.get(name, 0)
        self.rots[name] = i + 1
        return items[i % len(items)]

    def _wait(self, e, deps):
        eng = self.eng[e]
        kn = self.known[e]
        for (sk, val) in deps:
            if kn.get(sk, 0) >= val:
                continue
            if sk[0] == 'c':
                if sk[1] == e and e == 'pe':
                    continue
                s = self.sem[sk[1]]
            else:
                s = self.dpool[sk[1]][sk[2]]
            eng.wait_ge(s, val)
            kn[sk] = val

    @staticmethod
    def _deps(reads, writes):
        deps = []
        for b in reads:
            if b.w is not None:
                deps.append(b.w)
        for b in writes:
            if b.w is not None:
                deps.append(b.w)
            deps.extend(b.r.items())
        return deps

    @staticmethod
    def _commit(ev, reads, writes):
        for b in reads:
            if b.r.get(ev[0], 0) < ev[1]:
                b.r[ev[0]] = ev[1]
        for b in writes:
            b.w = ev
            b.r = {}

    def op(self, e, fn, reads=(), writes=()):
        if self.stopped:
            return
        reads = [x.b if isinstance(x, TT) else x for x in reads]
        writes = [x.b if isinstance(x, TT) else x for x in writes]
        self._wait(e, self._deps(reads, writes))
        ins = fn()
        self.cnt[e] += 1
        ins.then_inc(self.sem[e], 1)
        ev = (('c', e), self.cnt[e])
        self._commit(ev, reads, writes)

    def dma(self, q, out, in_, reads=(), writes=(), **kw):
        if self.stopped:
            return
        reads = [x.b if isinstance(x, TT) else x for x in reads]
        writes = [x.b if isinstance(x, TT) else x for x in writes]
        i = self.drr[q]
        self.drr[q] = (i + 1) % len(self.dpool[q])
        sk = ('d', q, i)
        deps = self._deps(reads, writes)
        if self.dcnt[q][i] > 0:
            deps.append((sk, self.dcnt[q][i]))
        self._wait(q, deps)
        ins = self.eng[q].dma_start(out=out, in_=in_, **kw)
        self.dcnt[q][i] += 16
        ins.then_inc(self.dpool[q][i], 16)
        self._commit((sk, self.dcnt[q][i]), reads, writes)

    def barrier(self):
        if self.stopped:
            return
        evs = [(('c', e), self.cnt[e]) for e in self.cnt if self.cnt[e] > 0]
        for q in self.dpool:
            if q == 'pool':
                continue
            for i in range(len(self.dpool[q])):
                if self.dcnt[q][i] > 0:
                    evs.append((('d', q, i), self.dcnt[q][i]))
        for e in self.eng:
            self._wait(e, evs)

    def finish(self):
        for q in self.dpool:
            for i, s in enumerate(self.dpool[q]):
                if self.dcnt[q][i] > 0:
                    self.nc.sync.wait_ge(s, self.dcnt[q][i])

    def mm(self, out, pairs, reads, writes, start=True, stop=True):
        nc = self.nc

        def fn():
            ins = None
            n = len(pairs)
            for i, (l, r) in enumerate(pairs):
                ins = nc.tensor.matmul(out, lhsT=l, rhs=r, start=(start and i == 0), stop=(stop and i == n - 1))
            return ins
        self.op('pe', fn, reads, writes)

    def tr(self, out, in_, ident, reads, writes):
        nc = self.nc
        self.op('pe', lambda: nc.tensor.transpose(out, in_, ident), reads, writes)

    def act(self, out, in_, func, reads, writes, **kw):
        nc = self.nc
        self.op('act', lambda: nc.scalar.activation(out=out, in_=in_, func=func, **kw), reads, writes)

    def tt(self, out, in0, in1, op, reads, writes, e='dve'):
        eng = self.eng[e]
        self.op(e, lambda: eng.tensor_tensor(out=out, in0=in0, in1=in1, op=op), reads, writes)

    def ts(self, out, in0, s1, s2, op0, op1, reads, writes):
        nc = self.nc
        if s2 is None:
            self.op('dve', lambda: nc.vector.tensor_scalar(out=out, in0=in0, scalar1=s1, scalar2=None, op0=op0), reads, writes)
        else:
            self.op('dve', lambda: nc.vector.tensor_scalar(out=out, in0=in0, scalar1=s1, scalar2=s2, op0=op0, op1=op1), reads, writes)

    def stt(self, out, in0, scalar, in1, op0, op1, reads, writes):
        nc = self.nc
        self.op('dve', lambda: nc.vector.scalar_tensor_tensor(out=out, in0=in0, scalar=scalar, in1=in1, op0=op0, op1=op1), reads, writes)

    def cp(self, out, in_, reads, writes, e=None):
        nc = self.nc
        if e is None:
            e = self.rot('cp', ['act', 'dve'])
        if e == 'act':
            self.op('act', lambda: nc.scalar.copy(out=out, in_=in_), reads, writes)
        else:
            self.op('dve', lambda: nc.vector.tensor_copy(out=out, in_=in_), reads, writes)

    def rsqrt(self, out, in_, scale, eps_ap, reads, writes):
        self.act(out, in_, AF.Sqrt, reads, writes, scale=scale, bias=eps_ap)
        nc = self.nc
        self.op('dve', lambda: nc.vector.reciprocal(out=out, in_=out), writes, writes)


def _col_layout():
    items = [("norm_mix_g", 16), ("norm_ffn_g", 16), ("final_norm_g", 16), ("ada_b", 96), ("q_norm_g", 4),
             ("kv_norm_g", 2), ("mu0", 28), ("mu1", 28), ("w0_0", 8), ("w0_1", 8), ("a0_0", 8), ("a0_1", 8),
             ("k_k", 8), ("k_a", 8), ("r_k_0", 8), ("r_k_1", 8), ("cw0", 44), ("cw1", 44), ("cw2", 44), ("cb", 44)]
    co, off = {}, 0
    for n, w in items:
        co[n] = off
        off += w
    return co, off


CO, NCOL = _col_layout()
CST = {"ident": 0, "bd": 128, "lt": 256, "le": 384, "gt": 512, "ge": 640, "hsel": 768, "swap": 770}
NCST = 770 + 64


def _in_chunks():
    ch = []
    for j in range(4):
        ch.append(("q", j * 128, 128, j))
    for j in range(2):
        ch.append(("kv", 512 + j * 128, 128, j))
    ch.append(("rope", 768, 64, 0))
    for j in range(27):
        ch.append(("rw", 832 + j * 128, 128, j))
    ch.append(("rw", 832 + 27 * 128, 32, 27))
    for j in range(32):
        ch.append(("gate", 4320 + j * 128, 128, j))
    slabs, cur = [], []
    for c in ch:
        if cur and ((c[1] + c[2] - cur[0][1] > 512) or (c[0] != cur[-1][0] and c[0] in ("rw", "gate"))):
            slabs.append(cur)
            cur = []
        cur.append(c)
    slabs.append(cur)
    return slabs


class _Stop(Exception):
    pass


_KSTOP = [99]


_SREF = [None]


def _ck(n):
    if _KSTOP[0] <= n:
        _SREF[0].stopped = True


def build(S_, NSEQ, L):
    NB = S_ // 512
    NT = S_ // 128
    NCH = S_ // C
    nc = bass.Bass("TRN2", target_bir_lowering=False)

    def din(name, shape, dt=F32):
        return nc.dram_tensor(name, shape, dt, kind="ExternalInput").ap()

    def dscr(name, shape, dt=F32):
        return nc.dram_tensor(name, shape, dt, kind="Internal").ap()

    xT = din("xT", [NSEQ, D, S_])
    c_fm = din("c_fm", [128, KD, NSEQ])
    ada_w = din("ada_w", [L, D, 6 * D])
    w_in = din("w_in", [L, D, DIN])
    w_uq = din("w_uq", [L, 512, 1536])
    w_ukv = din("w_ukv", [L, 256, 2048])
    w_o_att = din("w_o_att", [L, 1024, D])
    wdu = din("wdu", [L, 128, 1024])
    wiu = din("wiu", [L, 128, 1024])
    wgu = din("wgu", [L, 160, 1024])
    w_o_rwkv = din("w_o_rwkv", [L, 1024, D])
    w_out = din("w_out", [L, D, D])
    w_up = din("w_up", [L, D, 2 * DFF])
    w_down = din("w_down", [L, DFF, D])
    gn_g = din("gn_g", [L, 1, 1024])
    gn_b = din("gn_b", [L, 1, 1024])
    cols_d = din("cols", [L, 128, NCOL])
    cst_d = din("cst", [128, NCST])
    ropec = din("ropec", [64, S_])
    ropes = din("ropes", [64, S_])
    yT = nc.dram_tensor("yT", [NSEQ, D, S_], F32, kind="ExternalOutput").ap()

    rw_d = [dscr(f"rw{s}", [3584, S_]) for s in range(NSEQ)]
    rwm_d = [dscr(f"rwm{s}", [3584, S_]) for s in range(NSEQ)]
    gate_d = [dscr(f"gate{s}", [4096, S_]) for s in range(NSEQ)]
    ga_d = [dscr(f"ga{s}", [D, S_]) for s in range(NSEQ)]
    od_d = [[dscr(f"od{s}_{d}", [S_, 1040]) for d in range(2)] for s in range(NSEQ)]
    vtm_d = [dscr(f"vtm{s}", [S_, 1024]) for s in range(NSEQ)]
    x1_d = [dscr(f"x1_{s}", [D, S_]) for s in range(NSEQ)]
    x2_d = [dscr(f"x2_{s}", [D, S_]) for s in range(NSEQ)]
    B_rw = [Buf() for _ in range(NSEQ)]
    B_rwm = [Buf() for _ in range(NSEQ)]
    B_gate = [Buf() for _ in range(NSEQ)]
    B_ga = [Buf() for _ in range(NSEQ)]
    B_od = [[Buf() for _ in range(2)] for _ in range(NSEQ)]
    B_vtm = [Buf() for _ in range(NSEQ)]
    B_x1 = [Buf() for _ in range(NSEQ)]
    B_x2 = [Buf() for _ in range(NSEQ)]
    B_y = Buf()
    B_in = Buf()

    with ExitStack() as es:
        S = Sched(nc, es)

        _tc = [0]

        def tile(st, name, shape, dt=F32):
            _tc[0] += 1
            return TT(st.enter_context(nc.sbuf_tensor(f"{name}_{_tc[0]}", shape, dt)))

        PS = [TT(es.enter_context(nc.psum_tensor(f"ps{i}", [128, 512], F32))) for i in range(7)]
        PSB = TT(es.enter_context(nc.psum_tensor("psb", [128, 1024], BF16)))

        cst = tile(es, "cst", [128, NCST])
        S.dma('sp', cst.t[:], cst_d, [B_in], [cst])
        ident = cst.t[:, 0:128]
        bd_ones = cst.t[:, 128:256]
        hsel = cst.t[:, 768:770]
        swapm = cst.t[0:64, 770:834]
        identb = tile(es, "identb", [128, 128], BF16)
        S.cp(identb.t[:], ident, [cst], [identb], e='dve')
        onesb = tile(es, "onesb", [128, 128], BF16)
        S.op('dve', lambda: nc.vector.memset(onesb.t[:], 1.0), [], [onesb])
        onesf = tile(es, "onesf", [128, 128])
        S.op('dve', lambda: nc.vector.memset(onesf.t[:], 1.0), [], [onesf])
        epsc = tile(es, "epsc", [128, 2])
        S.op('dve', lambda: nc.vector.memset(epsc.t[:, 0:1], 1e-6), [], [epsc])
        S.op('dve', lambda: nc.vector.memset(epsc.t[:, 1:2], 64e-5), [], [epsc])
        zero_c = tile(es, "zero_c", [128, 1])
        S.op('dve', lambda: nc.vector.memset(zero_c.t[:], 0.0), [], [zero_c])
        mA = [tile(es, f"mA{d}", [128, 256]) for d in range(2)]
        mN = [tile(es, f"mN{d}", [128, 512]) for d in range(2)]
        for d in range(2):
            s_, i_, n_ = (("lt", "le", "gt") if d == 0 else ("gt", "ge", "lt"))
            S.cp(mA[d].t[:, 0:128], cst.t[:, CST[s_]:CST[s_] + 128], [cst], [mA[d]], e='dve')
            S.cp(mA[d].t[:, 128:256], cst.t[:, CST[i_]:CST[i_] + 128], [cst], [mA[d]], e='dve')
            for q in range(4):
                S.cp(mN[d].t[:, q * 128:(q + 1) * 128], cst.t[:, CST[n_]:CST[n_] + 128], [cst], [mN[d]], e='dve')

        WB = {}
        for l in range(L):
            for (nm, src, shape) in (("w_in", w_in, [D, DIN]), ("w_uq", w_uq, [512, 1536]), ("w_ukv", w_ukv, [256, 2048]),
                                     ("w_o_att", w_o_att, [1024, D]), ("w_o_rwkv", w_o_rwkv, [1024, D]), ("w_out", w_out, [D, D]),
                                     ("w_up", w_up, [D, 2 * DFF]), ("w_down", w_down, [DFF, D])):
                t = dscr(f"{nm}_bf{l}", shape, BF16)
                Bs = []
                for r0 in range(0, shape[0], 256):
                    B = Buf()
                    S.dma('pool', t[r0:r0 + 256, :], src[l, r0:r0 + 256, :], [B_in], [B])
                    Bs.append(B)
                WB[(nm, l)] = (t, Bs)

        cur_d, cur_B = [xT[s] for s in range(NSEQ)], [B_in for _ in range(NSEQ)]

        for l in range(L):
          try:
            with ExitStack() as ls:
                S.barrier()
                cols = tile(ls, "cols", [128, NCOL])
                S.dma('sp', cols.t[:], cols_d[l], [B_in], [cols])

                def col(name, k=0, n=1):
                    return cols.t[:, CO[name] + k:CO[name] + k + n]
                modT = tile(ls, "modT", [128, NSEQ, 96])
                der = tile(ls, "der", [128, NSEQ, 2, KD])
                omka = tile(ls, "omka", [128, 8])
                S.ts(omka.t[:], col("k_a", 0, 8), -1.0, 1.0, ALU.mult, ALU.add, [cols], [omka])

                with ExitStack() as ph:
                    S.barrier()
                    csil = tile(ph, "csil", [128, KD, NSEQ])
                    S.dma('sp', csil.t[:], c_fm, [B_in], [csil])
                    S.act(csil.t[:], csil.t[:], AF.Silu, [csil], [csil])
                    slabs = [tile(ph, f"mslab{i}", [128, KD, 512]) for i in range(2)]
                    psm = PS[0]
                    for sb in range(24):
                        sl = slabs[sb % 2]
                        S.dma('sp', sl.t[:], ada_w[l, :, sb * 512:(sb + 1) * 512].rearrange("(k p) c -> p k c", p=128), [B_in], [sl])
                        for jj in range(4):
                            j = sb * 4 + jj
                            S.mm(psm.t[:, j * NSEQ:(j + 1) * NSEQ],
                                 [(sl.t[:, k, jj * 128:(jj + 1) * 128], csil.t[:, k, :]) for k in range(KD)],
                                 [sl, csil], [psm])
                    pv = psm.t[:, 0:96 * NSEQ].rearrange("p (j s) -> p j s", s=NSEQ)
                    for s in range(NSEQ):
                        S.tt(modT.t[:, s, :], pv[:, :, s], col("ada_b", 0, 96), ALU.add, [psm, cols], [modT])
                        S.stt(der.t[:, s, 0, :], modT.t[:, s, 16:32], 1.0, col("norm_mix_g", 0, 16), ALU.add, ALU.mult, [modT, cols], [der])
                        S.stt(der.t[:, s, 1, :], modT.t[:, s, 64:80], 1.0, col("norm_ffn_g", 0, 16), ALU.add, ALU.mult, [modT, cols], [der])

                _ck(1)
                for s in range(NSEQ):
                    sh1 = lambda k: modT.t[:, s, 0 + k:1 + k]
                    gt1 = lambda k: modT.t[:, s, 32 + k:33 + k]
                    sh2 = lambda k: modT.t[:, s, 48 + k:49 + k]
                    gt2 = lambda k: modT.t[:, s, 80 + k:81 + k]
                    A1 = lambda k: der.t[:, s, 0, k:k + 1]
                    A2 = lambda k: der.t[:, s, 1, k:k + 1]
                    xin_d, xin_B = cur_d[s], cur_B[s]

                    def norm_block(ph_tiles, src_d, src_B, t0, n, Acol, shcol):
                        xt, hT, sq, rstd, tmp = ph_tiles
                        S.dma('sp', xt.t[:, :, 0:n], src_d[:, t0:t0 + n].rearrange("(k p) t -> p k t", p=128), [src_B], [xt])
                        pss = S.rot('nps', [PS[5], PS[6]])
                        for k in range(KD):
                            sqk = S.rot('sq', sq)
                            S.act(sqk.t[:, 0:n], xt.t[:, k, 0:n], AF.Square, [xt], [sqk])
                            S.mm(pss.t[:, 0:n], [(onesb.t[:], sqk.t[:, 0:n])], [onesb, sqk], [pss], start=(k == 0), stop=(k == KD - 1))
                        S.rsqrt(rstd.t[:, 0:n], pss.t[:, 0:n], 1.0 / D, epsc.t[:, 0:1], [pss, epsc], [rstd])
                        for k in range(KD):
                            tk = S.rot('ntmp', tmp)
                            S.tt(tk.t[:, 0:n], xt.t[:, k, 0:n], rstd.t[:, 0:n], ALU.mult, [xt, rstd], [tk])
                            S.act(hT.t[:, k, 0:n], tk.t[:, 0:n], AF.Identity, [tk, modT, der], [hT], scale=Acol(k), bias=shcol(k))

                    with ExitStack() as mla:
                        S.barrier()
                        qdnT = tile(mla, "qdnT", [128, 4, S_], BF16)
                        kvnT = tile(mla, "kvnT", [128, 2, S_], BF16)
                        krT = tile(mla, "krT", [128, S_], BF16)
                        S.op('dve', lambda: nc.vector.memset(krT.t[64:128, :], 0.0), [], [krT])
                        with ExitStack() as ph:
                            S.barrier()
                            xt = tile(ph, "a_xt", [128, KD, 512])
                            hT = tile(ph, "a_hT", [128, KD, 512], BF16)
                            sq = [tile(ph, f"a_sq{i}", [128, 512], BF16) for i in range(2)]
                            rstd = tile(ph, "a_rstd", [128, 512])
                            tmp = [tile(ph, f"a_tmp{i}", [128, 512]) for i in range(2)]
                            wsl = [tile(ph, f"a_w{i}", [128, KD, 512], BF16) for i in range(2)]
                            stg = [tile(ph, f"a_stg{i}", [128, 4, 512]) for i in range(2)]
                            qd = tile(ph, "a_qd", [128, 6, 512])
                            sqf = [tile(ph, f"a_sqf{i}", [128, 512]) for i in range(2)]
                            kr = tile(ph, "a_kr", [64, 512])
                            krs = tile(ph, "a_krs", [64, 512])
                            cs = tile(ph, "a_cs", [64, 2, 512])
                            rq = tile(ph, "a_rq", [128, 2, 512])
                            slabsA = _in_chunks()
                            for blk in range(NB):
                                t0 = blk * 512
                                norm_block((xt, hT, sq, rstd, tmp), xin_d, xin_B, t0, 512, A1, sh1)
                                S.dma('sp', cs.t[:, 0, :], ropec[:, t0:t0 + 512], [B_in], [cs])
                                S.dma('sp', cs.t[:, 1, :], ropes[:, t0:t0 + 512], [B_in], [cs])
                                for si, slab in enumerate(slabsA):
                                    c0 = slab[0][1]
                                    c1 = slab[-1][1] + slab[-1][2]
                                    w = wsl[si % 2]
                                    S.dma('sp', w.t[:, :, 0:c1 - c0], WB[("w_in", l)][0][:, c0:c1].rearrange("(k p) c -> p k c", p=128), WB[("w_in", l)][1], [w])
                                    st = None
                                    nst = 0
                                    for (kind, cc, cw, idx) in slab:
                                        ps = S.rot('aps', PS[0:4])
                                        S.mm(ps.t[0:cw, :], [(w.t[:, k, cc - c0:cc - c0 + cw], hT.t[:, k, :]) for k in range(KD)], [w, hT], [ps])
                                        if kind == "q":
                                            S.cp(qd.t[:, idx, :], ps.t[:, :], [ps], [qd])
                                        elif kind == "kv":
                                            S.cp(qd.t[:, 4 + idx, :], ps.t[:, :], [ps], [qd])
                                        elif kind == "rope":
                                            S.cp(kr.t[:, :], ps.t[0:64, :], [ps], [kr], e='act')
                                            ps2 = S.rot('aps', PS[0:4])
                                            S.mm(ps2.t[0:64, :], [(swapm, kr.t[:, :])], [cst, kr], [ps2])
                                            S.tt(krs.t[:], ps2.t[0:64, :], cs.t[:, 1, :], ALU.mult, [ps2, cs], [krs])
                                            S.tt(kr.t[:], kr.t[:], cs.t[:, 0, :], ALU.mult, [kr, cs], [kr])
                                            S.tt(krT.t[0:64, t0:t0 + 512], kr.t[:], krs.t[:], ALU.add, [kr, krs], [krT])
                                        else:
                                            if st is None:
                                                st = S.rot('astg', stg)
                                                nst = 0
                                                st_first = (kind, idx)
                                            if kind == "gate":
                                                S.act(st.t[0:cw, nst, :], ps.t[0:cw, :], AF.Sigmoid, [ps], [st])
                                            else:
                                                S.cp(st.t[0:cw, nst, :], ps.t[0:cw, :], [ps], [st])
                                            nst += 1
                                    if st is not None:
                                        kind0, i0 = st_first
                                        dd, dB = (rw_d[s], B_rw[s]) if kind0 == "rw" else (gate_d[s], B_gate[s])
                                        nfull = nst - 1 if slab[-1][2] == 32 else nst
                                        if nfull > 0:
                                            S.dma('sp', dd[i0 * 128:(i0 + nfull) * 128, t0:t0 + 512].rearrange("(a p) t -> p a t", p=128),
                                                  st.t[:, 0:nfull, :], [st], [dB])
                                        if nfull < nst:
                                            S.dma('sp', dd[(i0 + nfull) * 128:(i0 + nfull) * 128 + 32, t0:t0 + 512], st.t[0:32, nfull, :], [st], [dB])
                                for (j0, nj, gname, dst, ps) in ((0, 4, "q_norm_g", qdnT, PS[4]), (4, 2, "kv_norm_g", kvnT, PS[4])):
                                    for j in range(nj):
                                        sqk = S.rot('sqf', sqf)
                                        S.act(sqk.t[:], qd.t[:, j0 + j, :], AF.Square, [qd], [sqk])
                                        S.mm(ps.t[:, :], [(onesf.t[:], sqk.t[:])], [onesf, sqk], [ps], start=(j == 0), stop=(j == nj - 1))
                                    rr = rq.t[:, 0, :]
                                    S.rsqrt(rr, ps.t[:, :], 1.0 / (128 * nj), epsc.t[:, 0:1], [ps, epsc], [rq])
                                    for j in range(nj):
                                        S.stt(dst.t[:, j, t0:t0 + 512], qd.t[:, j0 + j, :], col(gname, j), rr, ALU.mult, ALU.mult, [qd, cols, rq], [dst])

                        _ck(2)
                        with ExitStack() as ph:
                            S.barrier()
                            oT = tile(ph, "b_oT", [128, NH, S_], BF16)
                            with ExitStack() as ph2:
                                S.barrier()
                                wuq = tile(ph2, "b_wuq", [128, 4, 1536], BF16)
                                wukv = tile(ph2, "b_wukv", [128, 2, 2048], BF16)
                                S.dma('sp', wuq.t[:], WB[("w_uq", l)][0].rearrange("(k p) c -> p k c", p=128), WB[("w_uq", l)][1], [wuq])
                                S.dma('sp', wukv.t[:], WB[("w_ukv", l)][0].rearrange("(k p) c -> p k c", p=128), WB[("w_ukv", l)][1], [wukv])
                                knT = tile(ph2, "b_knT", [128, S_], BF16)
                                Vt = tile(ph2, "b_Vt", [128, NT, 128], BF16)
                                qnT = tile(ph2, "b_qnT", [128, S_], BF16)
                                qrT = tile(ph2, "b_qrT", [128, S_], BF16)
                                S.op('dve', lambda: nc.vector.memset(qrT.t[64:128, :], 0.0), [], [qrT])
                                qr = tile(ph2, "b_qr", [64, 512])
                                qrs = tile(ph2, "b_qrs", [64, 512])
                                cs = tile(ph2, "b_cs", [64, 2, 512])
                                pT = [tile(ph2, f"b_pT{i}", [128, 512], BF16) for i in range(3)]
                                rec = tile(ph2, "b_rec", [128, 512])
                                scale = 192.0 ** -0.5
                                for h in range(NH):
                                    for blk in range(NB):
                                        t0 = blk * 512
                                        ps = S.rot('bps', PS[0:3])
                                        S.mm(ps.t[:, :], [(wukv.t[:, r, h * 256:h * 256 + 128], kvnT.t[:, r, t0:t0 + 512]) for r in range(2)], [wukv, kvnT], [ps])
                                        S.cp(knT.t[:, t0:t0 + 512], ps.t[:, :], [ps], [knT])
                                        ps = S.rot('bps', PS[0:3])
                                        for q in range(4):
                                            tt_ = blk * 4 + q
                                            S.mm(ps.t[:, q * 128:(q + 1) * 128], [(kvnT.t[:, r, tt_ * 128:(tt_ + 1) * 128], wukv.t[:, r, h * 256 + 128:h * 256 + 256]) for r in range(2)], [wukv, kvnT], [ps])
                                        S.cp(Vt.t[:, blk * 4:blk * 4 + 4, :], ps.t[:, :].rearrange("p (a b) -> p a b", b=128), [ps], [Vt])
                                        ps = S.rot('bps', PS[0:3])
                                        S.mm(ps.t[:, :], [(wuq.t[:, r, h * 192:h * 192 + 128], qdnT.t[:, r, t0:t0 + 512]) for r in range(4)], [wuq, qdnT], [ps])
                                        S.cp(qnT.t[:, t0:t0 + 512], ps.t[:, :], [ps], [qnT])
                                        ps = S.rot('bps', PS[0:3])
                                        S.mm(ps.t[0:64, :], [(wuq.t[:, r, h * 192 + 128:h * 192 + 192], qdnT.t[:, r, t0:t0 + 512]) for r in range(4)], [wuq, qdnT], [ps])
                                        S.cp(qr.t[:, :], ps.t[0:64, :], [ps], [qr], e='act')
                                        S.dma('sp', cs.t[:, 0, :], ropec[:, t0:t0 + 512], [B_in], [cs])
                                        S.dma('sp', cs.t[:, 1, :], ropes[:, t0:t0 + 512], [B_in], [cs])
                                        ps2 = S.rot('bps', PS[0:3])
                                        S.mm(ps2.t[0:64, :], [(swapm, qr.t[:, :])], [cst, qr], [ps2])
                                        S.tt(qrs.t[:], ps2.t[0:64, :], cs.t[:, 1, :], ALU.mult, [ps2, cs], [qrs])
                                        S.tt(qr.t[:], qr.t[:], cs.t[:, 0, :], ALU.mult, [qr, cs], [qr])
                                        S.tt(qrT.t[0:64, t0:t0 + 512], qr.t[:], qrs.t[:], ALU.add, [qr, qrs], [qrT])
                                    for qb in range(NB):
                                        q0 = qb * 512
                                        psO, psD = PS[3], PS[4]
                                        for kt in range(NT):
                                            k0 = kt * 128
                                            ps = S.rot('sps', [PS[5], PS[6], PS[0]])
                                            S.mm(ps.t[:, :], [(knT.t[:, k0:k0 + 128], qnT.t[:, q0:q0 + 512]), (krT.t[:, k0:k0 + 128], qrT.t[:, q0:q0 + 512])], [knT, qnT, krT, qrT], [ps])
                                            p = S.rot('pT', pT)
                                            S.act(p.t[:], ps.t[:, :], AF.Exp, [ps], [p], scale=scale)
                                            S.mm(psO.t[:, :], [(Vt.t[:, kt, :], p.t[:])], [Vt, p], [psO], start=(kt == 0), stop=(kt == NT - 1))
                                            S.mm(psD.t[:, :], [(onesb.t[:], p.t[:])], [onesb, p], [psD], start=(kt == 0), stop=(kt == NT - 1))
                                        S.op('dve', lambda: nc.vector.reciprocal(out=rec.t[:], in_=psD.t[:, :]), [psD], [rec])
                                        S.tt(oT.t[:, h, q0:q0 + 512], psO.t[:, :], rec.t[:], ALU.mult, [psO, rec], [oT])
                            _ck(3)
                            with ExitStack() as ph2:
                                S.barrier()
                                woa = tile(ph2, "b_woa", [128, NH, D], BF16)
                                S.dma('sp', woa.t[:], WB[("w_o_att", l)][0].rearrange("(k p) c -> p k c", p=128), WB[("w_o_att", l)][1], [woa])
                                gts = [tile(ph2, f"b_g{i}", [128, 4, 512]) for i in range(2)]
                                ost = [tile(ph2, f"b_o{i}", [128, 4, 512]) for i in range(2)]
                                for blk in range(NB):
                                    t0 = blk * 512
                                    for dg in range(4):
                                        g = S.rot('bg', gts)
                                        o = S.rot('bo', ost)
                                        S.dma('sp', g.t[:], gate_d[s][dg * 512:(dg + 1) * 512, t0:t0 + 512].rearrange("(a p) t -> p a t", p=128), [B_gate[s]], [g])
                                        for a in range(4):
                                            dc = dg * 4 + a
                                            ps = S.rot('bps', PS[0:3])
                                            S.mm(ps.t[:, :], [(woa.t[:, h, dc * 128:(dc + 1) * 128], oT.t[:, h, t0:t0 + 512]) for h in range(NH)], [woa, oT], [ps])
                                            S.tt(o.t[:, a, :], ps.t[:, :], g.t[:, a, :], ALU.mult, [ps, g], [o])
                                        S.dma('sp', ga_d[s][dg * 512:(dg + 1) * 512, t0:t0 + 512].rearrange("(a p) t -> p a t", p=128), o.t[:], [o], [B_ga[s]])

                    _ck(4)
                    with ExitStack() as ph:
                        S.barrier()
                        pin = [tile(ph, f"c0_in{i}", [128, S_ + 2]) for i in range(2)]
                        pout = [tile(ph, f"c0_out{i}", [128, S_]) for i in range(2)]
                        cm = tile(ph, "c0_cm", [128, 28])
                        S.tt(cm.t[:], col("mu0", 0, 28), col("mu1", 0, 28), ALU.add, [cols], [cm])
                        S.ts(cm.t[:], cm.t[:], -1.0, 1.0, ALU.mult, ALU.add, [cm], [cm])
                        for i in range(2):
                            S.op('dve', lambda: nc.vector.memset(pin[i].t[:, 0:1], 0.0), [], [pin[i]])
                            S.op('dve', lambda: nc.vector.memset(pin[i].t[:, S_ + 1:S_ + 2], 0.0), [], [pin[i]])
                        for j in range(28):
                            np_ = 128 if j < 27 else 32
                            a, o = pin[j % 2], pout[j % 2]
                            S.dma('sp', a.t[0:np_, 1:S_ + 1], rw_d[s][j * 128:j * 128 + np_, :], [B_rw[s]], [a])
                            S.ts(o.t[0:np_, :], a.t[0:np_, 1:S_ + 1], cm.t[0:np_, j:j + 1], None, ALU.mult, None, [a, cm], [o])
                            S.stt(o.t[0:np_, :], a.t[0:np_, 0:S_], col("mu0", j)[0:np_], o.t[0:np_, :], ALU.mult, ALU.add, [a, cols, o], [o])
                            S.stt(o.t[0:np_, :], a.t[0:np_, 2:S_ + 2], col("mu1", j)[0:np_], o.t[0:np_, :], ALU.mult, ALU.add, [a, cols, o], [o])
                            S.dma('sp', rwm_d[s][j * 128:j * 128 + np_, :], o.t[0:np_, :], [o], [B_rwm[s]])

                    _ck(5)
                    with ExitStack() as ph:
                        S.barrier()
                        wdu_t = tile(ph, "c_wdu", [128, 1024])
                        wiu_t = tile(ph, "c_wiu", [128, 1024])
                        S.dma('sp', wdu_t.t[:], wdu[l], [B_in], [wdu_t])
                        S.dma('sp', wiu_t.t[:], wiu[l], [B_in], [wiu_t])
                        X = [tile(ph, f"c_x{i}", [128, 8, 128]) for i in range(14)]
                        AR8 = tile(ph, "c_AR8", [128, 8, 2, 128])
                        AM = tile(ph, "c_AM", [128, 16, 384])
                        NP = [tile(ph, f"c_NP{i}", [128, 16, 128]) for i in range(2)]
                        GP = [tile(ph, f"c_GP{i}", [128, 16, 128]) for i in range(2)]
                        Y = tile(ph, "c_Y", [128, 16, 128])
                        WTin = tile(ph, "c_WTin", [128, 16, 64])
                        WT8 = tile(ph, "c_WT8", [128, 8, 128])
                        MT = tile(ph, "c_MT", [128, 16, 64])
                        Vtm = tile(ph, "c_Vtm", [128, 8, 128])
                        BHp = tile(ph, "c_BHp", [128, 16, 128])
                        KHp = tile(ph, "c_KHp", [128, 16, 128])
                        OT = tile(ph, "c_OT", [128, 1040])
                        wda = tile(ph, "c_wda", [128, 2, 128])
                        twd = tile(ph, "c_twd", [128, 128])
                        adm = tile(ph, "c_adm", [128, 128])
                        stm = [tile(ph, f"c_stm{i}", [128, 8, 64]) for i in range(2)]
                        PCt = tile(ph, "c_PC", [128, 8])
                        ST = [tile(ph, f"c_ST{d}", [128, 8, 64]) for d in range(2)]
                        stmp = tile(ph, "c_stmp", [128, 8, 64])
                        gB = {n: [Buf() for _ in range(4)] for n in ("AM", "N0", "N1", "G0", "G1", "Y")}
                        S.op('dve', lambda: nc.vector.memset(BHp.t[:], 0.0), [], [BHp])
                        S.op('dve', lambda: nc.vector.memset(KHp.t[:], 0.0), [], [KHp])
                        for d in range(2):
                            S.op('dve', lambda: nc.vector.memset(ST[d].t[:], 0.0), [], [ST[d]])
                        (r8, k8, v8, lw8, a8, L8, eLx, eLi, eNL, eCL, kk8, t8, b8, u8) = X
                        bc8 = lambda ap: ap.unsqueeze(2).broadcast_to([128, 8, 128])

                        def chunk_step(d, c):
                            t0 = c * C
                            rows = lambda base: rwm_d[s][base:base + 1024, t0:t0 + C].rearrange("(a p) t -> p a t", p=128)
                            S.dma('sp', r8.t[:], rows(0), [B_rwm[s]], [r8])
                            S.dma('sp', k8.t[:], rows(1024), [B_rwm[s]], [k8])
                            S.dma('sp', v8.t[:], rows(2048), [B_rwm[s]], [v8])
                            S.dma('sp', wda.t[:], rwm_d[s][3072:3328, t0:t0 + C].rearrange("(a p) t -> p a t", p=128), [B_rwm[s]], [wda])
                            S.act(twd.t[:], wda.t[:, 0, :], AF.Tanh, [wda], [twd])
                            S.ts(twd.t[:], twd.t[:], hsel[:, d:d + 1], None, ALU.mult, None, [twd, cst], [twd])
                            S.ts(adm.t[:], wda.t[:, 1, :], hsel[:, d:d + 1], None, ALU.mult, None, [wda, cst], [adm])
                            for (wt, src, bname, dst) in ((wdu_t, twd.t, f"w0_{d}", lw8), (wiu_t, None, f"a0_{d}", a8)):
                                for g2 in range(2):
                                    ps = S.rot('cps', PS[0:4])
                                    for q in range(4):
                                        hp = g2 * 4 + q
                                        rhs = twd.t[:, :] if src is not None else adm.t[:, :]
                                        S.mm(ps.t[:, q * 128:(q + 1) * 128], [(wt.t[:, hp * 128:(hp + 1) * 128], rhs)], [wt, twd, adm], [ps])
                                    for q in range(4):
                                        hp = g2 * 4 + q
                                        S.act(dst.t[:, hp, :], ps.t[:, q * 128:(q + 1) * 128], AF.Sigmoid, [ps, cols], [dst], bias=col(bname, hp))
                            _ck(5.1)
                            S.ts(lw8.t[:], lw8.t[:], -0.6065306597126334, None, ALU.mult, None, [lw8], [lw8])
                            for hp in range(8):
                                S.op('dve', lambda: nc.vector.tensor_tensor_scan(out=L8.t[:, hp, :], data0=lw8.t[:, hp, :], data1=lw8.t[:, hp, :], initial=0.0, op0=ALU.add, op1=ALU.bypass), [lw8], [L8])
                            LCb = L8.t[:, :, 127:128].broadcast_to([128, 8, 128])
                            S.act(PCt.t[:], L8.t[:, :, 127], AF.Exp, [L8], [PCt])
                            if d == 0:
                                S.tt(eLx.t[:], L8.t[:], lw8.t[:], ALU.subtract, [L8, lw8], [eLx])
                                S.tt(eCL.t[:], LCb, L8.t[:], ALU.subtract, [L8], [eCL])
                                S.act(eLi.t[:], L8.t[:], AF.Exp, [L8], [eLi])
                                S.act(eNL.t[:], L8.t[:], AF.Exp, [L8], [eNL], scale=-1.0)
                            else:
                                S.tt(eLx.t[:], LCb, L8.t[:], ALU.subtract, [L8], [eLx])
                                S.tt(eLi.t[:], eLx.t[:], lw8.t[:], ALU.add, [eLx, lw8], [eLi])
                                S.tt(eCL.t[:], LCb, eLi.t[:], ALU.subtract, [L8, eLi], [eCL])
                                S.act(eNL.t[:], eLi.t[:], AF.Exp, [eLi], [eNL], scale=-1.0)
                                S.act(eLi.t[:], eLi.t[:], AF.Exp, [eLi], [eLi])
                            S.act(eLx.t[:], eLx.t[:], AF.Exp, [eLx], [eLx])
                            S.act(eCL.t[:], eCL.t[:], AF.Exp, [eCL], [eCL])
                            _ck(5.2)
                            S.tt(kk8.t[:], k8.t[:], bc8(col("k_k", 0, 8)), ALU.mult, [k8, cols], [kk8])
                            S.tt(u8.t[:], kk8.t[:], kk8.t[:], ALU.mult, [kk8], [u8])
                            for g2 in range(2):
                                ps = S.rot('cps', PS[0:4])
                                for q in range(4):
                                    S.mm(ps.t[:, q * 128:(q + 1) * 128], [(bd_ones, u8.t[:, g2 * 4 + q, :])], [cst, u8], [ps])
                                S.ts(L8.t[:, g2 * 4:g2 * 4 + 4, :], ps.t[:, :].rearrange("p (a b) -> p a b", b=128), 1e-24, None, ALU.max, None, [ps], [L8])
                            S.act(L8.t[:], L8.t[:], AF.Sqrt, [L8], [L8])
                            S.op('dve', lambda: nc.vector.reciprocal(out=L8.t[:], in_=L8.t[:]), [L8], [L8])
                            S.tt(kk8.t[:], kk8.t[:], L8.t[:], ALU.mult, [kk8, L8], [kk8])
                            S.tt(t8.t[:], a8.t[:], bc8(col("k_a", 0, 8)), ALU.mult, [a8, cols], [t8])
                            S.tt(t8.t[:], t8.t[:], bc8(omka.t[:, 0:8]), ALU.add, [t8, omka], [t8])
                            S.tt(t8.t[:], t8.t[:], k8.t[:], ALU.mult, [t8, k8], [t8])
                            S.tt(b8.t[:], kk8.t[:], a8.t[:], ALU.mult, [kk8, a8], [b8])
                            S.stt(AR8.t[:, :, 0, :], kk8.t[:], -1.0, eLx.t[:], ALU.mult, ALU.mult, [kk8, eLx], [AR8])
                            S.tt(AR8.t[:, :, 1, :], r8.t[:], eLi.t[:], ALU.mult, [r8, eLi], [AR8])
                            S.tt(u8.t[:], r8.t[:], t8.t[:], ALU.mult, [r8, t8], [u8])
                            S.tt(u8.t[:], u8.t[:], bc8(col(f"r_k_{d}", 0, 8)), ALU.mult, [u8, cols], [u8])
                            psB = S.rot('cps', PS[0:4])
                            for hp in range(8):
                                S.mm(psB.t[:, hp * 2:hp * 2 + 2], [(u8.t[:, hp, :], hsel)], [u8, cst], [psB])
                            S.cp(OT.t[:, 1024:1040], psB.t[:, 0:16], [psB], [OT], e='act')
                            bt8, kt8, bh8, kh8 = a8, lw8, eLx, eLi
                            S.tt(bh8.t[:], b8.t[:], eCL.t[:], ALU.mult, [b8, eCL], [bh8])
                            S.tt(kh8.t[:], t8.t[:], eCL.t[:], ALU.mult, [t8, eCL], [kh8])
                            S.tt(bt8.t[:], b8.t[:], eNL.t[:], ALU.mult, [b8, eNL], [bt8])
                            S.tt(kt8.t[:], t8.t[:], eNL.t[:], ALU.mult, [t8, eNL], [kt8])
                            _ck(5.3)
                            for g2 in range(2):
                                ps = S.rot('cps', PS[0:4])
                                for q in range(4):
                                    S.tr(ps.t[:, q * 128:(q + 1) * 128], v8.t[:, g2 * 4 + q, :], ident, [v8, cst], [ps])
                                S.cp(Vtm.t[:, g2 * 4:g2 * 4 + 4, :], ps.t[:, :].rearrange("p (a b) -> p a b", b=128), [ps], [Vtm])
                                ps = S.rot('cps', PS[0:4])
                                for q in range(4):
                                    S.tr(ps.t[:, q * 128:(q + 1) * 128], AR8.t[:, g2 * 4 + q, 0, :], ident, [AR8, cst], [ps])
                                S.cp(Y.t[:, g2 * 8:g2 * 8 + 8, 0:64], ps.t[:, :].rearrange("p (a b) -> p a b", b=64), [ps], [gB["Y"][2 * g2], gB["Y"][2 * g2 + 1]])
                                for (src, dstp) in ((bh8, BHp), (kh8, KHp)):
                                    ps = S.rot('cps', PS[0:4])
                                    for q in range(4):
                                        S.tr(ps.t[:, q * 128:(q + 1) * 128], src.t[:, g2 * 4 + q, :], ident, [src, cst], [ps])
                                    pv4 = ps.t[:, :].rearrange("p (a h b) -> p a h b", h=2, b=64)
                                    dv4 = dstp.t[:, g2 * 8:g2 * 8 + 8, :].rearrange("p (a h) c -> p a h c", h=2)
                                    S.cp(dv4[:, :, 0, 0:64], pv4[:, :, 0, :], [ps], [dstp])
                                    S.cp(dv4[:, :, 1, 64:128], pv4[:, :, 1, :], [ps], [dstp])
                            if d == 0:
                                S.dma('sp', vtm_d[s][t0:t0 + C, :], Vtm.t[:].rearrange("p a b -> p (a b)"), [Vtm], [B_vtm[s]])
                            _ck(5.4)
                            am = [r8, k8]
                            rm = [L8, eNL]
                            for hh in range(2):
                                S.ts(am[hh].t[:], AR8.t[:, :, 0, :], hsel[:, hh:hh + 1], None, ALU.mult, None, [AR8, cst], [am[hh]])
                                S.ts(rm[hh].t[:], AR8.t[:, :, 1, :], hsel[:, hh:hh + 1], None, ALU.mult, None, [AR8, cst], [rm[hh]])
                            Aak = GP[1]
                            for h in range(16):
                                hp, hh = h // 2, h % 2
                                sl = slice(64 * hh, 64 * hh + 64)
                                g4 = h // 4
                                ps = S.rot('cps', PS[0:4])
                                S.mm(ps.t[:, 0:128], [(bt8.t[:, hp, :], am[hh].t[:, hp, :])], [bt8, am[hh]], [ps])
                                S.mm(ps.t[:, 128:256], [(bt8.t[:, hp, :], rm[hh].t[:, hp, :])], [bt8, rm[hh]], [ps])
                                S.mm(ps.t[:, 256:384], [(kt8.t[:, hp, :], am[hh].t[:, hp, :])], [kt8, am[hh]], [ps])
                                S.mm(ps.t[:, 384:512], [(kt8.t[:, hp, :], rm[hh].t[:, hp, :])], [kt8, rm[hh]], [ps])
                                S.tt(AM.t[:, h, 0:256], ps.t[:, 0:256], mA[d].t[:], ALU.mult, [ps, mA[d]], [gB["AM"][g4]])
                                S.tt(Aak.t[:, h, :], ps.t[:, 256:384], mA[d].t[:, 0:128], ALU.mult, [ps, mA[d]], [gB["G1"][g4]])
                                S.tt(AM.t[:, h, 256:384], ps.t[:, 384:512], mA[d].t[:, 128:256], ALU.mult, [ps, mA[d]], [gB["AM"][g4]])
                            for g4 in range(4):
                                ps = S.rot('cps', PS[0:4])
                                for q in range(4):
                                    h = g4 * 4 + q
                                    hp, hh = h // 2, h % 2
                                    sl = slice(64 * hh, 64 * hh + 64)
                                    S.mm(ps.t[:, q * 128:(q + 1) * 128], [(am[hh].t[:, hp, :], bt8.t[:, hp, :])], [am[hh], bt8], [ps])
                                S.tt(NP[0].t[:, g4 * 4:g4 * 4 + 4, :], ps.t[:, :].rearrange("p (a b) -> p a b", b=128), mN[d].t[:].rearrange("p (a b) -> p a b", b=128), ALU.mult, [ps, mN[d]], [gB["N0"][g4]])
                            for g8 in range(2):
                                ps = S.rot('cps', PS[0:4])
                                for q in range(8):
                                    h = g8 * 8 + q
                                    hp, hh = h // 2, h % 2
                                    S.mm(ps.t[:, q * 64:(q + 1) * 64], [(Aak.t[:, h, :], Vtm.t[:, hp, 64 * hh:64 * hh + 64])], [gB["G1"][h // 4], Vtm], [ps])
                                S.cp(Y.t[:, g8 * 8:g8 * 8 + 8, 64:128], ps.t[:, :].rearrange("p (a b) -> p a b", b=64), [ps], [gB["Y"][2 * g8], gB["Y"][2 * g8 + 1]])
                            _ck(5.5)
                            for p_ in range(7):
                                if p_ == 0:
                                    Gt, Gb = (lambda h: AM.t[:, h, 0:128]), gB["AM"]
                                else:
                                    gi = (p_ - 1) % 2
                                    Gt, Gb = (lambda h, gi=gi: GP[gi].t[:, h, :]), gB[f"G{gi}"]
                                ni = p_ % 2
                                Nt, Nb = (lambda h, ni=ni: NP[ni].t[:, h, :]), gB[f"N{ni}"]
                                for g4 in range(4):
                                    ps = S.rot('cps', PS[0:4])
                                    for q in range(4):
                                        h = g4 * 4 + q
                                        S.mm(ps.t[:, q * 128:(q + 1) * 128], [(Gt(h), Y.t[:, h, :])], [Gb[g4], gB["Y"][g4]], [ps])
                                    S.tt(Y.t[:, g4 * 4:g4 * 4 + 4, :], ps.t[:, :].rearrange("p (a b) -> p a b", b=128), Y.t[:, g4 * 4:g4 * 4 + 4, :], ALU.add, [ps, gB["Y"][g4]], [gB["Y"][g4]])
                                if p_ == 6:
                                    break
                                go = p_ % 2
                                no = (p_ + 1) % 2
                                for g4 in range(4):
                                    ps = S.rot('cps', PS[0:4])
                                    for q in range(4):
                                        h = g4 * 4 + q
                                        S.mm(ps.t[:, q * 128:(q + 1) * 128], [(Nt(h), Gt(h))], [Gb[g4], Nb[g4]], [ps])
                                    S.cp(GP[go].t[:, g4 * 4:g4 * 4 + 4, :], ps.t[:, :].rearrange("p (a b) -> p a b", b=128), [ps], [gB[f"G{go}"][g4]])
                                    if p_ < 5:
                                        ps = S.rot('cps', PS[0:4])
                                        for q in range(4):
                                            h = g4 * 4 + q
                                            S.mm(ps.t[:, q * 128:(q + 1) * 128], [(Gt(h), Nt(h))], [Gb[g4], Nb[g4]], [ps])
                                        S.cp(NP[no].t[:, g4 * 4:g4 * 4 + 4, :], ps.t[:, :].rearrange("p (a b) -> p a b", b=128), [ps], [gB[f"N{no}"][g4]])
                            _ck(5.6)
                            S.cp(WTin.t[:], Y.t[:, :, 0:64], gB["Y"], [WTin], e='dve')
                            for g2 in range(2):
                                ps = S.rot('cps', PS[0:4])
                                for q in range(4):
                                    hp = g2 * 4 + q
                                    S.tr(ps.t[:, q * 128:(q + 1) * 128], WTin.t[:, 2 * hp:2 * hp + 2, :].rearrange("p a b -> p (a b)"), ident, [WTin, cst], [ps])
                                S.cp(WT8.t[:, g2 * 4:g2 * 4 + 4, :], ps.t[:, :].rearrange("p (a b) -> p a b", b=128), [ps], [WT8])
                            _ck(5.7)
                            st = ST[d]
                            for hh in range(2):
                                S.ts(stm[hh].t[:], st.t[:], hsel[:, hh:hh + 1], None, ALU.mult, None, [st, cst], [stm[hh]])
                            for g8 in range(2):
                                ps = PS[4 + g8]
                                for q in range(8):
                                    h = g8 * 8 + q
                                    hp, hh = h // 2, h % 2
                                    sl = slice(64 * hh, 64 * hh + 64)
                                    S.mm(ps.t[:, q * 64:(q + 1) * 64], [(WT8.t[:, hp, :], stm[hh].t[:, hp, :])], [WT8, stm[hh]], [ps])
                                S.tt(MT.t[:, g8 * 8:g8 * 8 + 8, :], ps.t[:, :].rearrange("p (a b) -> p a b", b=64), Y.t[:, g8 * 8:g8 * 8 + 8, 64:128], ALU.add, [ps, gB["Y"][2 * g8], gB["Y"][2 * g8 + 1]], [MT])
                            for g8 in range(2):
                                ps = PS[4 + g8]
                                for q in range(8):
                                    h = g8 * 8 + q
                                    hp, hh = h // 2, h % 2
                                    sl = slice(64 * hh, 64 * hh + 64)
                                    S.mm(ps.t[:, q * 64:(q + 1) * 64],
                                         [(rm[hh].t[:, hp, :], st.t[:, hp, :]), (AM.t[:, h, 128:256], MT.t[:, h, :]), (AM.t[:, h, 256:384], Vtm.t[:, hp, 64 * hh:64 * hh + 64])],
                                         [rm[hh], st, gB["AM"][h // 4], MT, Vtm], [ps])
                                S.cp(OT.t[:, g8 * 512:(g8 + 1) * 512], ps.t[:, :], [ps], [OT], e='act')
                            S.dma('sp', od_d[s][d][t0:t0 + C, :], OT.t[:], [OT], [B_od[s][d]])
                            ps = PS[6]
                            for hp in range(8):
                                S.mm(ps.t[:, hp * 64:(hp + 1) * 64],
                                     [(BHp.t[:, 2 * hp, :], MT.t[:, 2 * hp, :]), (KHp.t[:, 2 * hp, :], Vtm.t[:, hp, 0:64]),
                                      (BHp.t[:, 2 * hp + 1, :], MT.t[:, 2 * hp + 1, :]), (KHp.t[:, 2 * hp + 1, :], Vtm.t[:, hp, 64:128])],
                                     [BHp, KHp, MT, Vtm], [ps])
                            S.tt(stmp.t[:], st.t[:], PCt.t[:].unsqueeze(2).broadcast_to([128, 8, 64]), ALU.mult, [st, PCt], [stmp])
                            S.tt(st.t[:], stmp.t[:], ps.t[:, :].rearrange("p (a b) -> p a b", b=64), ALU.add, [stmp, ps], [st])

                        for i in range(NCH):
                            chunk_step(0, i)
                            chunk_step(1, NCH - 1 - i)

                    _ck(6)
                    with ExitStack() as ph:
                        S.barrier()
                        worw = tile(ph, "d_worw", [128, 8, D], BF16)
                        wout = tile(ph, "d_wout", [128, KD, D], BF16)
                        S.dma('sp', worw.t[:], WB[("w_o_rwkv", l)][0].rearrange("(k p) c -> p k c", p=128), WB[("w_o_rwkv", l)][1], [worw])
                        S.dma('sp', wout.t[:], WB[("w_out", l)][0].rearrange("(k p) c -> p k c", p=128), WB[("w_out", l)][1], [wout])
                        wgA = tile(ph, "d_wgA", [128, 1024])
                        wgB = tile(ph, "d_wgB", [32, 1024])
                        S.dma('sp', wgA.t[:], wgu[l, 0:128, :], [B_in], [wgA])
                        S.dma('sp', wgB.t[:], wgu[l, 128:160, :], [B_in], [wgB])
                        gng = tile(ph, "d_gng", [128, 1024])
                        gnb = tile(ph, "d_gnb", [128, 1024])
                        S.dma('sp', gng.t[:], gn_g[l].partition_broadcast(128), [B_in], [gng])
                        S.dma('sp', gnb.t[:], gn_b[l].partition_broadcast(128), [B_in], [gnb])
                        of = [tile(ph, f"d_of{i}", [128, 1040]) for i in range(2)]
                        ob = [tile(ph, f"d_ob{i}", [128, 1040]) for i in range(2)]
                        vt = [tile(ph, f"d_vt{i}", [128, 1024]) for i in range(2)]
                        gdA = tile(ph, "d_gdA", [128, 128])
                        gdB = tile(ph, "d_gdB", [32, 128])
                        o3 = tile(ph, "d_o3", [128, 16, 64])
                        sq3 = tile(ph, "d_sq3", [128, 16, 64])
                        st16 = tile(ph, "d_st16", [128, 4, 16])
                        ofin = tile(ph, "d_ofin", [128, 1024], BF16)
                        ofT = tile(ph, "d_ofT", [128, 8, 512], BF16)
                        miT = tile(ph, "d_miT", [128, KD, 512], BF16)
                        gl = [tile(ph, f"d_gl{i}", [128, 512]) for i in range(3)]
                        tm = [tile(ph, f"d_tm{i}", [128, 512]) for i in range(2)]
                        xo = [tile(ph, f"d_xo{i}", [128, 512]) for i in range(2)]
                        v3 = lambda t: t.t[:, 0:1024].rearrange("p (a b) -> p a b", b=64)
                        b16 = lambda ap: ap.unsqueeze(2).broadcast_to([128, 16, 64])
                        for blk in range(NB):
                            t0 = blk * 512
                            for q4 in range(4):
                                tk = t0 + q4 * 128
                                f_, b_, v_ = of[q4 % 2], ob[q4 % 2], vt[q4 % 2]
                                S.dma('sp', f_.t[:], od_d[s][0][tk:tk + 128, :], [B_od[s][0]], [f_])
                                S.dma('sp', b_.t[:], od_d[s][1][tk:tk + 128, :], [B_od[s][1]], [b_])
                                S.dma('sp', v_.t[:], vtm_d[s][tk:tk + 128, :], [B_vtm[s]], [v_])
                                S.dma('sp', gdA.t[:], rwm_d[s][3328:3456, tk:tk + 128], [B_rwm[s]], [gdA])
                                S.dma('sp', gdB.t[:], rwm_d[s][3456:3488, tk:tk + 128], [B_rwm[s]], [gdB])
                                S.act(gdA.t[:], gdA.t[:], AF.Sigmoid, [gdA], [gdA])
                                S.act(gdB.t[:], gdB.t[:], AF.Sigmoid, [gdB], [gdB])
                                S.tt(f_.t[:], f_.t[:], b_.t[:], ALU.add, [f_, b_], [f_])
                                S.op('dve', lambda: nc.vector.reduce_sum(out=st16.t[:, 0, :], in_=v3(f_), axis=AX.X), [f_], [st16])
                                S.ts(st16.t[:, 0, :], st16.t[:, 0, :], 1.0 / 64, None, ALU.mult, None, [st16], [st16])
                                S.tt(o3.t[:], v3(f_), b16(st16.t[:, 0, :]), ALU.subtract, [f_, st16], [o3])
                                S.tt(sq3.t[:], o3.t[:], o3.t[:], ALU.mult, [o3], [sq3])
                                S.op('dve', lambda: nc.vector.reduce_sum(out=st16.t[:, 1, :], in_=sq3.t[:], axis=AX.X), [sq3], [st16])
                                S.rsqrt(st16.t[:, 2, :], st16.t[:, 1, :], 1.0 / 64, epsc.t[:, 1:2], [st16, epsc], [st16])
                                S.tt(o3.t[:], o3.t[:], b16(st16.t[:, 2, :]), ALU.mult, [o3, st16], [o3])
                                o3f = o3.t[:].rearrange("p a b -> p (a b)")
                                S.tt(o3f, o3f, gng.t[:], ALU.mult, [o3, gng], [o3])
                                S.tt(o3f, o3f, gnb.t[:], ALU.add, [o3, gnb], [o3])
                                S.tt(sq3.t[:], v3(v_), b16(f_.t[:, 1024:1040]), ALU.mult, [v_, f_], [sq3])
                                S.tt(o3.t[:], o3.t[:], sq3.t[:], ALU.add, [o3, sq3], [o3])
                                for hf in range(2):
                                    ps = S.rot('dps', PS[0:4])
                                    S.mm(ps.t[:, :], [(gdA.t[:], wgA.t[:, hf * 512:(hf + 1) * 512]), (gdB.t[:], wgB.t[:, hf * 512:(hf + 1) * 512])], [gdA, gdB, wgA, wgB], [ps])
                                    S.tt(ofin.t[:, hf * 512:(hf + 1) * 512], o3f[:, hf * 512:(hf + 1) * 512], ps.t[:, :], ALU.mult, [o3, ps], [ofin])
                                for cc in range(8):
                                    S.tr(PSB.t[:, cc * 128:(cc + 1) * 128], ofin.t[:, cc * 128:(cc + 1) * 128], identb.t[:], [ofin, identb], [PSB])
                                S.cp(ofT.t[:, :, q4 * 128:(q4 + 1) * 128], PSB.t[:, :].rearrange("p (a b) -> p a b", b=128), [PSB], [ofT])
                            for dc in range(KD):
                                g1, g2_ = S.rot('dgl', gl), S.rot('dgl', gl)
                                S.dma('sp', g1.t[:], gate_d[s][2048 + dc * 128:2048 + (dc + 1) * 128, t0:t0 + 512], [B_gate[s]], [g1])
                                S.dma('sp', g2_.t[:], ga_d[s][dc * 128:(dc + 1) * 128, t0:t0 + 512], [B_ga[s]], [g2_])
                                ps = S.rot('dps', PS[0:4])
                                S.mm(ps.t[:, :], [(worw.t[:, cc, dc * 128:(dc + 1) * 128], ofT.t[:, cc, :]) for cc in range(8)], [worw, ofT], [ps])
                                t_ = S.rot('dtm', tm)
                                S.tt(t_.t[:], ps.t[:, :], g1.t[:], ALU.mult, [ps, g1], [t_])
                                S.tt(miT.t[:, dc, :], t_.t[:], g2_.t[:], ALU.add, [t_, g2_], [miT])
                            for dc in range(KD):
                                xi = S.rot('dgl', gl)
                                S.dma('sp', xi.t[:], xin_d[dc * 128:(dc + 1) * 128, t0:t0 + 512], [xin_B], [xi])
                                ps = S.rot('dps', PS[0:4])
                                S.mm(ps.t[:, :], [(wout.t[:, k, dc * 128:(dc + 1) * 128], miT.t[:, k, :]) for k in range(KD)], [wout, miT], [ps])
                                xo_ = S.rot('dxo', xo)
                                S.stt(xo_.t[:], ps.t[:, :], gt1(dc), xi.t[:], ALU.mult, ALU.add, [ps, modT, xi], [xo_])
                                S.dma('sp', x1_d[s][dc * 128:(dc + 1) * 128, t0:t0 + 512], xo_.t[:], [xo_], [B_x1[s]])

                    _ck(7)
                    with ExitStack() as ph:
                        S.barrier()
                        xt = tile(ph, "f_xt", [128, KD, 512])
                        hT = tile(ph, "f_hT", [128, KD, 512], BF16)
                        xh = tile(ph, "f_xh", [128, KD, 2])
                        hTh = tile(ph, "f_hTh", [128, KD, 2], BF16)
                        sq = [tile(ph, f"f_sq{i}", [128, 512], BF16) for i in range(3)]
                        rstd = tile(ph, "f_rstd", [128, 512])
                        rstdh = tile(ph, "f_rstdh", [128, 2])
                        tmp = [tile(ph, f"f_tmp{i}", [128, 512]) for i in range(3)]
                        wsl = [tile(ph, f"f_w{i}", [128, KD, 512], BF16) for i in range(3)]
                        wds = [tile(ph, f"f_wd{i}", [128, 11, 512], BF16) for i in range(2)]
                        uT = tile(ph, "f_uT", [128, 44, 512], BF16)
                        acc = [tile(ph, f"f_acc{i}", [128, 512]) for i in range(2)]
                        hal = [tile(ph, f"f_hal{i}", [128, 2]) for i in range(2)]
                        xo = [tile(ph, f"f_xo{i}", [128, 512]) for i in range(2)]
                        last = (l == L - 1)
                        for blk in range(NB):
                            t0 = blk * 512
                            norm_block((xt, hT, sq, rstd, tmp), x1_d[s], B_x1[s], t0, 512, A2, sh2)
                            tl, tr_ = max(t0 - 1, 0), min(t0 + 512, S_ - 1)
                            S.dma('sp', xh.t[:, :, 0:1], x1_d[s][:, tl:tl + 1].rearrange("(k p) t -> p k t", p=128), [B_x1[s]], [xh], allow_slow_non_contiguous=True)
                            S.dma('sp', xh.t[:, :, 1:2], x1_d[s][:, tr_:tr_ + 1].rearrange("(k p) t -> p k t", p=128), [B_x1[s]], [xh], allow_slow_non_contiguous=True)
                            pss = PS[6]
                            for k in range(KD):
                                sqk = S.rot('sq', sq)
                                S.act(sqk.t[:, 0:2], xh.t[:, k, :], AF.Square, [xh], [sqk])
                                S.mm(pss.t[:, 0:2], [(onesb.t[:], sqk.t[:, 0:2])], [onesb, sqk], [pss], start=(k == 0), stop=(k == KD - 1))
                            S.rsqrt(rstdh.t[:], pss.t[:, 0:2], 1.0 / D, epsc.t[:, 0:1], [pss, epsc], [rstdh])
                            for k in range(KD):
                                tk = S.rot('ntmp', tmp)
                                S.tt(tk.t[:, 0:2], xh.t[:, k, :], rstdh.t[:], ALU.mult, [xh, rstdh], [tk])
                                S.act(hTh.t[:, k, :], tk.t[:, 0:2], AF.Identity, [tk, modT, der], [hTh], scale=A2(k), bias=sh2(k))
                            if blk == 0:
                                S.op('dve', lambda: nc.vector.memset(hTh.t[:, :, 0:1], 0.0), [], [hTh])
                            if blk == NB - 1:
                                S.op('dve', lambda: nc.vector.memset(hTh.t[:, :, 1:2], 0.0), [], [hTh])
                            for sj in range(11):
                                wa, wb = S.rot('fw', wsl), S.rot('fw', wsl)
                                S.dma('sp', wa.t[:], WB[("w_up", l)][0][:, sj * 512:(sj + 1) * 512].rearrange("(k p) c -> p k c", p=128), WB[("w_up", l)][1], [wa])
                                S.dma('sp', wb.t[:], WB[("w_up", l)][0][:, DFF + sj * 512:DFF + (sj + 1) * 512].rearrange("(k p) c -> p k c", p=128), WB[("w_up", l)][1], [wb])
                                for a in range(4):
                                    j = sj * 4 + a
                                    psa = S.rot('fpa', PS[0:2])
                                    psh = PS[5]
                                    psb_ = S.rot('fpb', PS[2:4])
                                    S.mm(psa.t[:, :], [(wa.t[:, k, a * 128:(a + 1) * 128], hT.t[:, k, :]) for k in range(KD)], [wa, hT], [psa])
                                    S.mm(psh.t[:, 0:2], [(wa.t[:, k, a * 128:(a + 1) * 128], hTh.t[:, k, :]) for k in range(KD)], [wa, hTh], [psh])
                                    S.mm(psb_.t[:, :], [(wb.t[:, k, a * 128:(a + 1) * 128], hT.t[:, k, :]) for k in range(KD)], [wb, hT], [psb_])
                                    ac = S.rot('facc', acc)
                                    hl = S.rot('fhal', hal)
                                    S.cp(hl.t[:], psh.t[:, 0:2], [psh], [hl], e='act')
                                    S.act(ac.t[:], psa.t[:, :], AF.Identity, [psa, cols], [ac], scale=col("cw1", j), bias=col("cb", j))
                                    S.stt(ac.t[:, 1:512], psa.t[:, 0:511], col("cw0", j), ac.t[:, 1:512], ALU.mult, ALU.add, [psa, cols, ac], [ac])
                                    S.stt(ac.t[:, 0:511], psa.t[:, 1:512], col("cw2", j), ac.t[:, 0:511], ALU.mult, ALU.add, [psa, cols, ac], [ac])
                                    S.stt(ac.t[:, 0:1], hl.t[:, 0:1], col("cw0", j), ac.t[:, 0:1], ALU.mult, ALU.add, [hl, cols, ac], [ac])
                                    S.stt(ac.t[:, 511:512], hl.t[:, 1:2], col("cw2", j), ac.t[:, 511:512], ALU.mult, ALU.add, [hl, cols, ac], [ac])
                                    S.act(ac.t[:], ac.t[:], AF.Silu, [ac], [ac])
                                    S.tt(uT.t[:, j, :], ac.t[:], psb_.t[:, :], ALU.mult, [ac, psb_], [uT])
                            for dg in range(4):
                                pso = PS[0:4]
                                for rs in range(4):
                                    wd_ = S.rot('fwd', wds)
                                    S.dma('sp', wd_.t[:], WB[("w_down", l)][0][rs * 1408:(rs + 1) * 1408, dg * 512:(dg + 1) * 512].rearrange("(a p) c -> p a c", p=128), WB[("w_down", l)][1], [wd_])
                                    for a in range(11):
                                        for d4 in range(4):
                                            S.mm(pso[d4].t[:, :], [(wd_.t[:, a, d4 * 128:(d4 + 1) * 128], uT.t[:, rs * 11 + a, :])], [wd_, uT], [pso[d4]],
                                                 start=(rs == 0 and a == 0), stop=(rs == 3 and a == 10))
                                for d4 in range(4):
                                    dc = dg * 4 + d4
                                    xo_ = S.rot('fxo', xo)
                                    S.stt(xo_.t[:], pso[d4].t[:, :], gt2(dc), xt.t[:, dc, :], ALU.mult, ALU.add, [pso[d4], modT, xt], [xo_])
                                    S.dma('sp', x2_d[s][dc * 128:(dc + 1) * 128, t0:t0 + 512], xo_.t[:], [xo_], [B_x2[s]])
                        if last:
                            for blk in range(NB):
                                t0 = blk * 512
                                S.dma('sp', xt.t[:], x2_d[s][:, t0:t0 + 512].rearrange("(k p) t -> p k t", p=128), [B_x2[s]], [xt])
                                pss = S.rot('nps', [PS[5], PS[6]])
                                for k in range(KD):
                                    sqk = S.rot('sq', sq)
                                    S.act(sqk.t[:], xt.t[:, k, :], AF.Square, [xt], [sqk])
                                    S.mm(pss.t[:, :], [(onesb.t[:], sqk.t[:])], [onesb, sqk], [pss], start=(k == 0), stop=(k == KD - 1))
                                S.rsqrt(rstd.t[:], pss.t[:, :], 1.0 / D, epsc.t[:, 0:1], [pss, epsc], [rstd])
                                for k in range(KD):
                                    xo_ = S.rot('fxo', xo)
                                    S.stt(xo_.t[:], xt.t[:, k, :], col("final_norm_g", k), rstd.t[:], ALU.mult, ALU.mult, [xt, cols, rstd], [xo_])
                                    S.dma('sp', yT[s, k * 128:(k + 1) * 128, t0:t0 + 512], xo_.t[:], [xo_], [B_y])
                    cur_d[s], cur_B[s] = x2_d[s], B_x2[s]
          except _Stop:
            break
        S.finish()
    return nc


def _fm(v):
    v = np.asarray(v, np.float32).reshape(-1)
    n = (v.size + 127) // 128
    p = np.zeros(n * 128, np.float32)
    p[:v.size] = v
    return np.ascontiguousarray(p.reshape(n, 128).T)


def _host_consts(S_):
    cst = np.zeros((128, NCST), np.float32)
    cst[:, 0:128] = np.eye(128)
    bd = np.zeros((128, 128), np.float32)
    bd[:64, :64] = 1
    bd[64:, 64:] = 1
    cst[:, 128:256] = bd
    p = np.arange(128)[:, None]
    f = np.arange(128)[None, :]
    cst[:, 256:384] = (p < f)
    cst[:, 384:512] = (p <= f)
    cst[:, 512:640] = (p > f)
    cst[:, 640:768] = (p >= f)
    cst[:64, 768] = 1
    cst[64:, 769] = 1
    sw = np.zeros((64, 64), np.float32)
    for m in range(32):
        sw[m + 32, m] = -1.0
        sw[m, m + 32] = 1.0
    cst[:64, 770:834] = sw
    inv = (1.0 / (np.float32(10000.0) ** (np.arange(0, 64, 2, dtype=np.float32) / np.float32(64)))).astype(np.float32)
    ang = np.arange(S_, dtype=np.float32)[:, None] * inv[None, :]
    cos = np.cos(ang).astype(np.float32).T
    sin = np.sin(ang).astype(np.float32).T
    return cst, np.ascontiguousarray(np.concatenate([cos, cos], 0)), np.ascontiguousarray(np.concatenate([sin, sin], 0))


def _pack_cols(I, L):
    out = np.zeros((L, 128, NCOL), np.float32)
    for l in range(L):
        def put(name, v):
            a = _fm(v)
            out[l, :, CO[name]:CO[name] + a.shape[1]] = a
        put("norm_mix_g", I["norm_mix_g"][l])
        put("norm_ffn_g", I["norm_ffn_g"][l])
        put("final_norm_g", I["final_norm_g"])
        put("ada_b", I["ada_b"][l])
        put("q_norm_g", I["q_norm_g"][l])
        put("kv_norm_g", I["kv_norm_g"][l])
        put("mu0", I["rwkv_mu"][l, 0])
        put("mu1", I["rwkv_mu"][l, 1])
        for d in range(2):
            put(f"w0_{d}", I["rwkv_w0"][l, d])
            put(f"a0_{d}", I["rwkv_a0"][l, d])
            put(f"r_k_{d}", I["rwkv_r_k"][l, d])
        put("k_k", I["rwkv_k_k"][l])
        put("k_a", I["rwkv_k_a"][l])
        for i in range(3):
            put(f"cw{i}", I["conv_w"][l, i])
        put("cb", I["conv_b"][l])
    return out


_NC_CACHE = {}


def run_groups(I, xs, cs, S_, NSEQ, L, n_cores):
    key = (S_, NSEQ, L)
    if key not in _NC_CACHE:
        _NC_CACHE[key] = build(S_, NSEQ, L)
    nc = _NC_CACHE[key]
    cst, rc, rs = _host_consts(S_)
    f32 = lambda a: np.ascontiguousarray(np.asarray(a, np.float32))
    shared = {
        "ada_w": f32(I["ada_w"]), "w_in": f32(I["w_in"]), "w_uq": f32(I["w_uq"]), "w_ukv": f32(I["w_ukv"]),
        "w_o_att": f32(I["w_o_att"]), "wdu": f32(I["rwkv_w_decay_up"]).reshape(L, 128, 1024),
        "wiu": f32(I["rwkv_w_iclr_up"]).reshape(L, 128, 1024), "wgu": f32(I["rwkv_w_gate_up"]),
        "w_o_rwkv": f32(I["w_o_rwkv"]), "w_out": f32(I["w_out"]), "w_up": f32(I["w_ffn_up"]), "w_down": f32(I["w_ffn_down"]),
        "gn_g": f32(I["rwkv_gn_g"]).reshape(L, 1, 1024), "gn_b": f32(I["rwkv_gn_b"]).reshape(L, 1, 1024),
        "cols": _pack_cols(I, L), "cst": cst, "ropec": rc, "ropes": rs,
    }
    in_maps = []
    for x, c in zip(xs, cs):
        m = dict(shared)
        m["xT"] = np.ascontiguousarray(np.transpose(x, (0, 2, 1)))
        m["c_fm"] = np.ascontiguousarray(np.transpose(np.asarray(c, np.float32).reshape(NSEQ, KD, 128), (2, 1, 0)))
        in_maps.append(m)
    res = run_bass_kernel_spmd(nc, in_maps, core_ids=list(range(n_cores)))
    return [np.ascontiguousarray(np.transpose(r["yT"], (0, 2, 1))) for r in res.results]


def kernel(**I):
    xp, xs_ = np.asarray(I["x_prompt"], np.float32), np.asarray(I["x_sample"], np.float32)
    cp, cs_ = np.asarray(I["c_prompt"], np.float32), np.asarray(I["c_sample"], np.float32)
    S_ = xp.shape[1]
    L = I["ada_w"].shape[0]
    allx = np.concatenate([xp, xs_], 0)
    allc = np.concatenate([cp, cs_], 0)
    n = allx.shape[0]
    NSEQ = 2
    assign = [[0, 1], [2, 3], [4, 5], [6, 7], [8, 8], [9, 9], [10, 10], [11, 11]]
    xs = [allx[a] for a in assign]
    cs = [allc[a] for a in assign]
    outs = run_groups(I, xs, cs, S_, NSEQ, L, 8)
    y = np.zeros_like(allx)
    for a, o in zip(assign, outs):
        for j, si in enumerate(a):
            y[si] = o[j]
    return (y[:xp.shape[0]], y[xp.shape[0]:])
```

```python
import numpy as np
from contextlib import ExitStack
import concourse.bass as bass
import concourse.mybir as mybir
from concourse.bass_utils import run_bass_kernel_spmd

F32 = mybir.dt.float32
BF16 = mybir.dt.bfloat16
AF = mybir.ActivationFunctionType
ALU = mybir.AluOpType
AX = mybir.AxisListType

D = 2048
KD = 16
DFF = 5632
NH = 8
DIN = 8416
C = 128


class Buf:
    __slots__ = ("w", "r")

    def __init__(self):
        self.w = None
        self.r = {}


class TT:
    __slots__ = ("t", "b")

    def __init__(self, t):
        self.t = t
        self.b = Buf()


class Sched:
    def __init__(self, nc, es, ndma=12):
        self.nc = nc
        self.eng = {'pe': nc.tensor, 'act': nc.scalar, 'dve': nc.vector, 'pool': nc.gpsimd, 'sp': nc.sync}
        self.sem = {}
        self.cnt = {}
        self.known = {k: {} for k in self.eng}
        for k in ['pe', 'act', 'dve', 'pool']:
            self.sem[k] = es.enter_context(nc.semaphore("s_" + k))
            self.cnt[k] = 0
        self.dpool, self.dcnt, self.drr = {}, {}, {}
        for q in ['sp', 'pool']:
            self.dpool[q] = [es.enter_context(nc.semaphore(f"d_{q}_{i}")) for i in range(ndma)]
            self.dcnt[q] = [0] * ndma
            self.drr[q] = 0
        self.rots = {}
        self.stopped = False
        _SREF[0] = self

    def rot(self, name, items):
        i = self.rots.get(name, 0)
        self.rots[name] = i + 1
        return items[i % len(items)]

    def _wait(self, e, deps):
        eng = self.eng[e]
        kn = self.known[e]
        for (sk, val) in deps:
            if kn.get(sk, 0) >= val:
                continue
            if sk[0] == 'c':
                if sk[1] == e and e == 'pe':
                    continue
                s = self.sem[sk[1]]
            else:
                s = self.dpool[sk[1]][sk[2]]
            eng.wait_ge(s, val)
            kn[sk] = val

    @staticmethod
    def _deps(reads, writes):
        deps = []
        for b in reads:
            if b.w is not None:
                deps.append(b.w)
        for b in writes:
            if b.w is not None:
                deps.append(b.w)
            deps.extend(b.r.items())
        return deps

    @staticmethod
    def _commit(ev, reads, writes):
        for b in reads:
            if b.r.get(ev[0], 0) < ev[1]:
                b.r[ev[0]] = ev[1]
        for b in writes:
            b.w = ev
            b.r = {}

    def op(self, e, fn, reads=(), writes=()):
        if self.stopped:
            return
        reads = [x.b if isinstance(x, TT) else x for x in reads]
        writes = [x.b if isinstance(x, TT) else x for x in writes]
        self._wait(e, self._deps(reads, writes))
        ins = fn()
        self.cnt[e] += 1
        ins.then_inc(self.sem[e], 1)
        ev = (('c', e), self.cnt[e])
        self._commit(ev, reads, writes)

    def dma(self, q, out, in_, reads=(), writes=(), **kw):
        if self.stopped:
            return
        reads = [x.b if isinstance(x, TT) else x for x in reads]
        writes = [x.b if isinstance(x, TT) else x for x in writes]
        i = self.drr[q]
        self.drr[q] = (i + 1) % len(self.dpool[q])
        sk = ('d', q, i)
        deps = self._deps(reads, writes)
        if self.dcnt[q][i] > 0:
            deps.append((sk, self.dcnt[q][i]))
        self._wait(q, deps)
        ins = self.eng[q].dma_start(out=out, in_=in_, **kw)
        self.dcnt[q][i] += 16
        ins.then_inc(self.dpool[q][i], 16)
        self._commit((sk, self.dcnt[q][i]), reads, writes)

    def barrier(self):
        if self.stopped:
            return
        evs = [(('c', e), self.cnt[e]) for e in self.cnt if self.cnt[e] > 0]
        for q in self.dpool:
            if q == 'pool':
                continue
            for i in range(len(self.dpool[q])):
                if self.dcnt[q][i] > 0:
                    evs.append((('d', q, i), self.dcnt[q][i]))
        for e in self.eng:
            self._wait(e, evs)

    def finish(self):
        for q in self.dpool:
            for i, s in enumerate(self.dpool[q]):
                if self.dcnt[q][i] > 0:
                    self.nc.sync.wait_ge(s, self.dcnt[q][i])

    def mm(self, out, pairs, reads, writes, start=True, stop=True):
        nc = self.nc

        def fn():
            ins = None
            n = len(pairs)
            for i, (l, r) in enumerate(pairs):
                ins = nc.tensor.matmul(out, lhsT=l, rhs=r, start=(start and i == 0), stop=(stop and i == n - 1))
            return ins
        self.op('pe', fn, reads, writes)

    def tr(self, out, in_, ident, reads, writes):
        nc = self.nc
        self.op('pe', lambda: nc.tensor.transpose(out, in_, ident), reads, writes)

    def act(self, out, in_, func, reads, writes, **kw):
        nc = self.nc
        self.op('act', lambda: nc.scalar.activation(out=out, in_=in_, func=func, **kw), reads, writes)

    def tt(self, out, in0, in1, op, reads, writes, e='dve'):
        eng = self.eng[e]
        self.op(e, lambda: eng.tensor_tensor(out=out, in0=in0, in1=in1, op=op), reads, writes)

    def ts(self, out, in0, s1, s2, op0, op1, reads, writes):
        nc = self.nc
        if s2 is None:
            self.op('dve', lambda: nc.vector.tensor_scalar(out=out, in0=in0, scalar1=s1, scalar2=None, op0=op0), reads, writes)
        else:
            self.op('dve', lambda: nc.vector.tensor_scalar(out=out, in0=in0, scalar1=s1, scalar2=s2, op0=op0, op1=op1), reads, writes)

    def stt(self, out, in0, scalar, in1, op0, op1, reads, writes):
        nc = self.nc
        self.op('dve', lambda: nc.vector.scalar_tensor_tensor(out=out, in0=in0, scalar=scalar, in1=in1, op0=op0, op1=op1), reads, writes)

    def cp(self, out, in_, reads, writes, e=None):
        nc = self.nc
        if e is None:
            e = self.rot('cp', ['act', 'dve'])
        if e == 'act':
            self.op('act', lambda: nc.scalar.copy(out=out, in_=in_), reads, writes)
        else:
            self.op('dve', lambda: nc.vector.tensor_copy(out=out, in_=in_), reads, writes)

    def rsqrt(self, out, in_, scale, eps_ap, reads, writes):
        self.act(out, in_, AF.Sqrt, reads, writes, scale=scale, bias=eps_ap)
        nc = self.nc
        self.op('dve', lambda: nc.vector.reciprocal(out=out, in_=out), writes, writes)


def _col_layout():
    items = [("norm_mix_g", 16), ("norm_ffn_g", 16), ("final_norm_g", 16), ("ada_b", 96), ("q_norm_g", 4),
             ("kv_norm_g", 2), ("mu0", 28), ("mu1", 28), ("w0_0", 8), ("w0_1", 8), ("a0_0", 8), ("a0_1", 8),
             ("k_k", 8), ("k_a", 8), ("r_k_0", 8), ("r_k_1", 8), ("cw0", 44), ("cw1", 44), ("cw2", 44), ("cb", 44)]
    co, off = {}, 0
    for n, w in items:
        co[n] = off
        off += w
    return co, off


CO, NCOL = _col_layout()
CST = {"ident": 0, "bd": 128, "lt": 256, "le": 384, "gt": 512, "ge": 640, "hsel": 768, "swap": 770}
NCST = 770 + 64


def _in_chunks():
    ch = []
    for j in range(4):
        ch.append(("q", j * 128, 128, j))
    for j in range(2):
        ch.append(("kv", 512 + j * 128, 128, j))
    ch.append(("rope", 768, 64, 0))
    for j in range(27):
        ch.append(("rw", 832 + j * 128, 128, j))
    ch.append(("rw", 832 + 27 * 128, 32, 27))
    for j in range(32):
        ch.append(("gate", 4320 + j * 128, 128, j))
    slabs, cur = [], []
    for c in ch:
        if cur and ((c[1] + c[2] - cur[0][1] > 512) or (c[0] != cur[-1][0] and c[0] in ("rw", "gate"))):
            slabs.append(cur)
            cur = []
        cur.append(c)
    slabs.append(cur)
    return slabs


class _Stop(Exception):
    pass


_KSTOP = [99]


_SREF = [None]


def _ck(n):
    if _KSTOP[0] <= n:
        _SREF[0].stopped = True


def build(S_, NSEQ, L):
    NB = S_ // 512
    NT = S_ // 128
    NCH = S_ // C
    nc = bass.Bass("TRN2", target_bir_lowering=False)

    def din(name, shape, dt=F32):
        return nc.dram_tensor(name, shape, dt, kind="ExternalInput").ap()

    def dscr(name, shape, dt=F32):
        return nc.dram_tensor(name, shape, dt, kind="Internal").ap()

    xT = din("xT", [NSEQ, D, S_])
    c_fm = din("c_fm", [128, KD, NSEQ])
    ada_w = din("ada_w", [L, D, 6 * D])
    w_in = din("w_in", [L, D, DIN])
    w_uq = din("w_uq", [L, 512, 1536])
    w_ukv = din("w_ukv", [L, 256, 2048])
    w_o_att = din("w_o_att", [L, 1024, D])
    wdu = din("wdu", [L, 128, 1024])
    wiu = din("wiu", [L, 128, 1024])
    wgu = din("wgu", [L, 160, 1024])
    w_o_rwkv = din("w_o_rwkv", [L, 1024, D])
    w_out = din("w_out", [L, D, D])
    w_up = din("w_up", [L, D, 2 * DFF])
    w_down = din("w_down", [L, DFF, D])
    gn_g = din("gn_g", [L, 1, 1024])
    gn_b = din("gn_b", [L, 1, 1024])
    cols_d = din("cols", [L, 128, NCOL])
    cst_d = din("cst", [128, NCST])
    ropec = din("ropec", [64, S_])
    ropes = din("ropes", [64, S_])
    yT = nc.dram_tensor("yT", [NSEQ, D, S_], F32, kind="ExternalOutput").ap()

    rw_d = [dscr(f"rw{s}", [3584, S_]) for s in range(NSEQ)]
    rwm_d = [dscr(f"rwm{s}", [3584, S_]) for s in range(NSEQ)]
    gate_d = [dscr(f"gate{s}", [4096, S_]) for s in range(NSEQ)]
    ga_d = [dscr(f"ga{s}", [D, S_]) for s in range(NSEQ)]
    od_d = [[dscr(f"od{s}_{d}", [S_, 1040]) for d in range(2)] for s in range(NSEQ)]
    vtm_d = [dscr(f"vtm{s}", [S_, 1024]) for s in range(NSEQ)]
    x1_d = [dscr(f"x1_{s}", [D, S_]) for s in range(NSEQ)]
    x2_d = [dscr(f"x2_{s}", [D, S_]) for s in range(NSEQ)]
    B_rw = [Buf() for _ in range(NSEQ)]
    B_rwm = [Buf() for _ in range(NSEQ)]
    B_gate = [Buf() for _ in range(NSEQ)]
    B_ga = [Buf() for _ in range(NSEQ)]
    B_od = [[Buf() for _ in range(2)] for _ in range(NSEQ)]
    B_vtm = [Buf() for _ in range(NSEQ)]
    B_x1 = [Buf() for _ in range(NSEQ)]
    B_x2 = [Buf() for _ in range(NSEQ)]
    B_y = Buf()
    B_in = Buf()

    with ExitStack() as es:
        S = Sched(nc, es)

        _tc = [0]

        def tile(st, name, shape, dt=F32):
            _tc[0] += 1
            return TT(st.enter_context(nc.sbuf_tensor(f"{name}_{_tc[0]}", shape, dt)))

        PS = [TT(es.enter_context(nc.psum_tensor(f"ps{i}", [128, 512], F32))) for i in range(7)]
        PSB = TT(es.enter_context(nc.psum_tensor("psb", [128, 1024], BF16)))

        cst = tile(es, "cst", [128, NCST])
        S.dma('sp', cst.t[:], cst_d, [B_in], [cst])
        ident = cst.t[:, 0:128]
        bd_ones = cst.t[:, 128:256]
        hsel = cst.t[:, 768:770]
        swapm = cst.t[0:64, 770:834]
        identb = tile(es, "identb", [128, 128], BF16)
        S.cp(identb.t[:], ident, [cst], [identb], e='dve')
        onesb = tile(es, "onesb", [128, 128], BF16)
        S.op('dve', lambda: nc.vector.memset(onesb.t[:], 1.0), [], [onesb])
        onesf = tile(es, "onesf", [128, 128])
        S.op('dve', lambda: nc.vector.memset(onesf.t[:], 1.0), [], [onesf])
        epsc = tile(es, "epsc", [128, 2])
        S.op('dve', lambda: nc.vector.memset(epsc.t[:, 0:1], 1e-6), [], [epsc])
        S.op('dve', lambda: nc.vector.memset(epsc.t[:, 1:2], 64e-5), [], [epsc])
        zero_c = tile(es, "zero_c", [128, 1])
        S.op('dve', lambda: nc.vector.memset(zero_c.t[:], 0.0), [], [zero_c])
        mA = [tile(es, f"mA{d}", [128, 256]) for d in range(2)]
        mN = [tile(es, f"mN{d}", [128, 512]) for d in range(2)]
        for d in range(2):
            s_, i_, n_ = (("lt", "le", "gt") if d == 0 else ("gt", "ge", "lt"))
            S.cp(mA[d].t[:, 0:128], cst.t[:, CST[s_]:CST[s_] + 128], [cst], [mA[d]], e='dve')
            S.cp(mA[d].t[:, 128:256], cst.t[:, CST[i_]:CST[i_] + 128], [cst], [mA[d]], e='dve')
            for q in range(4):
                S.cp(mN[d].t[:, q * 128:(q + 1) * 128], cst.t[:, CST[n_]:CST[n_] + 128], [cst], [mN[d]], e='dve')

        WB = {}
        for l in range(L):
            slA = _in_chunks()
            t = dscr(f"w_in_sl{l}", [len(slA), 128, KD, 512], BF16)
            Bs = []
            for si, slab in enumerate(slA):
                c0, c1 = slab[0][1], slab[-1][1] + slab[-1][2]
                B = Buf()
                S.dma('pool', t[si, :, :, 0:c1 - c0], w_in[l, :, c0:c1].rearrange("(k p) c -> p k c", p=128), [B_in], [B])
                Bs.append(B)
            WB[("w_in", l)] = (t, Bs)
            t = dscr(f"w_up_sl{l}", [22, 128, KD, 512], BF16)
            Bs = []
            for sj in range(22):
                c0 = sj * 512 if sj < 11 else DFF + (sj - 11) * 512
                B = Buf()
                S.dma('pool', t[sj], w_up[l, :, c0:c0 + 512].rearrange("(k p) c -> p k c", p=128), [B_in], [B])
                Bs.append(B)
            WB[("w_up", l)] = (t, Bs)
            t = dscr(f"w_down_sl{l}", [16, 128, 11, 512], BF16)
            Bs = []
            for dg in range(4):
                for rs in range(4):
                    B = Buf()
                    S.dma('pool', t[dg * 4 + rs], w_down[l, rs * 1408:(rs + 1) * 1408, dg * 512:(dg + 1) * 512].rearrange("(a p) c -> p a c", p=128), [B_in], [B])
                    Bs.append(B)
            WB[("w_down", l)] = (t, Bs)
            for (nm, src, shape) in (("w_uq", w_uq, [512, 1536]), ("w_ukv", w_ukv, [256, 2048]),
                                     ("w_o_att", w_o_att, [1024, D]), ("w_o_rwkv", w_o_rwkv, [1024, D]), ("w_out", w_out, [D, D])):
                t = dscr(f"{nm}_bf{l}", shape, BF16)
                Bs = []
                for r0 in range(0, shape[0], 256):
                    B = Buf()
                    S.dma('pool', t[r0:r0 + 256, :], src[l, r0:r0 + 256, :], [B_in], [B])
                    Bs.append(B)
                WB[(nm, l)] = (t, Bs)

        cur_d, cur_B = [xT[s] for s in range(NSEQ)], [B_in for _ in range(NSEQ)]

        for l in range(L):
          try:
            with ExitStack() as ls:
                S.barrier()
                cols = tile(ls, "cols", [128, NCOL])
                S.dma('sp', cols.t[:], cols_d[l], [B_in], [cols])

                def col(name, k=0, n=1):
                    return cols.t[:, CO[name] + k:CO[name] + k + n]
                modT = tile(ls, "modT", [128, NSEQ, 96])
                der = tile(ls, "der", [128, NSEQ, 2, KD])
                omka = tile(ls, "omka", [128, 8])
                S.ts(omka.t[:], col("k_a", 0, 8), -1.0, 1.0, ALU.mult, ALU.add, [cols], [omka])

                with ExitStack() as ph:
                    S.barrier()
                    csil = tile(ph, "csil", [128, KD, NSEQ])
                    S.dma('sp', csil.t[:], c_fm, [B_in], [csil])
                    S.act(csil.t[:], csil.t[:], AF.Silu, [csil], [csil])
                    slabs = [tile(ph, f"mslab{i}", [128, KD, 512]) for i in range(2)]
                    psm = PS[0]
                    for sb in range(24):
                        sl = slabs[sb % 2]
                        S.dma('sp', sl.t[:], ada_w[l, :, sb * 512:(sb + 1) * 512].rearrange("(k p) c -> p k c", p=128), [B_in], [sl])
                        for jj in range(4):
                            j = sb * 4 + jj
                            S.mm(psm.t[:, j * NSEQ:(j + 1) * NSEQ],
                                 [(sl.t[:, k, jj * 128:(jj + 1) * 128], csil.t[:, k, :]) for k in range(KD)],
                                 [sl, csil], [psm])
                    pv = psm.t[:, 0:96 * NSEQ].rearrange("p (j s) -> p j s", s=NSEQ)
                    for s in range(NSEQ):
                        S.tt(modT.t[:, s, :], pv[:, :, s], col("ada_b", 0, 96), ALU.add, [psm, cols], [modT])
                        S.stt(der.t[:, s, 0, :], modT.t[:, s, 16:32], 1.0, col("norm_mix_g", 0, 16), ALU.add, ALU.mult, [modT, cols], [der])
                        S.stt(der.t[:, s, 1, :], modT.t[:, s, 64:80], 1.0, col("norm_ffn_g", 0, 16), ALU.add, ALU.mult, [modT, cols], [der])

                _ck(1)
                for s in range(NSEQ):
                    sh1 = lambda k: modT.t[:, s, 0 + k:1 + k]
                    gt1 = lambda k: modT.t[:, s, 32 + k:33 + k]
                    sh2 = lambda k: modT.t[:, s, 48 + k:49 + k]
                    gt2 = lambda k: modT.t[:, s, 80 + k:81 + k]
                    A1 = lambda k: der.t[:, s, 0, k:k + 1]
                    A2 = lambda k: der.t[:, s, 1, k:k + 1]
                    xin_d, xin_B = cur_d[s], cur_B[s]

                    def norm_block(ph_tiles, src_d, src_B, t0, n, Acol, shcol):
                        xt, hT, sq, rstd, tmp = ph_tiles
                        S.dma('sp', xt.t[:, :, 0:n], src_d[:, t0:t0 + n].rearrange("(k p) t -> p k t", p=128), [src_B], [xt])
                        pss = S.rot('nps', [PS[5], PS[6]])
                        for k in range(KD):
                            sqk = S.rot('sq', sq)
                            S.act(sqk.t[:, 0:n], xt.t[:, k, 0:n], AF.Square, [xt], [sqk])
                            S.mm(pss.t[:, 0:n], [(onesb.t[:], sqk.t[:, 0:n])], [onesb, sqk], [pss], start=(k == 0), stop=(k == KD - 1))
                        S.rsqrt(rstd.t[:, 0:n], pss.t[:, 0:n], 1.0 / D, epsc.t[:, 0:1], [pss, epsc], [rstd])
                        for k in range(KD):
                            tk = S.rot('ntmp', tmp)
                            S.tt(tk.t[:, 0:n], xt.t[:, k, 0:n], rstd.t[:, 0:n], ALU.mult, [xt, rstd], [tk])
                            S.act(hT.t[:, k, 0:n], tk.t[:, 0:n], AF.Identity, [tk, modT, der], [hT], scale=Acol(k), bias=shcol(k))

                    with ExitStack() as mla:
                        S.barrier()
                        qdnT = tile(mla, "qdnT", [128, 4, S_], BF16)
                        kvnT = tile(mla, "kvnT", [128, 2, S_], BF16)
                        krT = tile(mla, "krT", [128, S_], BF16)
                        S.op('dve', lambda: nc.vector.memset(krT.t[64:128, :], 0.0), [], [krT])
                        with ExitStack() as ph:
                            S.barrier()
                            xt = tile(ph, "a_xt", [128, KD, 512])
                            hT = tile(ph, "a_hT", [128, KD, 512], BF16)
                            sq = [tile(ph, f"a_sq{i}", [128, 512], BF16) for i in range(2)]
                            rstd = tile(ph, "a_rstd", [128, 512])
                            tmp = [tile(ph, f"a_tmp{i}", [128, 512]) for i in range(2)]
                            wsl = [tile(ph, f"a_w{i}", [128, KD, 512], BF16) for i in range(2)]
                            stg = [tile(ph, f"a_stg{i}", [128, 4, 512]) for i in range(2)]
                            qd = tile(ph, "a_qd", [128, 6, 512])
                            sqf = [tile(ph, f"a_sqf{i}", [128, 512]) for i in range(2)]
                            kr = tile(ph, "a_kr", [64, 512])
                            krs = tile(ph, "a_krs", [64, 512])
                            cs = tile(ph, "a_cs", [64, 2, 512])
                            rq = tile(ph, "a_rq", [128, 2, 512])
                            slabsA = _in_chunks()
                            for blk in range(NB):
                                t0 = blk * 512
                                norm_block((xt, hT, sq, rstd, tmp), xin_d, xin_B, t0, 512, A1, sh1)
                                S.dma('sp', cs.t[:, 0, :], ropec[:, t0:t0 + 512], [B_in], [cs])
                                S.dma('sp', cs.t[:, 1, :], ropes[:, t0:t0 + 512], [B_in], [cs])
                                for si, slab in enumerate(slabsA):
                                    c0 = slab[0][1]
                                    c1 = slab[-1][1] + slab[-1][2]
                                    w = wsl[si % 2]
                                    S.dma('sp', w.t[:, :, 0:c1 - c0], WB[("w_in", l)][0][si, :, :, 0:c1 - c0], [WB[("w_in", l)][1][si]], [w])
                                    st = None
                                    nst = 0
                                    for (kind, cc, cw, idx) in slab:
                                        ps = S.rot('aps', PS[0:4])
                                        S.mm(ps.t[0:cw, :], [(w.t[:, k, cc - c0:cc - c0 + cw], hT.t[:, k, :]) for k in range(KD)], [w, hT], [ps])
                                        if kind == "q":
                                            S.cp(qd.t[:, idx, :], ps.t[:, :], [ps], [qd])
                                        elif kind == "kv":
                                            S.cp(qd.t[:, 4 + idx, :], ps.t[:, :], [ps], [qd])
                                        elif kind == "rope":
                                            S.cp(kr.t[:, :], ps.t[0:64, :], [ps], [kr], e='act')
                                            ps2 = S.rot('aps', PS[0:4])
                                            S.mm(ps2.t[0:64, :], [(swapm, kr.t[:, :])], [cst, kr], [ps2])
                                            S.tt(krs.t[:], ps2.t[0:64, :], cs.t[:, 1, :], ALU.mult, [ps2, cs], [krs])
                                            S.tt(kr.t[:], kr.t[:], cs.t[:, 0, :], ALU.mult, [kr, cs], [kr])
                                            S.tt(krT.t[0:64, t0:t0 + 512], kr.t[:], krs.t[:], ALU.add, [kr, krs], [krT])
                                        else:
                                            if st is None:
                                                st = S.rot('astg', stg)
                                                nst = 0
                                                st_first = (kind, idx)
                                            if kind == "gate":
                                                S.act(st.t[0:cw, nst, :], ps.t[0:cw, :], AF.Sigmoid, [ps], [st])
                                            else:
                                                S.cp(st.t[0:cw, nst, :], ps.t[0:cw, :], [ps], [st])
                                            nst += 1
                                    if st is not None:
                                        kind0, i0 = st_first
                                        dd, dB = (rw_d[s], B_rw[s]) if kind0 == "rw" else (gate_d[s], B_gate[s])
                                        nfull = nst - 1 if slab[-1][2] == 32 else nst
                                        if nfull > 0:
                                            S.dma('sp', dd[i0 * 128:(i0 + nfull) * 128, t0:t0 + 512].rearrange("(a p) t -> p a t", p=128),
                                                  st.t[:, 0:nfull, :], [st], [dB])
                                        if nfull < nst:
                                            S.dma('sp', dd[(i0 + nfull) * 128:(i0 + nfull) * 128 + 32, t0:t0 + 512], st.t[0:32, nfull, :], [st], [dB])
                                for (j0, nj, gname, dst, ps) in ((0, 4, "q_norm_g", qdnT, PS[4]), (4, 2, "kv_norm_g", kvnT, PS[4])):
                                    for j in range(nj):
                                        sqk = S.rot('sqf', sqf)
                                        S.act(sqk.t[:], qd.t[:, j0 + j, :], AF.Square, [qd], [sqk])
                                        S.mm(ps.t[:, :], [(onesf.t[:], sqk.t[:])], [onesf, sqk], [ps], start=(j == 0), stop=(j == nj - 1))
                                    rr = rq.t[:, 0, :]
                                    S.rsqrt(rr, ps.t[:, :], 1.0 / (128 * nj), epsc.t[:, 0:1], [ps, epsc], [rq])
                                    for j in range(nj):
                                        S.stt(dst.t[:, j, t0:t0 + 512], qd.t[:, j0 + j, :], col(gname, j), rr, ALU.mult, ALU.mult, [qd, cols, rq], [dst])

                        _ck(2)
                        with ExitStack() as ph:
                            S.barrier()
                            oT = tile(ph, "b_oT", [128, NH, S_], BF16)
                            with ExitStack() as ph2:
                                S.barrier()
                                wuq = tile(ph2, "b_wuq", [128, 4, 1536], BF16)
                                wukv = tile(ph2, "b_wukv", [128, 2, 2048], BF16)
                                S.dma('sp', wuq.t[:], WB[("w_uq", l)][0].rearrange("(k p) c -> p k c", p=128), WB[("w_uq", l)][1], [wuq])
                                S.dma('sp', wukv.t[:], WB[("w_ukv", l)][0].rearrange("(k p) c -> p k c", p=128), WB[("w_ukv", l)][1], [wukv])
                                knT = tile(ph2, "b_knT", [128, S_], BF16)
                                Vt = tile(ph2, "b_Vt", [128, NT, 128], BF16)
                                qnT = tile(ph2, "b_qnT", [128, S_], BF16)
                                qrT = tile(ph2, "b_qrT", [128, S_], BF16)
                                S.op('dve', lambda: nc.vector.memset(qrT.t[64:128, :], 0.0), [], [qrT])
                                qr = tile(ph2, "b_qr", [64, 512])
                                qrs = tile(ph2, "b_qrs", [64, 512])
                                cs = tile(ph2, "b_cs", [64, 2, 512])
                                pT = [tile(ph2, f"b_pT{i}", [128, 512], BF16) for i in range(3)]
                                rec = tile(ph2, "b_rec", [128, 512])
                                scale = 192.0 ** -0.5
                                for h in range(NH):
                                    for blk in range(NB):
                                        t0 = blk * 512
                                        ps = S.rot('bps', PS[0:3])
                                        S.mm(ps.t[:, :], [(wukv.t[:, r, h * 256:h * 256 + 128], kvnT.t[:, r, t0:t0 + 512]) for r in range(2)], [wukv, kvnT], [ps])
                                        S.cp(knT.t[:, t0:t0 + 512], ps.t[:, :], [ps], [knT])
                                        ps = S.rot('bps', PS[0:3])
                                        for q in range(4):
                                            tt_ = blk * 4 + q
                                            S.mm(ps.t[:, q * 128:(q + 1) * 128], [(kvnT.t[:, r, tt_ * 128:(tt_ + 1) * 128], wukv.t[:, r, h * 256 + 128:h * 256 + 256]) for r in range(2)], [wukv, kvnT], [ps])
                                        S.cp(Vt.t[:, blk * 4:blk * 4 + 4, :], ps.t[:, :].rearrange("p (a b) -> p a b", b=128), [ps], [Vt])
                                        ps = S.rot('bps', PS[0:3])
                                        S.mm(ps.t[:, :], [(wuq.t[:, r, h * 192:h * 192 + 128], qdnT.t[:, r, t0:t0 + 512]) for r in range(4)], [wuq, qdnT], [ps])
                                        S.cp(qnT.t[:, t0:t0 + 512], ps.t[:, :], [ps], [qnT])
                                        ps = S.rot('bps', PS[0:3])
                                        S.mm(ps.t[0:64, :], [(wuq.t[:, r, h * 192 + 128:h * 192 + 192], qdnT.t[:, r, t0:t0 + 512]) for r in range(4)], [wuq, qdnT], [ps])
                                        S.cp(qr.t[:, :], ps.t[0:64, :], [ps], [qr], e='act')
                                        S.dma('sp', cs.t[:, 0, :], ropec[:, t0:t0 + 512], [B_in], [cs])
                                        S.dma('sp', cs.t[:, 1, :], ropes[:, t0:t0 + 512], [B_in], [cs])
                                        ps2 = S.rot('bps', PS[0:3])
                                        S.mm(ps2.t[0:64, :], [(swapm, qr.t[:, :])], [cst, qr], [ps2])
                                        S.tt(qrs.t[:], ps2.t[0:64, :], cs.t[:, 1, :], ALU.mult, [ps2, cs], [qrs])
                                        S.tt(qr.t[:], qr.t[:], cs.t[:, 0, :], ALU.mult, [qr, cs], [qr])
                                        S.tt(qrT.t[0:64, t0:t0 + 512], qr.t[:], qrs.t[:], ALU.add, [qr, qrs], [qrT])
                                    for qb in range(NB):
                                        q0 = qb * 512
                                        psO, psD = PS[3], PS[4]
                                        def score(kt_):
                                            k0 = kt_ * 128
                                            ps_ = S.rot('sps', [PS[5], PS[6], PS[0]])
                                            S.mm(ps_.t[:, :], [(knT.t[:, k0:k0 + 128], qnT.t[:, q0:q0 + 512]), (krT.t[:, k0:k0 + 128], qrT.t[:, q0:q0 + 512])], [knT, qnT, krT, qrT], [ps_])
                                            return ps_
                                        ps_next = score(0)
                                        for kt in range(NT):
                                            ps = ps_next
                                            if kt + 1 < NT:
                                                ps_next = score(kt + 1)
                                            p = S.rot('pT', pT)
                                            S.act(p.t[:], ps.t[:, :], AF.Exp, [ps], [p], scale=scale)
                                            S.mm(psO.t[:, :], [(Vt.t[:, kt, :], p.t[:])], [Vt, p], [psO], start=(kt == 0), stop=(kt == NT - 1))
                                            S.mm(psD.t[:, :], [(onesb.t[:], p.t[:])], [onesb, p], [psD], start=(kt == 0), stop=(kt == NT - 1))
                                        S.op('dve', lambda: nc.vector.reciprocal(out=rec.t[:], in_=psD.t[:, :]), [psD], [rec])
                                        S.tt(oT.t[:, h, q0:q0 + 512], psO.t[:, :], rec.t[:], ALU.mult, [psO, rec], [oT])
                            _ck(3)
                            with ExitStack() as ph2:
                                S.barrier()
                                woa = tile(ph2, "b_woa", [128, NH, D], BF16)
                                S.dma('sp', woa.t[:], WB[("w_o_att", l)][0].rearrange("(k p) c -> p k c", p=128), WB[("w_o_att", l)][1], [woa])
                                gts = [tile(ph2, f"b_g{i}", [128, 4, 512]) for i in range(2)]
                                ost = [tile(ph2, f"b_o{i}", [128, 4, 512]) for i in range(2)]
                                for blk in range(NB):
                                    t0 = blk * 512
                                    for dg in range(4):
                                        g = S.rot('bg', gts)
                                        o = S.rot('bo', ost)
                                        S.dma('sp', g.t[:], gate_d[s][dg * 512:(dg + 1) * 512, t0:t0 + 512].rearrange("(a p) t -> p a t", p=128), [B_gate[s]], [g])
                                        for a in range(4):
                                            dc = dg * 4 + a
                                            ps = S.rot('bps', PS[0:3])
                                            S.mm(ps.t[:, :], [(woa.t[:, h, dc * 128:(dc + 1) * 128], oT.t[:, h, t0:t0 + 512]) for h in range(NH)], [woa, oT], [ps])
                                            S.tt(o.t[:, a, :], ps.t[:, :], g.t[:, a, :], ALU.mult, [ps, g], [o])
                                        S.dma('sp', ga_d[s][dg * 512:(dg + 1) * 512, t0:t0 + 512].rearrange("(a p) t -> p a t", p=128), o.t[:], [o], [B_ga[s]])

                    _ck(4)
                    with ExitStack() as ph:
                        S.barrier()
                        pin = [tile(ph, f"c0_in{i}", [128, S_ + 2]) for i in range(2)]
                        pout = [tile(ph, f"c0_out{i}", [128, S_]) for i in range(2)]
                        cm = tile(ph, "c0_cm", [128, 28])
                        S.tt(cm.t[:], col("mu0", 0, 28), col("mu1", 0, 28), ALU.add, [cols], [cm])
                        S.ts(cm.t[:], cm.t[:], -1.0, 1.0, ALU.mult, ALU.add, [cm], [cm])
                        for i in range(2):
                            S.op('dve', lambda: nc.vector.memset(pin[i].t[:, 0:1], 0.0), [], [pin[i]])
                            S.op('dve', lambda: nc.vector.memset(pin[i].t[:, S_ + 1:S_ + 2], 0.0), [], [pin[i]])
                        for j in range(28):
                            np_ = 128 if j < 27 else 32
                            a, o = pin[j % 2], pout[j % 2]
                            S.dma('sp', a.t[0:np_, 1:S_ + 1], rw_d[s][j * 128:j * 128 + np_, :], [B_rw[s]], [a])
                            S.ts(o.t[0:np_, :], a.t[0:np_, 1:S_ + 1], cm.t[0:np_, j:j + 1], None, ALU.mult, None, [a, cm], [o])
                            S.stt(o.t[0:np_, :], a.t[0:np_, 0:S_], col("mu0", j)[0:np_], o.t[0:np_, :], ALU.mult, ALU.add, [a, cols, o], [o])
                            S.stt(o.t[0:np_, :], a.t[0:np_, 2:S_ + 2], col("mu1", j)[0:np_], o.t[0:np_, :], ALU.mult, ALU.add, [a, cols, o], [o])
                            S.dma('sp', rwm_d[s][j * 128:j * 128 + np_, :], o.t[0:np_, :], [o], [B_rwm[s]])

                    _ck(5)
                    with ExitStack() as ph:
                        S.barrier()
                        wdu_t = tile(ph, "c_wdu", [128, 1024])
                        wiu_t = tile(ph, "c_wiu", [128, 1024])
                        S.dma('sp', wdu_t.t[:], wdu[l], [B_in], [wdu_t])
                        S.dma('sp', wiu_t.t[:], wiu[l], [B_in], [wiu_t])
                        X = [tile(ph, f"c_x{i}", [128, 8, 128]) for i in range(14)]
                        AR8 = tile(ph, "c_AR8", [128, 8, 2, 128])
                        AM = tile(ph, "c_AM", [128, 16, 384])
                        NP = [tile(ph, f"c_NP{i}", [128, 16, 128]) for i in range(2)]
                        GP = [tile(ph, f"c_GP{i}", [128, 16, 128]) for i in range(2)]
                        Y = tile(ph, "c_Y", [128, 16, 128])
                        WTin = tile(ph, "c_WTin", [128, 16, 64])
                        WT8 = tile(ph, "c_WT8", [128, 8, 128])
                        MT = tile(ph, "c_MT", [128, 16, 64])
                        Vtm = tile(ph, "c_Vtm", [128, 8, 128])
                        BHp = tile(ph, "c_BHp", [128, 16, 128])
                        KHp = tile(ph, "c_KHp", [128, 16, 128])
                        OT = tile(ph, "c_OT", [128, 1040])
                        wda = tile(ph, "c_wda", [128, 2, 128])
                        twd = tile(ph, "c_twd", [128, 128])
                        adm = tile(ph, "c_adm", [128, 128])
                        stm = [tile(ph, f"c_stm{i}", [128, 8, 64]) for i in range(2)]
                        PCt = tile(ph, "c_PC", [128, 8])
                        ST = [tile(ph, f"c_ST{d}", [128, 8, 64]) for d in range(2)]
                        stmp = tile(ph, "c_stmp", [128, 8, 64])
                        gB = {n: [Buf() for _ in range(4)] for n in ("AM", "N0", "N1", "G0", "G1", "Y")}
                        S.op('dve', lambda: nc.vector.memset(BHp.t[:], 0.0), [], [BHp])
                        S.op('dve', lambda: nc.vector.memset(KHp.t[:], 0.0), [], [KHp])
                        for d in range(2):
                            S.op('dve', lambda: nc.vector.memset(ST[d].t[:], 0.0), [], [ST[d]])
                        (r8, k8, v8, lw8, a8, L8, eLx, eLi, eNL, eCL, kk8, t8, b8, u8) = X
                        bc8 = lambda ap: ap.unsqueeze(2).broadcast_to([128, 8, 128])

                        def chunk_step(d, c):
                            t0 = c * C
                            rows = lambda base: rwm_d[s][base:base + 1024, t0:t0 + C].rearrange("(a p) t -> p a t", p=128)
                            S.dma('sp', r8.t[:], rows(0), [B_rwm[s]], [r8])
                            S.dma('sp', k8.t[:], rows(1024), [B_rwm[s]], [k8])
                            S.dma('sp', v8.t[:], rows(2048), [B_rwm[s]], [v8])
                            S.dma('sp', wda.t[:], rwm_d[s][3072:3328, t0:t0 + C].rearrange("(a p) t -> p a t", p=128), [B_rwm[s]], [wda])
                            S.act(twd.t[:], wda.t[:, 0, :], AF.Tanh, [wda], [twd])
                            S.ts(twd.t[:], twd.t[:], hsel[:, d:d + 1], None, ALU.mult, None, [twd, cst], [twd])
                            S.ts(adm.t[:], wda.t[:, 1, :], hsel[:, d:d + 1], None, ALU.mult, None, [wda, cst], [adm])
                            for (wt, src, bname, dst) in ((wdu_t, twd.t, f"w0_{d}", lw8), (wiu_t, None, f"a0_{d}", a8)):
                                for g2 in range(2):
                                    ps = S.rot('cps', PS[0:4])
                                    for q in range(4):
                                        hp = g2 * 4 + q
                                        rhs = twd.t[:, :] if src is not None else adm.t[:, :]
                                        S.mm(ps.t[:, q * 128:(q + 1) * 128], [(wt.t[:, hp * 128:(hp + 1) * 128], rhs)], [wt, twd, adm], [ps])
                                    for q in range(4):
                                        hp = g2 * 4 + q
                                        S.act(dst.t[:, hp, :], ps.t[:, q * 128:(q + 1) * 128], AF.Sigmoid, [ps, cols], [dst], bias=col(bname, hp))
                            _ck(5.1)
                            S.ts(lw8.t[:], lw8.t[:], -0.6065306597126334, None, ALU.mult, None, [lw8], [lw8])
                            for hp in range(8):
                                S.op('dve', lambda: nc.vector.tensor_tensor_scan(out=L8.t[:, hp, :], data0=lw8.t[:, hp, :], data1=lw8.t[:, hp, :], initial=0.0, op0=ALU.add, op1=ALU.bypass), [lw8], [L8])
                            LCb = L8.t[:, :, 127:128].broadcast_to([128, 8, 128])
                            S.act(PCt.t[:], L8.t[:, :, 127], AF.Exp, [L8], [PCt])
                            if d == 0:
                                S.tt(eLx.t[:], L8.t[:], lw8.t[:], ALU.subtract, [L8, lw8], [eLx])
                                S.tt(eCL.t[:], LCb, L8.t[:], ALU.subtract, [L8], [eCL])
                                S.act(eLi.t[:], L8.t[:], AF.Exp, [L8], [eLi])
                                S.act(eNL.t[:], L8.t[:], AF.Exp, [L8], [eNL], scale=-1.0)
                            else:
                                S.tt(eLx.t[:], LCb, L8.t[:], ALU.subtract, [L8], [eLx])
                                S.tt(eLi.t[:], eLx.t[:], lw8.t[:], ALU.add, [eLx, lw8], [eLi])
                                S.tt(eCL.t[:], LCb, eLi.t[:], ALU.subtract, [L8, eLi], [eCL])
                                S.act(eNL.t[:], eLi.t[:], AF.Exp, [eLi], [eNL], scale=-1.0)
                                S.act(eLi.t[:], eLi.t[:], AF.Exp, [eLi], [eLi])
                            S.act(eLx.t[:], eLx.t[:], AF.Exp, [eLx], [eLx])
                            S.act(eCL.t[:], eCL.t[:], AF.Exp, [eCL], [eCL])
                            _ck(5.2)
                            S.tt(kk8.t[:], k8.t[:], bc8(col("k_k", 0, 8)), ALU.mult, [k8, cols], [kk8])
                            S.tt(u8.t[:], kk8.t[:], kk8.t[:], ALU.mult, [kk8], [u8])
                            for g2 in range(2):
                                ps = S.rot('cps', PS[0:4])
                                for q in range(4):
                                    S.mm(ps.t[:, q * 128:(q + 1) * 128], [(bd_ones, u8.t[:, g2 * 4 + q, :])], [cst, u8], [ps])
                                S.ts(L8.t[:, g2 * 4:g2 * 4 + 4, :], ps.t[:, :].rearrange("p (a b) -> p a b", b=128), 1e-24, None, ALU.max, None, [ps], [L8])
                            S.act(L8.t[:], L8.t[:], AF.Sqrt, [L8], [L8])
                            S.op('dve', lambda: nc.vector.reciprocal(out=L8.t[:], in_=L8.t[:]), [L8], [L8])
                            S.tt(kk8.t[:], kk8.t[:], L8.t[:], ALU.mult, [kk8, L8], [kk8])
                            S.tt(t8.t[:], a8.t[:], bc8(col("k_a", 0, 8)), ALU.mult, [a8, cols], [t8])
                            S.tt(t8.t[:], t8.t[:], bc8(omka.t[:, 0:8]), ALU.add, [t8, omka], [t8])
                            S.tt(t8.t[:], t8.t[:], k8.t[:], ALU.mult, [t8, k8], [t8])
                            S.tt(b8.t[:], kk8.t[:], a8.t[:], ALU.mult, [kk8, a8], [b8])
                            S.stt(AR8.t[:, :, 0, :], kk8.t[:], -1.0, eLx.t[:], ALU.mult, ALU.mult, [kk8, eLx], [AR8])
                            S.tt(AR8.t[:, :, 1, :], r8.t[:], eLi.t[:], ALU.mult, [r8, eLi], [AR8])
                            S.tt(u8.t[:], r8.t[:], t8.t[:], ALU.mult, [r8, t8], [u8])
                            S.tt(u8.t[:], u8.t[:], bc8(col(f"r_k_{d}", 0, 8)), ALU.mult, [u8, cols], [u8])
                            psB = S.rot('cps', PS[0:4])
                            for hp in range(8):
                                S.mm(psB.t[:, hp * 2:hp * 2 + 2], [(u8.t[:, hp, :], hsel)], [u8, cst], [psB])
                            S.cp(OT.t[:, 1024:1040], psB.t[:, 0:16], [psB], [OT], e='act')
                            bt8, kt8, bh8, kh8 = a8, lw8, eLx, eLi
                            S.tt(bh8.t[:], b8.t[:], eCL.t[:], ALU.mult, [b8, eCL], [bh8])
                            S.tt(kh8.t[:], t8.t[:], eCL.t[:], ALU.mult, [t8, eCL], [kh8])
                            S.tt(bt8.t[:], b8.t[:], eNL.t[:], ALU.mult, [b8, eNL], [bt8])
                            S.tt(kt8.t[:], t8.t[:], eNL.t[:], ALU.mult, [t8, eNL], [kt8])
                            _ck(5.3)
                            for g2 in range(2):
                                ps = S.rot('cps', PS[0:4])
                                for q in range(4):
                                    S.tr(ps.t[:, q * 128:(q + 1) * 128], v8.t[:, g2 * 4 + q, :], ident, [v8, cst], [ps])
                                S.cp(Vtm.t[:, g2 * 4:g2 * 4 + 4, :], ps.t[:, :].rearrange("p (a b) -> p a b", b=128), [ps], [Vtm])
                                ps = S.rot('cps', PS[0:4])
                                for q in range(4):
                                    S.tr(ps.t[:, q * 128:(q + 1) * 128], AR8.t[:, g2 * 4 + q, 0, :], ident, [AR8, cst], [ps])
                                S.cp(Y.t[:, g2 * 8:g2 * 8 + 8, 0:64], ps.t[:, :].rearrange("p (a b) -> p a b", b=64), [ps], [gB["Y"][2 * g2], gB["Y"][2 * g2 + 1]])
                                for (src, dstp) in ((bh8, BHp), (kh8, KHp)):
                                    ps = S.rot('cps', PS[0:4])
                                    for q in range(4):
                                        S.tr(ps.t[:, q * 128:(q + 1) * 128], src.t[:, g2 * 4 + q, :], ident, [src, cst], [ps])
                                    pv4 = ps.t[:, :].rearrange("p (a h b) -> p a h b", h=2, b=64)
                                    dv4 = dstp.t[:, g2 * 8:g2 * 8 + 8, :].rearrange("p (a h) c -> p a h c", h=2)
                                    S.cp(dv4[:, :, 0, 0:64], pv4[:, :, 0, :], [ps], [dstp])
                                    S.cp(dv4[:, :, 1, 64:128], pv4[:, :, 1, :], [ps], [dstp])
                            if d == 0:
                                S.dma('sp', vtm_d[s][t0:t0 + C, :], Vtm.t[:].rearrange("p a b -> p (a b)"), [Vtm], [B_vtm[s]])
                            _ck(5.4)
                            am = [r8, k8]
                            rm = [L8, eNL]
                            for hh in range(2):
                                S.ts(am[hh].t[:], AR8.t[:, :, 0, :], hsel[:, hh:hh + 1], None, ALU.mult, None, [AR8, cst], [am[hh]])
                                S.ts(rm[hh].t[:], AR8.t[:, :, 1, :], hsel[:, hh:hh + 1], None, ALU.mult, None, [AR8, cst], [rm[hh]])
                            Aak = GP[1]
                            for h in range(16):
                                hp, hh = h // 2, h % 2
                                sl = slice(64 * hh, 64 * hh + 64)
                                g4 = h // 4
                                ps = S.rot('cps', PS[0:4])
                                S.mm(ps.t[:, 0:128], [(bt8.t[:, hp, :], am[hh].t[:, hp, :])], [bt8, am[hh]], [ps])
                                S.mm(ps.t[:, 128:256], [(bt8.t[:, hp, :], rm[hh].t[:, hp, :])], [bt8, rm[hh]], [ps])
                                S.mm(ps.t[:, 256:384], [(kt8.t[:, hp, :], am[hh].t[:, hp, :])], [kt8, am[hh]], [ps])
                                S.mm(ps.t[:, 384:512], [(kt8.t[:, hp, :], rm[hh].t[:, hp, :])], [kt8, rm[hh]], [ps])
                                S.tt(AM.t[:, h, 0:256], ps.t[:, 0:256], mA[d].t[:], ALU.mult, [ps, mA[d]], [gB["AM"][g4]])
                                S.tt(Aak.t[:, h, :], ps.t[:, 256:384], mA[d].t[:, 0:128], ALU.mult, [ps, mA[d]], [gB["G1"][g4]])
                                S.tt(AM.t[:, h, 256:384], ps.t[:, 384:512], mA[d].t[:, 128:256], ALU.mult, [ps, mA[d]], [gB["AM"][g4]])
                            for g4 in range(4):
                                ps = S.rot('cps', PS[0:4])
                                for q in range(4):
                                    h = g4 * 4 + q
                                    hp, hh = h // 2, h % 2
                                    sl = slice(64 * hh, 64 * hh + 64)
                                    S.mm(ps.t[:, q * 128:(q + 1) * 128], [(am[hh].t[:, hp, :], bt8.t[:, hp, :])], [am[hh], bt8], [ps])
                                S.tt(NP[0].t[:, g4 * 4:g4 * 4 + 4, :], ps.t[:, :].rearrange("p (a b) -> p a b", b=128), mN[d].t[:].rearrange("p (a b) -> p a b", b=128), ALU.mult, [ps, mN[d]], [gB["N0"][g4]])
                            for g8 in range(2):
                                ps = S.rot('cps', PS[0:4])
                                for q in range(8):
                                    h = g8 * 8 + q
                                    hp, hh = h // 2, h % 2
                                    S.mm(ps.t[:, q * 64:(q + 1) * 64], [(Aak.t[:, h, :], Vtm.t[:, hp, 64 * hh:64 * hh + 64])], [gB["G1"][h // 4], Vtm], [ps])
                                S.cp(Y.t[:, g8 * 8:g8 * 8 + 8, 64:128], ps.t[:, :].rearrange("p (a b) -> p a b", b=64), [ps], [gB["Y"][2 * g8], gB["Y"][2 * g8 + 1]])
                            _ck(5.5)
                            for p_ in range(7):
                                if p_ == 0:
                                    Gt, Gb = (lambda h: AM.t[:, h, 0:128]), gB["AM"]
                                else:
                                    gi = (p_ - 1) % 2
                                    Gt, Gb = (lambda h, gi=gi: GP[gi].t[:, h, :]), gB[f"G{gi}"]
                                ni = p_ % 2
                                Nt, Nb = (lambda h, ni=ni: NP[ni].t[:, h, :]), gB[f"N{ni}"]
                                for g4 in range(4):
                                    ps = S.rot('cps', PS[0:4])
                                    for q in range(4):
                                        h = g4 * 4 + q
                                        S.mm(ps.t[:, q * 128:(q + 1) * 128], [(Gt(h), Y.t[:, h, :])], [Gb[g4], gB["Y"][g4]], [ps])
                                    S.tt(Y.t[:, g4 * 4:g4 * 4 + 4, :], ps.t[:, :].rearrange("p (a b) -> p a b", b=128), Y.t[:, g4 * 4:g4 * 4 + 4, :], ALU.add, [ps, gB["Y"][g4]], [gB["Y"][g4]])
                                if p_ == 6:
                                    break
                                go = p_ % 2
                                no = (p_ + 1) % 2
                                for g4 in range(4):
                                    ps = S.rot('cps', PS[0:4])
                                    for q in range(4):
                                        h = g4 * 4 + q
                                        S.mm(ps.t[:, q * 128:(q + 1) * 128], [(Nt(h), Gt(h))], [Gb[g4], Nb[g4]], [ps])
                                    S.cp(GP[go].t[:, g4 * 4:g4 * 4 + 4, :], ps.t[:, :].rearrange("p (a b) -> p a b", b=128), [ps], [gB[f"G{go}"][g4]])
                                    if p_ < 5:
                                        ps = S.rot('cps', PS[0:4])
                                        for q in range(4):
                                            h = g4 * 4 + q
                                            S.mm(ps.t[:, q * 128:(q + 1) * 128], [(Gt(h), Nt(h))], [Gb[g4], Nb[g4]], [ps])
                                        S.cp(NP[no].t[:, g4 * 4:g4 * 4 + 4, :], ps.t[:, :].rearrange("p (a b) -> p a b", b=128), [ps], [gB[f"N{no}"][g4]])
                            _ck(5.6)
                            S.cp(WTin.t[:], Y.t[:, :, 0:64], gB["Y"], [WTin], e='dve')
                            for g2 in range(2):
                                ps = S.rot('cps', PS[0:4])
                                for q in range(4):
                                    hp = g2 * 4 + q
                                    S.tr(ps.t[:, q * 128:(q + 1) * 128], WTin.t[:, 2 * hp:2 * hp + 2, :].rearrange("p a b -> p (a b)"), ident, [WTin, cst], [ps])
                                S.cp(WT8.t[:, g2 * 4:g2 * 4 + 4, :], ps.t[:, :].rearrange("p (a b) -> p a b", b=128), [ps], [WT8])
                            _ck(5.7)
                            st = ST[d]
                            for hh in range(2):
                                S.ts(stm[hh].t[:], st.t[:], hsel[:, hh:hh + 1], None, ALU.mult, None, [st, cst], [stm[hh]])
                            for g8 in range(2):
                                ps = PS[4 + g8]
                                for q in range(8):
                                    h = g8 * 8 + q
                                    hp, hh = h // 2, h % 2
                                    sl = slice(64 * hh, 64 * hh + 64)
                                    S.mm(ps.t[:, q * 64:(q + 1) * 64], [(WT8.t[:, hp, :], stm[hh].t[:, hp, :])], [WT8, stm[hh]], [ps])
                                S.tt(MT.t[:, g8 * 8:g8 * 8 + 8, :], ps.t[:, :].rearrange("p (a b) -> p a b", b=64), Y.t[:, g8 * 8:g8 * 8 + 8, 64:128], ALU.add, [ps, gB["Y"][2 * g8], gB["Y"][2 * g8 + 1]], [MT])
                            for g8 in range(2):
                                ps = PS[4 + g8]
                                for q in range(8):
                                    h = g8 * 8 + q
                                    hp, hh = h // 2, h % 2
                                    sl = slice(64 * hh, 64 * hh + 64)
                                    S.mm(ps.t[:, q * 64:(q + 1) * 64],
                                         [(rm[hh].t[:, hp, :], st.t[:, hp, :]), (AM.t[:, h, 128:256], MT.t[:, h, :]), (AM.t[:, h, 256:384], Vtm.t[:, hp, 64 * hh:64 * hh + 64])],
                                         [rm[hh], st, gB["AM"][h // 4], MT, Vtm], [ps])
                                S.cp(OT.t[:, g8 * 512:(g8 + 1) * 512], ps.t[:, :], [ps], [OT], e='act')
                            S.dma('sp', od_d[s][d][t0:t0 + C, :], OT.t[:], [OT], [B_od[s][d]])
                            ps = PS[6]
                            for hp in range(8):
                                S.mm(ps.t[:, hp * 64:(hp + 1) * 64],
                                     [(BHp.t[:, 2 * hp, :], MT.t[:, 2 * hp, :]), (KHp.t[:, 2 * hp, :], Vtm.t[:, hp, 0:64]),
                                      (BHp.t[:, 2 * hp + 1, :], MT.t[:, 2 * hp + 1, :]), (KHp.t[:, 2 * hp + 1, :], Vtm.t[:, hp, 64:128])],
                                     [BHp, KHp, MT, Vtm], [ps])
                            S.tt(stmp.t[:], st.t[:], PCt.t[:].unsqueeze(2).broadcast_to([128, 8, 64]), ALU.mult, [st, PCt], [stmp])
                            S.tt(st.t[:], stmp.t[:], ps.t[:, :].rearrange("p (a b) -> p a b", b=64), ALU.add, [stmp, ps], [st])

                        for i in range(NCH):
                            chunk_step(0, i)
                            chunk_step(1, NCH - 1 - i)

                    _ck(6)
                    with ExitStack() as ph:
                        S.barrier()
                        worw = tile(ph, "d_worw", [128, 8, D], BF16)
                        wout = tile(ph, "d_wout", [128, KD, D], BF16)
                        S.dma('sp', worw.t[:], WB[("w_o_rwkv", l)][0].rearrange("(k p) c -> p k c", p=128), WB[("w_o_rwkv", l)][1], [worw])
                        S.dma('sp', wout.t[:], WB[("w_out", l)][0].rearrange("(k p) c -> p k c", p=128), WB[("w_out", l)][1], [wout])
                        wgA = tile(ph, "d_wgA", [128, 1024])
                        wgB = tile(ph, "d_wgB", [32, 1024])
                        S.dma('sp', wgA.t[:], wgu[l, 0:128, :], [B_in], [wgA])
                        S.dma('sp', wgB.t[:], wgu[l, 128:160, :], [B_in], [wgB])
                        gng = tile(ph, "d_gng", [128, 1024])
                        gnb = tile(ph, "d_gnb", [128, 1024])
                        S.dma('sp', gng.t[:], gn_g[l].partition_broadcast(128), [B_in], [gng])
                        S.dma('sp', gnb.t[:], gn_b[l].partition_broadcast(128), [B_in], [gnb])
                        of = [tile(ph, f"d_of{i}", [128, 1040]) for i in range(2)]
                        ob = [tile(ph, f"d_ob{i}", [128, 1040]) for i in range(2)]
                        vt = [tile(ph, f"d_vt{i}", [128, 1024]) for i in range(2)]
                        gdA = tile(ph, "d_gdA", [128, 128])
                        gdB = tile(ph, "d_gdB", [32, 128])
                        o3 = tile(ph, "d_o3", [128, 16, 64])
                        sq3 = tile(ph, "d_sq3", [128, 16, 64])
                        st16 = tile(ph, "d_st16", [128, 4, 16])
                        ofin = tile(ph, "d_ofin", [128, 1024], BF16)
                        ofT = tile(ph, "d_ofT", [128, 8, 512], BF16)
                        miT = tile(ph, "d_miT", [128, KD, 512], BF16)
                        gl = [tile(ph, f"d_gl{i}", [128, 512]) for i in range(3)]
                        tm = [tile(ph, f"d_tm{i}", [128, 512]) for i in range(2)]
                        xo = [tile(ph, f"d_xo{i}", [128, 512]) for i in range(2)]
                        v3 = lambda t: t.t[:, 0:1024].rearrange("p (a b) -> p a b", b=64)
                        b16 = lambda ap: ap.unsqueeze(2).broadcast_to([128, 16, 64])
                        for blk in range(NB):
                            t0 = blk * 512
                            for q4 in range(4):
                                tk = t0 + q4 * 128
                                f_, b_, v_ = of[q4 % 2], ob[q4 % 2], vt[q4 % 2]
                                S.dma('sp', f_.t[:], od_d[s][0][tk:tk + 128, :], [B_od[s][0]], [f_])
                                S.dma('sp', b_.t[:], od_d[s][1][tk:tk + 128, :], [B_od[s][1]], [b_])
                                S.dma('sp', v_.t[:], vtm_d[s][tk:tk + 128, :], [B_vtm[s]], [v_])
                                S.dma('sp', gdA.t[:], rwm_d[s][3328:3456, tk:tk + 128], [B_rwm[s]], [gdA])
                                S.dma('sp', gdB.t[:], rwm_d[s][3456:3488, tk:tk + 128], [B_rwm[s]], [gdB])
                                S.act(gdA.t[:], gdA.t[:], AF.Sigmoid, [gdA], [gdA])
                                S.act(gdB.t[:], gdB.t[:], AF.Sigmoid, [gdB], [gdB])
                                S.tt(f_.t[:], f_.t[:], b_.t[:], ALU.add, [f_, b_], [f_])
                                S.op('dve', lambda: nc.vector.reduce_sum(out=st16.t[:, 0, :], in_=v3(f_), axis=AX.X), [f_], [st16])
                                S.ts(st16.t[:, 0, :], st16.t[:, 0, :], 1.0 / 64, None, ALU.mult, None, [st16], [st16])
                                S.tt(o3.t[:], v3(f_), b16(st16.t[:, 0, :]), ALU.subtract, [f_, st16], [o3])
                                S.tt(sq3.t[:], o3.t[:], o3.t[:], ALU.mult, [o3], [sq3])
                                S.op('dve', lambda: nc.vector.reduce_sum(out=st16.t[:, 1, :], in_=sq3.t[:], axis=AX.X), [sq3], [st16])
                                S.rsqrt(st16.t[:, 2, :], st16.t[:, 1, :], 1.0 / 64, epsc.t[:, 1:2], [st16, epsc], [st16])
                                S.tt(o3.t[:], o3.t[:], b16(st16.t[:, 2, :]), ALU.mult, [o3, st16], [o3])
                                o3f = o3.t[:].rearrange("p a b -> p (a b)")
                                S.tt(o3f, o3f, gng.t[:], ALU.mult, [o3, gng], [o3])
                                S.tt(o3f, o3f, gnb.t[:], ALU.add, [o3, gnb], [o3])
                                S.tt(sq3.t[:], v3(v_), b16(f_.t[:, 1024:1040]), ALU.mult, [v_, f_], [sq3])
                                S.tt(o3.t[:], o3.t[:], sq3.t[:], ALU.add, [o3, sq3], [o3])
                                for hf in range(2):
                                    ps = S.rot('dps', PS[0:4])
                                    S.mm(ps.t[:, :], [(gdA.t[:], wgA.t[:, hf * 512:(hf + 1) * 512]), (gdB.t[:], wgB.t[:, hf * 512:(hf + 1) * 512])], [gdA, gdB, wgA, wgB], [ps])
                                    S.tt(ofin.t[:, hf * 512:(hf + 1) * 512], o3f[:, hf * 512:(hf + 1) * 512], ps.t[:, :], ALU.mult, [o3, ps], [ofin])
                                for cc in range(8):
                                    S.tr(PSB.t[:, cc * 128:(cc + 1) * 128], ofin.t[:, cc * 128:(cc + 1) * 128], identb.t[:], [ofin, identb], [PSB])
                                S.cp(ofT.t[:, :, q4 * 128:(q4 + 1) * 128], PSB.t[:, :].rearrange("p (a b) -> p a b", b=128), [PSB], [ofT])
                            for dc in range(KD):
                                g1, g2_ = S.rot('dgl', gl), S.rot('dgl', gl)
                                S.dma('sp', g1.t[:], gate_d[s][2048 + dc * 128:2048 + (dc + 1) * 128, t0:t0 + 512], [B_gate[s]], [g1])
                                S.dma('sp', g2_.t[:], ga_d[s][dc * 128:(dc + 1) * 128, t0:t0 + 512], [B_ga[s]], [g2_])
                                ps = S.rot('dps', PS[0:4])
                                S.mm(ps.t[:, :], [(worw.t[:, cc, dc * 128:(dc + 1) * 128], ofT.t[:, cc, :]) for cc in range(8)], [worw, ofT], [ps])
                                t_ = S.rot('dtm', tm)
                                S.tt(t_.t[:], ps.t[:, :], g1.t[:], ALU.mult, [ps, g1], [t_])
                                S.tt(miT.t[:, dc, :], t_.t[:], g2_.t[:], ALU.add, [t_, g2_], [miT])
                            for dc in range(KD):
                                xi = S.rot('dgl', gl)
                                S.dma('sp', xi.t[:], xin_d[dc * 128:(dc + 1) * 128, t0:t0 + 512], [xin_B], [xi])
                                ps = S.rot('dps', PS[0:4])
                                S.mm(ps.t[:, :], [(wout.t[:, k, dc * 128:(dc + 1) * 128], miT.t[:, k, :]) for k in range(KD)], [wout, miT], [ps])
                                xo_ = S.rot('dxo', xo)
                                S.stt(xo_.t[:], ps.t[:, :], gt1(dc), xi.t[:], ALU.mult, ALU.add, [ps, modT, xi], [xo_])
                                S.dma('sp', x1_d[s][dc * 128:(dc + 1) * 128, t0:t0 + 512], xo_.t[:], [xo_], [B_x1[s]])

                    _ck(7)
                    with ExitStack() as ph:
                        S.barrier()
                        xt = tile(ph, "f_xt", [128, KD, 512])
                        hT = tile(ph, "f_hT", [128, KD, 512], BF16)
                        xh = tile(ph, "f_xh", [128, KD, 2])
                        hTh = tile(ph, "f_hTh", [128, KD, 2], BF16)
                        sq = [tile(ph, f"f_sq{i}", [128, 512], BF16) for i in range(3)]
                        rstd = tile(ph, "f_rstd", [128, 512])
                        rstdh = tile(ph, "f_rstdh", [128, 2])
                        tmp = [tile(ph, f"f_tmp{i}", [128, 512]) for i in range(3)]
                        wsl = [tile(ph, f"f_w{i}", [128, KD, 512], BF16) for i in range(3)]
                        wds = [tile(ph, f"f_wd{i}", [128, 11, 512], BF16) for i in range(2)]
                        uT = tile(ph, "f_uT", [128, 44, 512], BF16)
                        acc = [tile(ph, f"f_acc{i}", [128, 512]) for i in range(2)]
                        hal = [tile(ph, f"f_hal{i}", [128, 2]) for i in range(2)]
                        xo = [tile(ph, f"f_xo{i}", [128, 512]) for i in range(2)]
                        last = (l == L - 1)
                        for blk in range(NB):
                            t0 = blk * 512
                            norm_block((xt, hT, sq, rstd, tmp), x1_d[s], B_x1[s], t0, 512, A2, sh2)
                            tl, tr_ = max(t0 - 1, 0), min(t0 + 512, S_ - 1)
                            S.dma('sp', xh.t[:, :, 0:1], x1_d[s][:, tl:tl + 1].rearrange("(k p) t -> p k t", p=128), [B_x1[s]], [xh], allow_slow_non_contiguous=True)
                            S.dma('sp', xh.t[:, :, 1:2], x1_d[s][:, tr_:tr_ + 1].rearrange("(k p) t -> p k t", p=128), [B_x1[s]], [xh], allow_slow_non_contiguous=True)
                            pss = PS[6]
                            for k in range(KD):
                                sqk = S.rot('sq', sq)
                                S.act(sqk.t[:, 0:2], xh.t[:, k, :], AF.Square, [xh], [sqk])
                                S.mm(pss.t[:, 0:2], [(onesb.t[:], sqk.t[:, 0:2])], [onesb, sqk], [pss], start=(k == 0), stop=(k == KD - 1))
                            S.rsqrt(rstdh.t[:], pss.t[:, 0:2], 1.0 / D, epsc.t[:, 0:1], [pss, epsc], [rstdh])
                            for k in range(KD):
                                tk = S.rot('ntmp', tmp)
                                S.tt(tk.t[:, 0:2], xh.t[:, k, :], rstdh.t[:], ALU.mult, [xh, rstdh], [tk])
                                S.act(hTh.t[:, k, :], tk.t[:, 0:2], AF.Identity, [tk, modT, der], [hTh], scale=A2(k), bias=sh2(k))
                            if blk == 0:
                                S.op('dve', lambda: nc.vector.memset(hTh.t[:, :, 0:1], 0.0), [], [hTh])
                            if blk == NB - 1:
                                S.op('dve', lambda: nc.vector.memset(hTh.t[:, :, 1:2], 0.0), [], [hTh])
                            for sj in range(11):
                                wa, wb = S.rot('fw', wsl), S.rot('fw', wsl)
                                S.dma('sp', wa.t[:], WB[("w_up", l)][0][sj], [WB[("w_up", l)][1][sj]], [wa])
                                S.dma('sp', wb.t[:], WB[("w_up", l)][0][11 + sj], [WB[("w_up", l)][1][11 + sj]], [wb])
                                for a in range(4):
                                    j = sj * 4 + a
                                    psa = S.rot('fpa', PS[0:2])
                                    psh = PS[5]
                                    psb_ = S.rot('fpb', PS[2:4])
                                    S.mm(psa.t[:, :], [(wa.t[:, k, a * 128:(a + 1) * 128], hT.t[:, k, :]) for k in range(KD)], [wa, hT], [psa])
                                    S.mm(psh.t[:, 0:2], [(wa.t[:, k, a * 128:(a + 1) * 128], hTh.t[:, k, :]) for k in range(KD)], [wa, hTh], [psh])
                                    S.mm(psb_.t[:, :], [(wb.t[:, k, a * 128:(a + 1) * 128], hT.t[:, k, :]) for k in range(KD)], [wb, hT], [psb_])
                                    ac = S.rot('facc', acc)
                                    hl = S.rot('fhal', hal)
                                    S.cp(hl.t[:], psh.t[:, 0:2], [psh], [hl], e='act')
                                    S.act(ac.t[:], psa.t[:, :], AF.Identity, [psa, cols], [ac], scale=col("cw1", j), bias=col("cb", j))
                                    S.stt(ac.t[:, 1:512], psa.t[:, 0:511], col("cw0", j), ac.t[:, 1:512], ALU.mult, ALU.add, [psa, cols, ac], [ac])
                                    S.stt(ac.t[:, 0:511], psa.t[:, 1:512], col("cw2", j), ac.t[:, 0:511], ALU.mult, ALU.add, [psa, cols, ac], [ac])
                                    S.stt(ac.t[:, 0:1], hl.t[:, 0:1], col("cw0", j), ac.t[:, 0:1], ALU.mult, ALU.add, [hl, cols, ac], [ac])
                                    S.stt(ac.t[:, 511:512], hl.t[:, 1:2], col("cw2", j), ac.t[:, 511:512], ALU.mult, ALU.add, [hl, cols, ac], [ac])
                                    S.act(ac.t[:], ac.t[:], AF.Silu, [ac], [ac])
                                    S.tt(uT.t[:, j, :], ac.t[:], psb_.t[:, :], ALU.mult, [ac, psb_], [uT])
                            for dg in range(4):
                                pso = PS[0:4]
                                for rs in range(4):
                                    wd_ = S.rot('fwd', wds)
                                    S.dma('sp', wd_.t[:], WB[("w_down", l)][0][dg * 4 + rs], [WB[("w_down", l)][1][dg * 4 + rs]], [wd_])
                                    for a in range(11):
                                        for d4 in range(4):
                                            S.mm(pso[d4].t[:, :], [(wd_.t[:, a, d4 * 128:(d4 + 1) * 128], uT.t[:, rs * 11 + a, :])], [wd_, uT], [pso[d4]],
                                                 start=(rs == 0 and a == 0), stop=(rs == 3 and a == 10))
                                for d4 in range(4):
                                    dc = dg * 4 + d4
                                    xo_ = S.rot('fxo', xo)
                                    S.stt(xo_.t[:], pso[d4].t[:, :], gt2(dc), xt.t[:, dc, :], ALU.mult, ALU.add, [pso[d4], modT, xt], [xo_])
                                    S.dma('sp', x2_d[s][dc * 128:(dc + 1) * 128, t0:t0 + 512], xo_.t[:], [xo_], [B_x2[s]])
                        if last:
                            for blk in range(NB):
                                t0 = blk * 512
                                S.dma('sp', xt.t[:], x2_d[s][:, t0:t0 + 512].rearrange("(k p) t -> p k t", p=128), [B_x2[s]], [xt])
                                pss = S.rot('nps', [PS[5], PS[6]])
                                for k in range(KD):
                                    sqk = S.rot('sq', sq)
                                    S.act(sqk.t[:], xt.t[:, k, :], AF.Square, [xt], [sqk])
                                    S.mm(pss.t[:, :], [(onesb.t[:], sqk.t[:])], [onesb, sqk], [pss], start=(k == 0), stop=(k == KD - 1))
                                S.rsqrt(rstd.t[:], pss.t[:, :], 1.0 / D, epsc.t[:, 0:1], [pss, epsc], [rstd])
                                for k in range(KD):
                                    xo_ = S.rot('fxo', xo)
                                    S.stt(xo_.t[:], xt.t[:, k, :], col("final_norm_g", k), rstd.t[:], ALU.mult, ALU.mult, [xt, cols, rstd], [xo_])
                                    S.dma('sp', yT[s, k * 128:(k + 1) * 128, t0:t0 + 512], xo_.t[:], [xo_], [B_y])
                    cur_d[s], cur_B[s] = x2_d[s], B_x2[s]
          except _Stop:
            break
        S.finish()
    return nc


def _fm(v):
    v = np.asarray(v, np.float32).reshape(-1)
    n = (v.size + 127) // 128
    p = np.zeros(n * 128, np.float32)
    p[:v.size] = v
    return np.ascontiguousarray(p.reshape(n, 128).T)


def _host_consts(S_):
    cst = np.zeros((128, NCST), np.float32)
    cst[:, 0:128] = np.eye(128)
    bd = np.zeros((128, 128), np.float32)
    bd[:64, :64] = 1
    bd[64:, 64:] = 1
    cst[:, 128:256] = bd
    p = np.arange(128)[:, None]
    f = np.arange(128)[None, :]
    cst[:, 256:384] = (p < f)
    cst[:, 384:512] = (p <= f)
    cst[:, 512:640] = (p > f)
    cst[:, 640:768] = (p >= f)
    cst[:64, 768] = 1
    cst[64:, 769] = 1
    sw = np.zeros((64, 64), np.float32)
    for m in range(32):
        sw[m + 32, m] = -1.0
        sw[m, m + 32] = 1.0
    cst[:64, 770:834] = sw
    inv = (1.0 / (np.float32(10000.0) ** (np.arange(0, 64, 2, dtype=np.float32) / np.float32(64)))).astype(np.float32)
    ang = np.arange(S_, dtype=np.float32)[:, None] * inv[None, :]
    cos = np.cos(ang).astype(np.float32).T
    sin = np.sin(ang).astype(np.float32).T
    return cst, np.ascontiguousarray(np.concatenate([cos, cos], 0)), np.ascontiguousarray(np.concatenate([sin, sin], 0))


def _pack_cols(I, L):
    out = np.zeros((L, 128, NCOL), np.float32)
    for l in range(L):
        def put(name, v):
            a = _fm(v)
            out[l, :, CO[name]:CO[name] + a.shape[1]] = a
        put("norm_mix_g", I["norm_mix_g"][l])
        put("norm_ffn_g", I["norm_ffn_g"][l])
        put("final_norm_g", I["final_norm_g"])
        put("ada_b", I["ada_b"][l])
        put("q_norm_g", I["q_norm_g"][l])
        put("kv_norm_g", I["kv_norm_g"][l])
        put("mu0", I["rwkv_mu"][l, 0])
        put("mu1", I["rwkv_mu"][l, 1])
        for d in range(2):
            put(f"w0_{d}", I["rwkv_w0"][l, d])
            put(f"a0_{d}", I["rwkv_a0"][l, d])
            put(f"r_k_{d}", I["rwkv_r_k"][l, d])
        put("k_k", I["rwkv_k_k"][l])
        put("k_a", I["rwkv_k_a"][l])
        for i in range(3):
            put(f"cw{i}", I["conv_w"][l, i])
        put("cb", I["conv_b"][l])
    return out


_NC_CACHE = {}


def run_groups(I, xs, cs, S_, NSEQ, L, n_cores):
    key = (S_, NSEQ, L)
    if key not in _NC_CACHE:
        _NC_CACHE[key] = build(S_, NSEQ, L)
    nc = _NC_CACHE[key]
    cst, rc, rs = _host_consts(S_)
    f32 = lambda a: np.ascontiguousarray(np.asarray(a, np.float32))
    shared = {
        "ada_w": f32(I["ada_w"]), "w_in": f32(I["w_in"]), "w_uq": f32(I["w_uq"]), "w_ukv": f32(I["w_ukv"]),
        "w_o_att": f32(I["w_o_att"]), "wdu": f32(I["rwkv_w_decay_up"]).reshape(L, 128, 1024),
        "wiu": f32(I["rwkv_w_iclr_up"]).reshape(L, 128, 1024), "wgu": f32(I["rwkv_w_gate_up"]),
        "w_o_rwkv": f32(I["w_o_rwkv"]), "w_out": f32(I["w_out"]), "w_up": f32(I["w_ffn_up"]), "w_down": f32(I["w_ffn_down"]),
        "gn_g": f32(I["rwkv_gn_g"]).reshape(L, 1, 1024), "gn_b": f32(I["rwkv_gn_b"]).reshape(L, 1, 1024),
        "cols": _pack_cols(I, L), "cst": cst, "ropec": rc, "ropes": rs,
    }
    in_maps = []
    for x, c in zip(xs, cs):
        m = dict(shared)
        m["xT"] = np.ascontiguousarray(np.transpose(x, (0, 2, 1)))
        m["c_fm"] = np.ascontiguousarray(np.transpose(np.asarray(c, np.float32).reshape(NSEQ, KD, 128), (2, 1, 0)))
        in_maps.append(m)
    res = run_bass_kernel_spmd(nc, in_maps, core_ids=list(range(n_cores)))
    return [np.ascontiguousarray(np.transpose(r["yT"], (0, 2, 1))) for r in res.results]


def kernel(**I):
    xp, xs_ = np.asarray(I["x_prompt"], np.float32), np.asarray(I["x_sample"], np.float32)
    cp, cs_ = np.asarray(I["c_prompt"], np.float32), np.asarray(I["c_sample"], np.float32)
    S_ = xp.shape[1]
    L = I["ada_w"].shape[0]
    allx = np.concatenate([xp, xs_], 0)
    allc = np.concatenate([cp, cs_], 0)
    n = allx.shape[0]
    NSEQ = 2
    assign = [[0, 1], [2, 3], [4, 5], [6, 7], [8, 8], [9, 9], [10, 10], [11, 11]]
    xs = [allx[a] for a in assign]
    cs = [allc[a] for a in assign]
    outs = run_groups(I, xs, cs, S_, NSEQ, L, 8)
    y = np.zeros_like(allx)
    for a, o in zip(assign, outs):
        for j, si in enumerate(a):
            y[si] = o[j]
    return (y[:xp.shape[0]], y[xp.shape[0]:])
```

```python
import numpy as np
from contextlib import ExitStack
import concourse.bass as bass
import concourse.mybir as mybir
from concourse.bass_utils import run_bass_kernel_spmd

F32 = mybir.dt.float32
BF16 = mybir.dt.bfloat16
AF = mybir.ActivationFunctionType
ALU = mybir.AluOpType
AX = mybir.AxisListType

D = 2048
KD = 16
DFF = 5632
NH = 8
DIN = 8416
C = 128


class Buf:
    __slots__ = ("w", "r")

    def __init__(self):
        self.w = None
        self.r = {}


class TT:
    __slots__ = ("t", "b")

    def __init__(self, t):
        self.t = t
        self.b = Buf()


class Sched:
    def __init__(self, nc, es, ndma=12):
        self.nc = nc
        self.eng = {'pe': nc.tensor, 'act': nc.scalar, 'dve': nc.vector, 'pool': nc.gpsimd, 'sp': nc.sync}
        self.sem = {}
        self.cnt = {}
        self.known = {k: {} for k in self.eng}
        for k in ['pe', 'act', 'dve', 'pool']:
            self.sem[k] = es.enter_context(nc.semaphore("s_" + k))
            self.cnt[k] = 0
        self.dpool, self.dcnt, self.drr = {}, {}, {}
        for q in ['sp', 'pool']:
            self.dpool[q] = [es.enter_context(nc.semaphore(f"d_{q}_{i}")) for i in range(ndma)]
            self.dcnt[q] = [0] * ndma
            self.drr[q] = 0
        self.rots = {}
        self.stopped = False
        _SREF[0] = self

    def rot(self, name, items):
        i = self.rots.get(name, 0)
        self.rots[name] = i + 1
        return items[i % len(items)]

    def _wait(self, e, deps):
        eng = self.eng[e]
        kn = self.known[e]
        for (sk, val) in deps:
            if kn.get(sk, 0) >= val:
                continue
            if sk[0] == 'c':
                if sk[1] == e and e == 'pe':
                    continue
                s = self.sem[sk[1]]
            else:
                s = self.dpool[sk[1]][sk[2]]
            eng.wait_ge(s, val)
            kn[sk] = val

    @staticmethod
    def _deps(reads, writes):
        deps = []
        for b in reads:
            if b.w is not None:
                deps.append(b.w)
        for b in writes:
            if b.w is not None:
                deps.append(b.w)
            deps.extend(b.r.items())
        return deps

    @staticmethod
    def _commit(ev, reads, writes):
        for b in reads:
            if b.r.get(ev[0], 0) < ev[1]:
                b.r[ev[0]] = ev[1]
        for b in writes:
            b.w = ev
            b.r = {}

    def op(self, e, fn, reads=(), writes=()):
        if self.stopped:
            return
        reads = [x.b if isinstance(x, TT) else x for x in reads]
        writes = [x.b if isinstance(x, TT) else x for x in writes]
        self._wait(e, self._deps(reads, writes))
        ins = fn()
        self.cnt[e] += 1
        ins.then_inc(self.sem[e], 1)
        ev = (('c', e), self.cnt[e])
        self._commit(ev, reads, writes)

    def dma(self, q, out, in_, reads=(), writes=(), **kw):
        if self.stopped:
            return
        reads = [x.b if isinstance(x, TT) else x for x in reads]
        writes = [x.b if isinstance(x, TT) else x for x in writes]
        i = self.drr[q]
        self.drr[q] = (i + 1) % len(self.dpool[q])
        sk = ('d', q, i)
        deps = self._deps(reads, writes)
        if self.dcnt[q][i] > 0:
            deps.append((sk, self.dcnt[q][i]))
        self._wait(q, deps)
        ins = self.eng[q].dma_start(out=out, in_=in_, **kw)
        self.dcnt[q][i] += 16
        ins.then_inc(self.dpool[q][i], 16)
        self._commit((sk, self.dcnt[q][i]), reads, writes)

    def barrier(self):
        if self.stopped:
            return
        evs = [(('c', e), self.cnt[e]) for e in self.cnt if self.cnt[e] > 0]
        for q in self.dpool:
            for i in range(len(self.dpool[q])):
                if self.dcnt[q][i] > 0:
                    evs.append((('d', q, i), self.dcnt[q][i]))
        for e in self.eng:
            self._wait(e, evs)

    def finish(self):
        for q in self.dpool:
            for i, s in enumerate(self.dpool[q]):
                if self.dcnt[q][i] > 0:
                    self.nc.sync.wait_ge(s, self.dcnt[q][i])

    def mm(self, out, pairs, reads, writes, start=True, stop=True):
        nc = self.nc

        def fn():
            ins = None
            n = len(pairs)
            for i, (l, r) in enumerate(pairs):
                ins = nc.tensor.matmul(out, lhsT=l, rhs=r, start=(start and i == 0), stop=(stop and i == n - 1))
            return ins
        self.op('pe', fn, reads, writes)

    def tr(self, out, in_, ident, reads, writes):
        nc = self.nc
        self.op('pe', lambda: nc.tensor.transpose(out, in_, ident), reads, writes)

    def act(self, out, in_, func, reads, writes, **kw):
        nc = self.nc
        self.op('act', lambda: nc.scalar.activation(out=out, in_=in_, func=func, **kw), reads, writes)

    def tt(self, out, in0, in1, op, reads, writes, e='dve'):
        eng = self.eng[e]
        self.op(e, lambda: eng.tensor_tensor(out=out, in0=in0, in1=in1, op=op), reads, writes)

    def ts(self, out, in0, s1, s2, op0, op1, reads, writes):
        nc = self.nc
        if s2 is None:
            self.op('dve', lambda: nc.vector.tensor_scalar(out=out, in0=in0, scalar1=s1, scalar2=None, op0=op0), reads, writes)
        else:
            self.op('dve', lambda: nc.vector.tensor_scalar(out=out, in0=in0, scalar1=s1, scalar2=s2, op0=op0, op1=op1), reads, writes)

    def stt(self, out, in0, scalar, in1, op0, op1, reads, writes):
        nc = self.nc
        self.op('dve', lambda: nc.vector.scalar_tensor_tensor(out=out, in0=in0, scalar=scalar, in1=in1, op0=op0, op1=op1), reads, writes)

    def cp(self, out, in_, reads, writes, e=None):
        nc = self.nc
        if e is None:
            e = self.rot('cp', ['act', 'dve'])
        if e == 'act':
            self.op('act', lambda: nc.scalar.copy(out=out, in_=in_), reads, writes)
        else:
            self.op('dve', lambda: nc.vector.tensor_copy(out=out, in_=in_), reads, writes)

    def rsqrt(self, out, in_, scale, eps_ap, reads, writes):
        self.act(out, in_, AF.Sqrt, reads, writes, scale=scale, bias=eps_ap)
        nc = self.nc
        self.op('dve', lambda: nc.vector.reciprocal(out=out, in_=out), writes, writes)


def _col_layout():
    items = [("norm_mix_g", 16), ("norm_ffn_g", 16), ("final_norm_g", 16), ("ada_b", 96), ("q_norm_g", 4),
             ("kv_norm_g", 2), ("mu0", 28), ("mu1", 28), ("w0_0", 8), ("w0_1", 8), ("a0_0", 8), ("a0_1", 8),
             ("k_k", 8), ("k_a", 8), ("r_k_0", 8), ("r_k_1", 8), ("cw0", 44), ("cw1", 44), ("cw2", 44), ("cb", 44)]
    co, off = {}, 0
    for n, w in items:
        co[n] = off
        off += w
    return co, off


CO, NCOL = _col_layout()
CST = {"ident": 0, "bd": 128, "lt": 256, "le": 384, "gt": 512, "ge": 640, "hsel": 768, "swap": 770}
NCST = 770 + 64


def _in_chunks():
    ch = []
    for j in range(4):
        ch.append(("q", j * 128, 128, j))
    for j in range(2):
        ch.append(("kv", 512 + j * 128, 128, j))
    ch.append(("rope", 768, 64, 0))
    for j in range(27):
        ch.append(("rw", 832 + j * 128, 128, j))
    ch.append(("rw", 832 + 27 * 128, 32, 27))
    for j in range(32):
        ch.append(("gate", 4320 + j * 128, 128, j))
    slabs, cur = [], []
    for c in ch:
        if cur and ((c[1] + c[2] - cur[0][1] > 512) or (c[0] != cur[-1][0] and c[0] in ("rw", "gate"))):
            slabs.append(cur)
            cur = []
        cur.append(c)
    slabs.append(cur)
    return slabs


class _Stop(Exception):
    pass


_KSTOP = [99]


_SREF = [None]


def _ck(n):
    if _KSTOP[0] <= n:
        _SREF[0].stopped = True


def build(S_, NSEQ, L):
    NB = S_ // 512
    NT = S_ // 128
    NCH = S_ // C
    nc = bass.Bass("TRN2", target_bir_lowering=False)

    def din(name, shape, dt=F32):
        return nc.dram_tensor(name, shape, dt, kind="ExternalInput").ap()

    def dscr(name, shape, dt=F32):
        return nc.dram_tensor(name, shape, dt, kind="Internal").ap()

    xT = din("xT", [NSEQ, D, S_])
    c_fm = din("c_fm", [128, KD, NSEQ])
    ada_w = din("ada_w", [L, D, 6 * D])
    w_in = din("w_in", [L, D, DIN])
    w_uq = din("w_uq", [L, 512, 1536])
    w_ukv = din("w_ukv", [L, 256, 2048])
    w_o_att = din("w_o_att", [L, 1024, D])
    wdu = din("wdu", [L, 128, 1024])
    wiu = din("wiu", [L, 128, 1024])
    wgu = din("wgu", [L, 160, 1024])
    w_o_rwkv = din("w_o_rwkv", [L, 1024, D])
    w_out = din("w_out", [L, D, D])
    w_up = din("w_up", [L, D, 2 * DFF])
    w_down = din("w_down", [L, DFF, D])
    gn_g = din("gn_g", [L, 1, 1024])
    gn_b = din("gn_b", [L, 1, 1024])
    cols_d = din("cols", [L, 128, NCOL])
    cst_d = din("cst", [128, NCST])
    ropec = din("ropec", [64, S_])
    ropes = din("ropes", [64, S_])
    yT = nc.dram_tensor("yT", [NSEQ, D, S_], F32, kind="ExternalOutput").ap()

    rw_d = [dscr(f"rw{s}", [3584, S_]) for s in range(NSEQ)]
    rwm_d = [dscr(f"rwm{s}", [3584, S_]) for s in range(NSEQ)]
    gate_d = [dscr(f"gate{s}", [4096, S_]) for s in range(NSEQ)]
    ga_d = [dscr(f"ga{s}", [D, S_]) for s in range(NSEQ)]
    od_d = [[dscr(f"od{s}_{d}", [S_, 1040]) for d in range(2)] for s in range(NSEQ)]
    vtm_d = [dscr(f"vtm{s}", [S_, 1024]) for s in range(NSEQ)]
    x1_d = [dscr(f"x1_{s}", [D, S_]) for s in range(NSEQ)]
    x2_d = [dscr(f"x2_{s}", [D, S_]) for s in range(NSEQ)]
    B_rw = [Buf() for _ in range(NSEQ)]
    B_rwm = [Buf() for _ in range(NSEQ)]
    B_gate = [Buf() for _ in range(NSEQ)]
    B_ga = [Buf() for _ in range(NSEQ)]
    B_od = [[Buf() for _ in range(2)] for _ in range(NSEQ)]
    B_vtm = [Buf() for _ in range(NSEQ)]
    B_x1 = [Buf() for _ in range(NSEQ)]
    B_x2 = [Buf() for _ in range(NSEQ)]
    B_y = Buf()
    B_in = Buf()

    with ExitStack() as es:
        S = Sched(nc, es)

        _tc = [0]

        def tile(st, name, shape, dt=F32):
            _tc[0] += 1
            return TT(st.enter_context(nc.sbuf_tensor(f"{name}_{_tc[0]}", shape, dt)))

        PS = [TT(es.enter_context(nc.psum_tensor(f"ps{i}", [128, 512], F32))) for i in range(7)]
        PSB = TT(es.enter_context(nc.psum_tensor("psb", [128, 1024], BF16)))

        cst = tile(es, "cst", [128, NCST])
        S.dma('sp', cst.t[:], cst_d, [B_in], [cst])
        ident = cst.t[:, 0:128]
        bd_ones = cst.t[:, 128:256]
        hsel = cst.t[:, 768:770]
        swapm = cst.t[0:64, 770:834]
        identb = tile(es, "identb", [128, 128], BF16)
        S.cp(identb.t[:], ident, [cst], [identb], e='dve')
        onesb = tile(es, "onesb", [128, 128], BF16)
        S.op('dve', lambda: nc.vector.memset(onesb.t[:], 1.0), [], [onesb])
        onesf = tile(es, "onesf", [128, 128])
        S.op('dve', lambda: nc.vector.memset(onesf.t[:], 1.0), [], [onesf])
        epsc = tile(es, "epsc", [128, 2])
        S.op('dve', lambda: nc.vector.memset(epsc.t[:, 0:1], 1e-6), [], [epsc])
        S.op('dve', lambda: nc.vector.memset(epsc.t[:, 1:2], 64e-5), [], [epsc])
        zero_c = tile(es, "zero_c", [128, 1])
        S.op('dve', lambda: nc.vector.memset(zero_c.t[:], 0.0), [], [zero_c])
        mA = [tile(es, f"mA{d}", [128, 256]) for d in range(2)]
        mN = [tile(es, f"mN{d}", [128, 512]) for d in range(2)]
        for d in range(2):
            s_, i_, n_ = (("lt", "le", "gt") if d == 0 else ("gt", "ge", "lt"))
            S.cp(mA[d].t[:, 0:128], cst.t[:, CST[s_]:CST[s_] + 128], [cst], [mA[d]], e='dve')
            S.cp(mA[d].t[:, 128:256], cst.t[:, CST[i_]:CST[i_] + 128], [cst], [mA[d]], e='dve')
            for q in range(4):
                S.cp(mN[d].t[:, q * 128:(q + 1) * 128], cst.t[:, CST[n_]:CST[n_] + 128], [cst], [mN[d]], e='dve')

        WB = {}

        def emit_casts(l):
            slA = _in_chunks()
            t = dscr(f"w_in_sl{l}", [len(slA), 128, KD, 512], BF16)
            Bs = []
            for si, slab in enumerate(slA):
                c0, c1 = slab[0][1], slab[-1][1] + slab[-1][2]
                B = Buf()
                S.dma('pool', t[si, :, :, 0:c1 - c0], w_in[l, :, c0:c1].rearrange("(k p) c -> p k c", p=128), [B_in], [B])
                Bs.append(B)
            WB[("w_in", l)] = (t, Bs)
            t = dscr(f"w_up_sl{l}", [22, 128, KD, 512], BF16)
            Bs = []
            for sj in range(22):
                c0 = sj * 512 if sj < 11 else DFF + (sj - 11) * 512
                B = Buf()
                S.dma('pool', t[sj], w_up[l, :, c0:c0 + 512].rearrange("(k p) c -> p k c", p=128), [B_in], [B])
                Bs.append(B)
            WB[("w_up", l)] = (t, Bs)
            t = dscr(f"w_down_sl{l}", [16, 128, 11, 512], BF16)
            Bs = []
            for dg in range(4):
                for rs in range(4):
                    B = Buf()
                    S.dma('pool', t[dg * 4 + rs], w_down[l, rs * 1408:(rs + 1) * 1408, dg * 512:(dg + 1) * 512].rearrange("(a p) c -> p a c", p=128), [B_in], [B])
                    Bs.append(B)
            WB[("w_down", l)] = (t, Bs)
            for (nm, src, shape) in (("w_uq", w_uq, [512, 1536]), ("w_ukv", w_ukv, [256, 2048]),
                                     ("w_o_att", w_o_att, [1024, D]), ("w_o_rwkv", w_o_rwkv, [1024, D]), ("w_out", w_out, [D, D])):
                t = dscr(f"{nm}_bf{l}", shape, BF16)
                Bs = []
                for r0 in range(0, shape[0], 256):
                    B = Buf()
                    S.dma('pool', t[r0:r0 + 256, :], src[l, r0:r0 + 256, :], [B_in], [B])
                    Bs.append(B)
                WB[(nm, l)] = (t, Bs)

        emit_casts(0)
        cur_d, cur_B = [xT[s] for s in range(NSEQ)], [B_in for _ in range(NSEQ)]

        for l in range(L):
          try:
            with ExitStack() as ls:
                S.barrier()
                cols = tile(ls, "cols", [128, NCOL])
                S.dma('sp', cols.t[:], cols_d[l], [B_in], [cols])

                def col(name, k=0, n=1):
                    return cols.t[:, CO[name] + k:CO[name] + k + n]
                modT = tile(ls, "modT", [128, NSEQ, 96])
                der = tile(ls, "der", [128, NSEQ, 2, KD])
                omka = tile(ls, "omka", [128, 8])
                S.ts(omka.t[:], col("k_a", 0, 8), -1.0, 1.0, ALU.mult, ALU.add, [cols], [omka])

                with ExitStack() as ph:
                    S.barrier()
                    csil = tile(ph, "csil", [128, KD, NSEQ])
                    S.dma('sp', csil.t[:], c_fm, [B_in], [csil])
                    S.act(csil.t[:], csil.t[:], AF.Silu, [csil], [csil])
                    slabs = [tile(ph, f"mslab{i}", [128, KD, 512]) for i in range(2)]
                    psm = PS[0]
                    for sb in range(24):
                        sl = slabs[sb % 2]
                        S.dma('sp', sl.t[:], ada_w[l, :, sb * 512:(sb + 1) * 512].rearrange("(k p) c -> p k c", p=128), [B_in], [sl])
                        for jj in range(4):
                            j = sb * 4 + jj
                            S.mm(psm.t[:, j * NSEQ:(j + 1) * NSEQ],
                                 [(sl.t[:, k, jj * 128:(jj + 1) * 128], csil.t[:, k, :]) for k in range(KD)],
                                 [sl, csil], [psm])
                    pv = psm.t[:, 0:96 * NSEQ].rearrange("p (j s) -> p j s", s=NSEQ)
                    for s in range(NSEQ):
                        S.tt(modT.t[:, s, :], pv[:, :, s], col("ada_b", 0, 96), ALU.add, [psm, cols], [modT])
                        S.stt(der.t[:, s, 0, :], modT.t[:, s, 16:32], 1.0, col("norm_mix_g", 0, 16), ALU.add, ALU.mult, [modT, cols], [der])
                        S.stt(der.t[:, s, 1, :], modT.t[:, s, 64:80], 1.0, col("norm_ffn_g", 0, 16), ALU.add, ALU.mult, [modT, cols], [der])

                _ck(1)
                for s in range(NSEQ):
                    sh1 = lambda k: modT.t[:, s, 0 + k:1 + k]
                    gt1 = lambda k: modT.t[:, s, 32 + k:33 + k]
                    sh2 = lambda k: modT.t[:, s, 48 + k:49 + k]
                    gt2 = lambda k: modT.t[:, s, 80 + k:81 + k]
                    A1 = lambda k: der.t[:, s, 0, k:k + 1]
                    A2 = lambda k: der.t[:, s, 1, k:k + 1]
                    xin_d, xin_B = cur_d[s], cur_B[s]

                    def norm_block(ph_tiles, src_d, src_B, t0, n, Acol, shcol):
                        xt, hT, sq, rstd, tmp = ph_tiles
                        S.dma('sp', xt.t[:, :, 0:n], src_d[:, t0:t0 + n].rearrange("(k p) t -> p k t", p=128), [src_B], [xt])
                        pss = S.rot('nps', [PS[5], PS[6]])
                        for k in range(KD):
                            sqk = S.rot('sq', sq)
                            S.act(sqk.t[:, 0:n], xt.t[:, k, 0:n], AF.Square, [xt], [sqk])
                            S.mm(pss.t[:, 0:n], [(onesb.t[:], sqk.t[:, 0:n])], [onesb, sqk], [pss], start=(k == 0), stop=(k == KD - 1))
                        S.rsqrt(rstd.t[:, 0:n], pss.t[:, 0:n], 1.0 / D, epsc.t[:, 0:1], [pss, epsc], [rstd])
                        for k in range(KD):
                            tk = S.rot('ntmp', tmp)
                            S.tt(tk.t[:, 0:n], xt.t[:, k, 0:n], rstd.t[:, 0:n], ALU.mult, [xt, rstd], [tk])
                            S.act(hT.t[:, k, 0:n], tk.t[:, 0:n], AF.Identity, [tk, modT, der], [hT], scale=Acol(k), bias=shcol(k))

                    with ExitStack() as mla:
                        S.barrier()
                        qdnT = tile(mla, "qdnT", [128, 4, S_], BF16)
                        kvnT = tile(mla, "kvnT", [128, 2, S_], BF16)
                        krT = tile(mla, "krT", [128, S_], BF16)
                        S.op('dve', lambda: nc.vector.memset(krT.t[64:128, :], 0.0), [], [krT])
                        with ExitStack() as ph:
                            S.barrier()
                            xt = tile(ph, "a_xt", [128, KD, 512])
                            hT = tile(ph, "a_hT", [128, KD, 512], BF16)
                            sq = [tile(ph, f"a_sq{i}", [128, 512], BF16) for i in range(2)]
                            rstd = tile(ph, "a_rstd", [128, 512])
                            tmp = [tile(ph, f"a_tmp{i}", [128, 512]) for i in range(2)]
                            wsl = [tile(ph, f"a_w{i}", [128, KD, 512], BF16) for i in range(2)]
                            stg = [tile(ph, f"a_stg{i}", [128, 4, 512]) for i in range(2)]
                            qd = tile(ph, "a_qd", [128, 6, 512])
                            sqf = [tile(ph, f"a_sqf{i}", [128, 512]) for i in range(2)]
                            kr = tile(ph, "a_kr", [64, 512])
                            krs = tile(ph, "a_krs", [64, 512])
                            cs = tile(ph, "a_cs", [64, 2, 512])
                            rq = tile(ph, "a_rq", [128, 2, 512])
                            slabsA = _in_chunks()
                            for blk in range(NB):
                                t0 = blk * 512
                                norm_block((xt, hT, sq, rstd, tmp), xin_d, xin_B, t0, 512, A1, sh1)
                                S.dma('sp', cs.t[:, 0, :], ropec[:, t0:t0 + 512], [B_in], [cs])
                                S.dma('sp', cs.t[:, 1, :], ropes[:, t0:t0 + 512], [B_in], [cs])
                                for si, slab in enumerate(slabsA):
                                    c0 = slab[0][1]
                                    c1 = slab[-1][1] + slab[-1][2]
                                    w = wsl[si % 2]
                                    S.dma('pool', w.t[:, :, 0:c1 - c0], WB[("w_in", l)][0][si, :, :, 0:c1 - c0], [WB[("w_in", l)][1][si]], [w])
                                    st = None
                                    nst = 0
                                    for (kind, cc, cw, idx) in slab:
                                        ps = S.rot('aps', PS[0:4])
                                        S.mm(ps.t[0:cw, :], [(w.t[:, k, cc - c0:cc - c0 + cw], hT.t[:, k, :]) for k in range(KD)], [w, hT], [ps])
                                        if kind == "q":
                                            S.cp(qd.t[:, idx, :], ps.t[:, :], [ps], [qd])
                                        elif kind == "kv":
                                            S.cp(qd.t[:, 4 + idx, :], ps.t[:, :], [ps], [qd])
                                        elif kind == "rope":
                                            S.cp(kr.t[:, :], ps.t[0:64, :], [ps], [kr], e='act')
                                            ps2 = S.rot('aps', PS[0:4])
                                            S.mm(ps2.t[0:64, :], [(swapm, kr.t[:, :])], [cst, kr], [ps2])
                                            S.tt(krs.t[:], ps2.t[0:64, :], cs.t[:, 1, :], ALU.mult, [ps2, cs], [krs])
                                            S.tt(kr.t[:], kr.t[:], cs.t[:, 0, :], ALU.mult, [kr, cs], [kr])
                                            S.tt(krT.t[0:64, t0:t0 + 512], kr.t[:], krs.t[:], ALU.add, [kr, krs], [krT])
                                        else:
                                            if st is None:
                                                st = S.rot('astg', stg)
                                                nst = 0
                                                st_first = (kind, idx)
                                            if kind == "gate":
                                                S.act(st.t[0:cw, nst, :], ps.t[0:cw, :], AF.Sigmoid, [ps], [st])
                                            else:
                                                S.cp(st.t[0:cw, nst, :], ps.t[0:cw, :], [ps], [st])
                                            nst += 1
                                    if st is not None:
                                        kind0, i0 = st_first
                                        dd, dB = (rw_d[s], B_rw[s]) if kind0 == "rw" else (gate_d[s], B_gate[s])
                                        nfull = nst - 1 if slab[-1][2] == 32 else nst
                                        if nfull > 0:
                                            S.dma('sp', dd[i0 * 128:(i0 + nfull) * 128, t0:t0 + 512].rearrange("(a p) t -> p a t", p=128),
                                                  st.t[:, 0:nfull, :], [st], [dB])
                                        if nfull < nst:
                                            S.dma('sp', dd[(i0 + nfull) * 128:(i0 + nfull) * 128 + 32, t0:t0 + 512], st.t[0:32, nfull, :], [st], [dB])
                                for (j0, nj, gname, dst, ps) in ((0, 4, "q_norm_g", qdnT, PS[4]), (4, 2, "kv_norm_g", kvnT, PS[4])):
                                    for j in range(nj):
                                        sqk = S.rot('sqf', sqf)
                                        S.act(sqk.t[:], qd.t[:, j0 + j, :], AF.Square, [qd], [sqk])
                                        S.mm(ps.t[:, :], [(onesf.t[:], sqk.t[:])], [onesf, sqk], [ps], start=(j == 0), stop=(j == nj - 1))
                                    rr = rq.t[:, 0, :]
                                    S.rsqrt(rr, ps.t[:, :], 1.0 / (128 * nj), epsc.t[:, 0:1], [ps, epsc], [rq])
                                    for j in range(nj):
                                        S.stt(dst.t[:, j, t0:t0 + 512], qd.t[:, j0 + j, :], col(gname, j), rr, ALU.mult, ALU.mult, [qd, cols, rq], [dst])

                        _ck(2)
                        with ExitStack() as ph:
                            S.barrier()
                            oT = tile(ph, "b_oT", [128, NH, S_], BF16)
                            with ExitStack() as ph2:
                                S.barrier()
                                wuq = tile(ph2, "b_wuq", [128, 4, 1536], BF16)
                                wukv = tile(ph2, "b_wukv", [128, 2, 2048], BF16)
                                S.dma('sp', wuq.t[:], WB[("w_uq", l)][0].rearrange("(k p) c -> p k c", p=128), WB[("w_uq", l)][1], [wuq])
                                S.dma('sp', wukv.t[:], WB[("w_ukv", l)][0].rearrange("(k p) c -> p k c", p=128), WB[("w_ukv", l)][1], [wukv])
                                knT = tile(ph2, "b_knT", [128, S_], BF16)
                                Vt = tile(ph2, "b_Vt", [128, NT, 128], BF16)
                                qnT = tile(ph2, "b_qnT", [128, S_], BF16)
                                qrT = tile(ph2, "b_qrT", [128, S_], BF16)
                                S.op('dve', lambda: nc.vector.memset(qrT.t[64:128, :], 0.0), [], [qrT])
                                qr = tile(ph2, "b_qr", [64, 512])
                                qrs = tile(ph2, "b_qrs", [64, 512])
                                cs = tile(ph2, "b_cs", [64, 2, 512])
                                pT = [tile(ph2, f"b_pT{i}", [128, 512], BF16) for i in range(3)]
                                rec = tile(ph2, "b_rec", [128, 512])
                                scale = 192.0 ** -0.5
                                for h in range(NH):
                                    for blk in range(NB):
                                        t0 = blk * 512
                                        ps = S.rot('bps', PS[0:3])
                                        S.mm(ps.t[:, :], [(wukv.t[:, r, h * 256:h * 256 + 128], kvnT.t[:, r, t0:t0 + 512]) for r in range(2)], [wukv, kvnT], [ps])
                                        S.cp(knT.t[:, t0:t0 + 512], ps.t[:, :], [ps], [knT])
                                        ps = S.rot('bps', PS[0:3])
                                        for q in range(4):
                                            tt_ = blk * 4 + q
                                            S.mm(ps.t[:, q * 128:(q + 1) * 128], [(kvnT.t[:, r, tt_ * 128:(tt_ + 1) * 128], wukv.t[:, r, h * 256 + 128:h * 256 + 256]) for r in range(2)], [wukv, kvnT], [ps])
                                        S.cp(Vt.t[:, blk * 4:blk * 4 + 4, :], ps.t[:, :].rearrange("p (a b) -> p a b", b=128), [ps], [Vt])
                                        ps = S.rot('bps', PS[0:3])
                                        S.mm(ps.t[:, :], [(wuq.t[:, r, h * 192:h * 192 + 128], qdnT.t[:, r, t0:t0 + 512]) for r in range(4)], [wuq, qdnT], [ps])
                                        S.cp(qnT.t[:, t0:t0 + 512], ps.t[:, :], [ps], [qnT])
                                        ps = S.rot('bps', PS[0:3])
                                        S.mm(ps.t[0:64, :], [(wuq.t[:, r, h * 192 + 128:h * 192 + 192], qdnT.t[:, r, t0:t0 + 512]) for r in range(4)], [wuq, qdnT], [ps])
                                        S.cp(qr.t[:, :], ps.t[0:64, :], [ps], [qr], e='act')
                                        S.dma('sp', cs.t[:, 0, :], ropec[:, t0:t0 + 512], [B_in], [cs])
                                        S.dma('sp', cs.t[:, 1, :], ropes[:, t0:t0 + 512], [B_in], [cs])
                                        ps2 = S.rot('bps', PS[0:3])
                                        S.mm(ps2.t[0:64, :], [(swapm, qr.t[:, :])], [cst, qr], [ps2])
                                        S.tt(qrs.t[:], ps2.t[0:64, :], cs.t[:, 1, :], ALU.mult, [ps2, cs], [qrs])
                                        S.tt(qr.t[:], qr.t[:], cs.t[:, 0, :], ALU.mult, [qr, cs], [qr])
                                        S.tt(qrT.t[0:64, t0:t0 + 512], qr.t[:], qrs.t[:], ALU.add, [qr, qrs], [qrT])
                                    for qb in range(NB):
                                        q0 = qb * 512
                                        psO, psD = PS[3], PS[4]
                                        def score(kt_):
                                            k0 = kt_ * 128
                                            ps_ = S.rot('sps', [PS[5], PS[6], PS[0]])
                                            S.mm(ps_.t[:, :], [(knT.t[:, k0:k0 + 128], qnT.t[:, q0:q0 + 512]), (krT.t[:, k0:k0 + 128], qrT.t[:, q0:q0 + 512])], [knT, qnT, krT, qrT], [ps_])
                                            return ps_
                                        ps_next = score(0)
                                        for kt in range(NT):
                                            ps = ps_next
                                            if kt + 1 < NT:
                                                ps_next = score(kt + 1)
                                            p = S.rot('pT', pT)
                                            S.act(p.t[:], ps.t[:, :], AF.Exp, [ps], [p], scale=scale)
                                            S.mm(psO.t[:, :], [(Vt.t[:, kt, :], p.t[:])], [Vt, p], [psO], start=(kt == 0), stop=(kt == NT - 1))
                                            S.mm(psD.t[:, :], [(onesb.t[:], p.t[:])], [onesb, p], [psD], start=(kt == 0), stop=(kt == NT - 1))
                                        S.op('dve', lambda: nc.vector.reciprocal(out=rec.t[:], in_=psD.t[:, :]), [psD], [rec])
                                        S.tt(oT.t[:, h, q0:q0 + 512], psO.t[:, :], rec.t[:], ALU.mult, [psO, rec], [oT])
                            _ck(3)
                            with ExitStack() as ph2:
                                S.barrier()
                                woa = tile(ph2, "b_woa", [128, NH, D], BF16)
                                S.dma('sp', woa.t[:], WB[("w_o_att", l)][0].rearrange("(k p) c -> p k c", p=128), WB[("w_o_att", l)][1], [woa])
                                gts = [tile(ph2, f"b_g{i}", [128, 4, 512]) for i in range(2)]
                                ost = [tile(ph2, f"b_o{i}", [128, 4, 512]) for i in range(2)]
                                for blk in range(NB):
                                    t0 = blk * 512
                                    for dg in range(4):
                                        g = S.rot('bg', gts)
                                        o = S.rot('bo', ost)
                                        S.dma('sp', g.t[:], gate_d[s][dg * 512:(dg + 1) * 512, t0:t0 + 512].rearrange("(a p) t -> p a t", p=128), [B_gate[s]], [g])
                                        for a in range(4):
                                            dc = dg * 4 + a
                                            ps = S.rot('bps', PS[0:3])
                                            S.mm(ps.t[:, :], [(woa.t[:, h, dc * 128:(dc + 1) * 128], oT.t[:, h, t0:t0 + 512]) for h in range(NH)], [woa, oT], [ps])
                                            S.tt(o.t[:, a, :], ps.t[:, :], g.t[:, a, :], ALU.mult, [ps, g], [o])
                                        S.dma('sp', ga_d[s][dg * 512:(dg + 1) * 512, t0:t0 + 512].rearrange("(a p) t -> p a t", p=128), o.t[:], [o], [B_ga[s]])

                    _ck(4)
                    with ExitStack() as ph:
                        S.barrier()
                        pin = [tile(ph, f"c0_in{i}", [128, S_ + 2]) for i in range(2)]
                        pout = [tile(ph, f"c0_out{i}", [128, S_]) for i in range(2)]
                        cm = tile(ph, "c0_cm", [128, 28])
                        S.tt(cm.t[:], col("mu0", 0, 28), col("mu1", 0, 28), ALU.add, [cols], [cm])
                        S.ts(cm.t[:], cm.t[:], -1.0, 1.0, ALU.mult, ALU.add, [cm], [cm])
                        for i in range(2):
                            S.op('dve', lambda: nc.vector.memset(pin[i].t[:, 0:1], 0.0), [], [pin[i]])
                            S.op('dve', lambda: nc.vector.memset(pin[i].t[:, S_ + 1:S_ + 2], 0.0), [], [pin[i]])
                        for j in range(28):
                            np_ = 128 if j < 27 else 32
                            a, o = pin[j % 2], pout[j % 2]
                            S.dma('sp', a.t[0:np_, 1:S_ + 1], rw_d[s][j * 128:j * 128 + np_, :], [B_rw[s]], [a])
                            S.ts(o.t[0:np_, :], a.t[0:np_, 1:S_ + 1], cm.t[0:np_, j:j + 1], None, ALU.mult, None, [a, cm], [o])
                            S.stt(o.t[0:np_, :], a.t[0:np_, 0:S_], col("mu0", j)[0:np_], o.t[0:np_, :], ALU.mult, ALU.add, [a, cols, o], [o])
                            S.stt(o.t[0:np_, :], a.t[0:np_, 2:S_ + 2], col("mu1", j)[0:np_], o.t[0:np_, :], ALU.mult, ALU.add, [a, cols, o], [o])
                            S.dma('sp', rwm_d[s][j * 128:j * 128 + np_, :], o.t[0:np_, :], [o], [B_rwm[s]])

                    _ck(5)
                    with ExitStack() as ph:
                        S.barrier()
                        if s == NSEQ - 1 and l + 1 < L:
                            emit_casts(l + 1)
                        wdu_t = tile(ph, "c_wdu", [128, 1024])
                        wiu_t = tile(ph, "c_wiu", [128, 1024])
                        S.dma('sp', wdu_t.t[:], wdu[l], [B_in], [wdu_t])
                        S.dma('sp', wiu_t.t[:], wiu[l], [B_in], [wiu_t])
                        X = [tile(ph, f"c_x{i}", [128, 8, 128]) for i in range(14)]
                        AR8 = tile(ph, "c_AR8", [128, 8, 2, 128])
                        AM = tile(ph, "c_AM", [128, 16, 384])
                        NP = [tile(ph, f"c_NP{i}", [128, 16, 128]) for i in range(2)]
                        GP = [tile(ph, f"c_GP{i}", [128, 16, 128]) for i in range(2)]
                        Y = tile(ph, "c_Y", [128, 16, 128])
                        WTin = tile(ph, "c_WTin", [128, 16, 64])
                        WT8 = tile(ph, "c_WT8", [128, 8, 128])
                        MT = tile(ph, "c_MT", [128, 16, 64])
                        Vtm = tile(ph, "c_Vtm", [128, 8, 128])
                        BHp = tile(ph, "c_BHp", [128, 16, 128])
                        KHp = tile(ph, "c_KHp", [128, 16, 128])
                        OT = tile(ph, "c_OT", [128, 1040])
                        wda = tile(ph, "c_wda", [128, 2, 128])
                        twd = tile(ph, "c_twd", [128, 128])
                        adm = tile(ph, "c_adm", [128, 128])
                        stm = [tile(ph, f"c_stm{i}", [128, 8, 64]) for i in range(2)]
                        PCt = tile(ph, "c_PC", [128, 8])
                        ST = [tile(ph, f"c_ST{d}", [128, 8, 64]) for d in range(2)]
                        stmp = tile(ph, "c_stmp", [128, 8, 64])
                        gB = {n: [Buf() for _ in range(4)] for n in ("AM", "N0", "N1", "G0", "G1", "Y")}
                        S.op('dve', lambda: nc.vector.memset(BHp.t[:], 0.0), [], [BHp])
                        S.op('dve', lambda: nc.vector.memset(KHp.t[:], 0.0), [], [KHp])
                        for d in range(2):
                            S.op('dve', lambda: nc.vector.memset(ST[d].t[:], 0.0), [], [ST[d]])
                        (r8, k8, v8, lw8, a8, L8, eLx, eLi, eNL, eCL, kk8, t8, b8, u8) = X
                        bc8 = lambda ap: ap.unsqueeze(2).broadcast_to([128, 8, 128])

                        def chunk_step(d, c):
                            t0 = c * C
                            rows = lambda base: rwm_d[s][base:base + 1024, t0:t0 + C].rearrange("(a p) t -> p a t", p=128)
                            S.dma('sp', r8.t[:], rows(0), [B_rwm[s]], [r8])
                            S.dma('sp', k8.t[:], rows(1024), [B_rwm[s]], [k8])
                            S.dma('sp', v8.t[:], rows(2048), [B_rwm[s]], [v8])
                            S.dma('sp', wda.t[:], rwm_d[s][3072:3328, t0:t0 + C].rearrange("(a p) t -> p a t", p=128), [B_rwm[s]], [wda])
                            S.act(twd.t[:], wda.t[:, 0, :], AF.Tanh, [wda], [twd])
                            S.ts(twd.t[:], twd.t[:], hsel[:, d:d + 1], None, ALU.mult, None, [twd, cst], [twd])
                            S.ts(adm.t[:], wda.t[:, 1, :], hsel[:, d:d + 1], None, ALU.mult, None, [wda, cst], [adm])
                            for (wt, src, bname, dst) in ((wdu_t, twd.t, f"w0_{d}", lw8), (wiu_t, None, f"a0_{d}", a8)):
                                for g2 in range(2):
                                    ps = S.rot('cps', PS[0:4])
                                    for q in range(4):
                                        hp = g2 * 4 + q
                                        rhs = twd.t[:, :] if src is not None else adm.t[:, :]
                                        S.mm(ps.t[:, q * 128:(q + 1) * 128], [(wt.t[:, hp * 128:(hp + 1) * 128], rhs)], [wt, twd, adm], [ps])
                                    for q in range(4):
                                        hp = g2 * 4 + q
                                        S.act(dst.t[:, hp, :], ps.t[:, q * 128:(q + 1) * 128], AF.Sigmoid, [ps, cols], [dst], bias=col(bname, hp))
                            _ck(5.1)
                            S.ts(lw8.t[:], lw8.t[:], -0.6065306597126334, None, ALU.mult, None, [lw8], [lw8])
                            for hp in range(8):
                                S.op('dve', lambda: nc.vector.tensor_tensor_scan(out=L8.t[:, hp, :], data0=lw8.t[:, hp, :], data1=lw8.t[:, hp, :], initial=0.0, op0=ALU.add, op1=ALU.bypass), [lw8], [L8])
                            LCb = L8.t[:, :, 127:128].broadcast_to([128, 8, 128])
                            S.act(PCt.t[:], L8.t[:, :, 127], AF.Exp, [L8], [PCt])
                            if d == 0:
                                S.tt(eLx.t[:], L8.t[:], lw8.t[:], ALU.subtract, [L8, lw8], [eLx])
                                S.tt(eCL.t[:], LCb, L8.t[:], ALU.subtract, [L8], [eCL])
                                S.act(eLi.t[:], L8.t[:], AF.Exp, [L8], [eLi])
                                S.act(eNL.t[:], L8.t[:], AF.Exp, [L8], [eNL], scale=-1.0)
                            else:
                                S.tt(eLx.t[:], LCb, L8.t[:], ALU.subtract, [L8], [eLx])
                                S.tt(eLi.t[:], eLx.t[:], lw8.t[:], ALU.add, [eLx, lw8], [eLi])
                                S.tt(eCL.t[:], LCb, eLi.t[:], ALU.subtract, [L8, eLi], [eCL])
                                S.act(eNL.t[:], eLi.t[:], AF.Exp, [eLi], [eNL], scale=-1.0)
                                S.act(eLi.t[:], eLi.t[:], AF.Exp, [eLi], [eLi])
                            S.act(eLx.t[:], eLx.t[:], AF.Exp, [eLx], [eLx])
                            S.act(eCL.t[:], eCL.t[:], AF.Exp, [eCL], [eCL])
                            _ck(5.2)
                            S.tt(kk8.t[:], k8.t[:], bc8(col("k_k", 0, 8)), ALU.mult, [k8, cols], [kk8])
                            S.tt(u8.t[:], kk8.t[:], kk8.t[:], ALU.mult, [kk8], [u8])
                            for g2 in range(2):
                                ps = S.rot('cps', PS[0:4])
                                for q in range(4):
                                    S.mm(ps.t[:, q * 128:(q + 1) * 128], [(bd_ones, u8.t[:, g2 * 4 + q, :])], [cst, u8], [ps])
                                S.ts(L8.t[:, g2 * 4:g2 * 4 + 4, :], ps.t[:, :].rearrange("p (a b) -> p a b", b=128), 1e-24, None, ALU.max, None, [ps], [L8])
                            S.act(L8.t[:], L8.t[:], AF.Sqrt, [L8], [L8])
                            S.op('dve', lambda: nc.vector.reciprocal(out=L8.t[:], in_=L8.t[:]), [L8], [L8])
                            S.tt(kk8.t[:], kk8.t[:], L8.t[:], ALU.mult, [kk8, L8], [kk8])
                            S.tt(t8.t[:], a8.t[:], bc8(col("k_a", 0, 8)), ALU.mult, [a8, cols], [t8])
                            S.tt(t8.t[:], t8.t[:], bc8(omka.t[:, 0:8]), ALU.add, [t8, omka], [t8])
                            S.tt(t8.t[:], t8.t[:], k8.t[:], ALU.mult, [t8, k8], [t8])
                            S.tt(b8.t[:], kk8.t[:], a8.t[:], ALU.mult, [kk8, a8], [b8])
                            S.stt(AR8.t[:, :, 0, :], kk8.t[:], -1.0, eLx.t[:], ALU.mult, ALU.mult, [kk8, eLx], [AR8])
                            S.tt(AR8.t[:, :, 1, :], r8.t[:], eLi.t[:], ALU.mult, [r8, eLi], [AR8])
                            S.tt(u8.t[:], r8.t[:], t8.t[:], ALU.mult, [r8, t8], [u8])
                            S.tt(u8.t[:], u8.t[:], bc8(col(f"r_k_{d}", 0, 8)), ALU.mult, [u8, cols], [u8])
                            psB = S.rot('cps', PS[0:4])
                            for hp in range(8):
                                S.mm(psB.t[:, hp * 2:hp * 2 + 2], [(u8.t[:, hp, :], hsel)], [u8, cst], [psB])
                            S.cp(OT.t[:, 1024:1040], psB.t[:, 0:16], [psB], [OT], e='act')
                            bt8, kt8, bh8, kh8 = a8, lw8, eLx, eLi
                            S.tt(bh8.t[:], b8.t[:], eCL.t[:], ALU.mult, [b8, eCL], [bh8])
                            S.tt(kh8.t[:], t8.t[:], eCL.t[:], ALU.mult, [t8, eCL], [kh8])
                            S.tt(bt8.t[:], b8.t[:], eNL.t[:], ALU.mult, [b8, eNL], [bt8])
                            S.tt(kt8.t[:], t8.t[:], eNL.t[:], ALU.mult, [t8, eNL], [kt8])
                            _ck(5.3)
                            for g2 in range(2):
                                ps = S.rot('cps', PS[0:4])
                                for q in range(4):
                                    S.tr(ps.t[:, q * 128:(q + 1) * 128], v8.t[:, g2 * 4 + q, :], ident, [v8, cst], [ps])
                                S.cp(Vtm.t[:, g2 * 4:g2 * 4 + 4, :], ps.t[:, :].rearrange("p (a b) -> p a b", b=128), [ps], [Vtm])
                                ps = S.rot('cps', PS[0:4])
                                for q in range(4):
                                    S.tr(ps.t[:, q * 128:(q + 1) * 128], AR8.t[:, g2 * 4 + q, 0, :], ident, [AR8, cst], [ps])
                                S.cp(Y.t[:, g2 * 8:g2 * 8 + 8, 0:64], ps.t[:, :].rearrange("p (a b) -> p a b", b=64), [ps], [gB["Y"][2 * g2], gB["Y"][2 * g2 + 1]])
                                for (src, dstp) in ((bh8, BHp), (kh8, KHp)):
                                    ps = S.rot('cps', PS[0:4])
                                    for q in range(4):
                                        S.tr(ps.t[:, q * 128:(q + 1) * 128], src.t[:, g2 * 4 + q, :], ident, [src, cst], [ps])
                                    pv4 = ps.t[:, :].rearrange("p (a h b) -> p a h b", h=2, b=64)
                                    dv4 = dstp.t[:, g2 * 8:g2 * 8 + 8, :].rearrange("p (a h) c -> p a h c", h=2)
                                    S.cp(dv4[:, :, 0, 0:64], pv4[:, :, 0, :], [ps], [dstp])
                                    S.cp(dv4[:, :, 1, 64:128], pv4[:, :, 1, :], [ps], [dstp])
                            if d == 0:
                                S.dma('sp', vtm_d[s][t0:t0 + C, :], Vtm.t[:].rearrange("p a b -> p (a b)"), [Vtm], [B_vtm[s]])
                            _ck(5.4)
                            am = [r8, k8]
                            rm = [L8, eNL]
                            for hh in range(2):
                                S.ts(am[hh].t[:], AR8.t[:, :, 0, :], hsel[:, hh:hh + 1], None, ALU.mult, None, [AR8, cst], [am[hh]])
                                S.ts(rm[hh].t[:], AR8.t[:, :, 1, :], hsel[:, hh:hh + 1], None, ALU.mult, None, [AR8, cst], [rm[hh]])
                            Aak = GP[1]
                            for h in range(16):
                                hp, hh = h // 2, h % 2
                                sl = slice(64 * hh, 64 * hh + 64)
                                g4 = h // 4
                                ps = S.rot('cps', PS[0:4])
                                S.mm(ps.t[:, 0:128], [(bt8.t[:, hp, :], am[hh].t[:, hp, :])], [bt8, am[hh]], [ps])
                                S.mm(ps.t[:, 128:256], [(bt8.t[:, hp, :], rm[hh].t[:, hp, :])], [bt8, rm[hh]], [ps])
                                S.mm(ps.t[:, 256:384], [(kt8.t[:, hp, :], am[hh].t[:, hp, :])], [kt8, am[hh]], [ps])
                                S.mm(ps.t[:, 384:512], [(kt8.t[:, hp, :], rm[hh].t[:, hp, :])], [kt8, rm[hh]], [ps])
                                S.tt(AM.t[:, h, 0:256], ps.t[:, 0:256], mA[d].t[:], ALU.mult, [ps, mA[d]], [gB["AM"][g4]])
                                S.tt(Aak.t[:, h, :], ps.t[:, 256:384], mA[d].t[:, 0:128], ALU.mult, [ps, mA[d]], [gB["G1"][g4]])
                                S.tt(AM.t[:, h, 256:384], ps.t[:, 384:512], mA[d].t[:, 128:256], ALU.mult, [ps, mA[d]], [gB["AM"][g4]])
                            for g4 in range(4):
                                ps = S.rot('cps', PS[0:4])
                                for q in range(4):
                                    h = g4 * 4 + q
                                    hp, hh = h // 2, h % 2
                                    sl = slice(64 * hh, 64 * hh + 64)
                                    S.mm(ps.t[:, q * 128:(q + 1) * 128], [(am[hh].t[:, hp, :], bt8.t[:, hp, :])], [am[hh], bt8], [ps])
                                S.tt(NP[0].t[:, g4 * 4:g4 * 4 + 4, :], ps.t[:, :].rearrange("p (a b) -> p a b", b=128), mN[d].t[:].rearrange("p (a b) -> p a b", b=128), ALU.mult, [ps, mN[d]], [gB["N0"][g4]])
                            for g8 in range(2):
                                ps = S.rot('cps', PS[0:4])
                                for q in range(8):
                                    h = g8 * 8 + q
                                    hp, hh = h // 2, h % 2
                                    S.mm(ps.t[:, q * 64:(q + 1) * 64], [(Aak.t[:, h, :], Vtm.t[:, hp, 64 * hh:64 * hh + 64])], [gB["G1"][h // 4], Vtm], [ps])
                                S.cp(Y.t[:, g8 * 8:g8 * 8 + 8, 64:128], ps.t[:, :].rearrange("p (a b) -> p a b", b=64), [ps], [gB["Y"][2 * g8], gB["Y"][2 * g8 + 1]])
                            _ck(5.5)
                            for p_ in range(7):
                                if p_ == 0:
                                    Gt, Gb = (lambda h: AM.t[:, h, 0:128]), gB["AM"]
                                else:
                                    gi = (p_ - 1) % 2
                                    Gt, Gb = (lambda h, gi=gi: GP[gi].t[:, h, :]), gB[f"G{gi}"]
                                ni = p_ % 2
                                Nt, Nb = (lambda h, ni=ni: NP[ni].t[:, h, :]), gB[f"N{ni}"]
                                for g4 in range(4):
                                    ps = S.rot('cps', PS[0:4])
                                    for q in range(4):
                                        h = g4 * 4 + q
                                        S.mm(ps.t[:, q * 128:(q + 1) * 128], [(Gt(h), Y.t[:, h, :])], [Gb[g4], gB["Y"][g4]], [ps])
                                    S.tt(Y.t[:, g4 * 4:g4 * 4 + 4, :], ps.t[:, :].rearrange("p (a b) -> p a b", b=128), Y.t[:, g4 * 4:g4 * 4 + 4, :], ALU.add, [ps, gB["Y"][g4]], [gB["Y"][g4]])
                                if p_ == 6:
                                    break
                                go = p_ % 2
                                no = (p_ + 1) % 2
                                for g4 in range(4):
                                    ps = S.rot('cps', PS[0:4])
                                    for q in range(4):
                                        h = g4 * 4 + q
                                        S.mm(ps.t[:, q * 128:(q + 1) * 128], [(Nt(h), Gt(h))], [Gb[g4], Nb[g4]], [ps])
                                    S.cp(GP[go].t[:, g4 * 4:g4 * 4 + 4, :], ps.t[:, :].rearrange("p (a b) -> p a b", b=128), [ps], [gB[f"G{go}"][g4]])
                                    if p_ < 5:
                                        ps = S.rot('cps', PS[0:4])
                                        for q in range(4):
                                            h = g4 * 4 + q
                                            S.mm(ps.t[:, q * 128:(q + 1) * 128], [(Gt(h), Nt(h))], [Gb[g4], Nb[g4]], [ps])
                                        S.cp(NP[no].t[:, g4 * 4:g4 * 4 + 4, :], ps.t[:, :].rearrange("p (a b) -> p a b", b=128), [ps], [gB[f"N{no}"][g4]])
                            _ck(5.6)
                            S.cp(WTin.t[:], Y.t[:, :, 0:64], gB["Y"], [WTin], e='dve')
                            for g2 in range(2):
                                ps = S.rot('cps', PS[0:4])
                                for q in range(4):
                                    hp = g2 * 4 + q
                                    S.tr(ps.t[:, q * 128:(q + 1) * 128], WTin.t[:, 2 * hp:2 * hp + 2, :].rearrange("p a b -> p (a b)"), ident, [WTin, cst], [ps])
                                S.cp(WT8.t[:, g2 * 4:g2 * 4 + 4, :], ps.t[:, :].rearrange("p (a b) -> p a b", b=128), [ps], [WT8])
                            _ck(5.7)
                            st = ST[d]
                            for hh in range(2):
                                S.ts(stm[hh].t[:], st.t[:], hsel[:, hh:hh + 1], None, ALU.mult, None, [st, cst], [stm[hh]])
                            for g8 in range(2):
                                ps = PS[4 + g8]
                                for q in range(8):
                                    h = g8 * 8 + q
                                    hp, hh = h // 2, h % 2
                                    sl = slice(64 * hh, 64 * hh + 64)
                                    S.mm(ps.t[:, q * 64:(q + 1) * 64], [(WT8.t[:, hp, :], stm[hh].t[:, hp, :])], [WT8, stm[hh]], [ps])
                                S.tt(MT.t[:, g8 * 8:g8 * 8 + 8, :], ps.t[:, :].rearrange("p (a b) -> p a b", b=64), Y.t[:, g8 * 8:g8 * 8 + 8, 64:128], ALU.add, [ps, gB["Y"][2 * g8], gB["Y"][2 * g8 + 1]], [MT])
                            for g8 in range(2):
                                ps = PS[4 + g8]
                                for q in range(8):
                                    h = g8 * 8 + q
                                    hp, hh = h // 2, h % 2
                                    sl = slice(64 * hh, 64 * hh + 64)
                                    S.mm(ps.t[:, q * 64:(q + 1) * 64],
                                         [(rm[hh].t[:, hp, :], st.t[:, hp, :]), (AM.t[:, h, 128:256], MT.t[:, h, :]), (AM.t[:, h, 256:384], Vtm.t[:, hp, 64 * hh:64 * hh + 64])],
                                         [rm[hh], st, gB["AM"][h // 4], MT, Vtm], [ps])
                                S.cp(OT.t[:, g8 * 512:(g8 + 1) * 512], ps.t[:, :], [ps], [OT], e='act')
                            S.dma('sp', od_d[s][d][t0:t0 + C, :], OT.t[:], [OT], [B_od[s][d]])
                            ps = PS[6]
                            for hp in range(8):
                                S.mm(ps.t[:, hp * 64:(hp + 1) * 64],
                                     [(BHp.t[:, 2 * hp, :], MT.t[:, 2 * hp, :]), (KHp.t[:, 2 * hp, :], Vtm.t[:, hp, 0:64]),
                                      (BHp.t[:, 2 * hp + 1, :], MT.t[:, 2 * hp + 1, :]), (KHp.t[:, 2 * hp + 1, :], Vtm.t[:, hp, 64:128])],
                                     [BHp, KHp, MT, Vtm], [ps])
                            S.tt(stmp.t[:], st.t[:], PCt.t[:].unsqueeze(2).broadcast_to([128, 8, 64]), ALU.mult, [st, PCt], [stmp])
                            S.tt(st.t[:], stmp.t[:], ps.t[:, :].rearrange("p (a b) -> p a b", b=64), ALU.add, [stmp, ps], [st])

                        for i in range(NCH):
                            chunk_step(0, i)
                            chunk_step(1, NCH - 1 - i)

                    _ck(6)
                    with ExitStack() as ph:
                        S.barrier()
                        worw = tile(ph, "d_worw", [128, 8, D], BF16)
                        wout = tile(ph, "d_wout", [128, KD, D], BF16)
                        S.dma('sp', worw.t[:], WB[("w_o_rwkv", l)][0].rearrange("(k p) c -> p k c", p=128), WB[("w_o_rwkv", l)][1], [worw])
                        S.dma('sp', wout.t[:], WB[("w_out", l)][0].rearrange("(k p) c -> p k c", p=128), WB[("w_out", l)][1], [wout])
                        wgA = tile(ph, "d_wgA", [128, 1024])
                        wgB = tile(ph, "d_wgB", [32, 1024])
                        S.dma('sp', wgA.t[:], wgu[l, 0:128, :], [B_in], [wgA])
                        S.dma('sp', wgB.t[:], wgu[l, 128:160, :], [B_in], [wgB])
                        gng = tile(ph, "d_gng", [128, 1024])
                        gnb = tile(ph, "d_gnb", [128, 1024])
                        S.dma('sp', gng.t[:], gn_g[l].partition_broadcast(128), [B_in], [gng])
                        S.dma('sp', gnb.t[:], gn_b[l].partition_broadcast(128), [B_in], [gnb])
                        of = [tile(ph, f"d_of{i}", [128, 1040]) for i in range(2)]
                        ob = [tile(ph, f"d_ob{i}", [128, 1040]) for i in range(2)]
                        vt = [tile(ph, f"d_vt{i}", [128, 1024]) for i in range(2)]
                        gdA = tile(ph, "d_gdA", [128, 128])
                        gdB = tile(ph, "d_gdB", [32, 128])
                        o3 = tile(ph, "d_o3", [128, 16, 64])
                        sq3 = tile(ph, "d_sq3", [128, 16, 64])
                        st16 = tile(ph, "d_st16", [128, 4, 16])
                        ofin = tile(ph, "d_ofin", [128, 1024], BF16)
                        ofT = tile(ph, "d_ofT", [128, 8, 512], BF16)
                        miT = tile(ph, "d_miT", [128, KD, 512], BF16)
                        gl = [tile(ph, f"d_gl{i}", [128, 512]) for i in range(3)]
                        tm = [tile(ph, f"d_tm{i}", [128, 512]) for i in range(2)]
                        xo = [tile(ph, f"d_xo{i}", [128, 512]) for i in range(2)]
                        v3 = lambda t: t.t[:, 0:1024].rearrange("p (a b) -> p a b", b=64)
                        b16 = lambda ap: ap.unsqueeze(2).broadcast_to([128, 16, 64])
                        for blk in range(NB):
                            t0 = blk * 512
                            for q4 in range(4):
                                tk = t0 + q4 * 128
                                f_, b_, v_ = of[q4 % 2], ob[q4 % 2], vt[q4 % 2]
                                S.dma('sp', f_.t[:], od_d[s][0][tk:tk + 128, :], [B_od[s][0]], [f_])
                                S.dma('sp', b_.t[:], od_d[s][1][tk:tk + 128, :], [B_od[s][1]], [b_])
                                S.dma('sp', v_.t[:], vtm_d[s][tk:tk + 128, :], [B_vtm[s]], [v_])
                                S.dma('sp', gdA.t[:], rwm_d[s][3328:3456, tk:tk + 128], [B_rwm[s]], [gdA])
                                S.dma('sp', gdB.t[:], rwm_d[s][3456:3488, tk:tk + 128], [B_rwm[s]], [gdB])
                                S.act(gdA.t[:], gdA.t[:], AF.Sigmoid, [gdA], [gdA])
                                S.act(gdB.t[:], gdB.t[:], AF.Sigmoid, [gdB], [gdB])
                                S.tt(f_.t[:], f_.t[:], b_.t[:], ALU.add, [f_, b_], [f_])
                                S.op('dve', lambda: nc.vector.reduce_sum(out=st16.t[:, 0, :], in_=v3(f_), axis=AX.X), [f_], [st16])
                                S.ts(st16.t[:, 0, :], st16.t[:, 0, :], 1.0 / 64, None, ALU.mult, None, [st16], [st16])
                                S.tt(o3.t[:], v3(f_), b16(st16.t[:, 0, :]), ALU.subtract, [f_, st16], [o3])
                                S.tt(sq3.t[:], o3.t[:], o3.t[:], ALU.mult, [o3], [sq3])
                                S.op('dve', lambda: nc.vector.reduce_sum(out=st16.t[:, 1, :], in_=sq3.t[:], axis=AX.X), [sq3], [st16])
                                S.rsqrt(st16.t[:, 2, :], st16.t[:, 1, :], 1.0 / 64, epsc.t[:, 1:2], [st16, epsc], [st16])
                                S.tt(o3.t[:], o3.t[:], b16(st16.t[:, 2, :]), ALU.mult, [o3, st16], [o3])
                                o3f = o3.t[:].rearrange("p a b -> p (a b)")
                                S.tt(o3f, o3f, gng.t[:], ALU.mult, [o3, gng], [o3])
                                S.tt(o3f, o3f, gnb.t[:], ALU.add, [o3, gnb], [o3])
                                S.tt(sq3.t[:], v3(v_), b16(f_.t[:, 1024:1040]), ALU.mult, [v_, f_], [sq3])
                                S.tt(o3.t[:], o3.t[:], sq3.t[:], ALU.add, [o3, sq3], [o3])
                                for hf in range(2):
                                    ps = S.rot('dps', PS[0:4])
                                    S.mm(ps.t[:, :], [(gdA.t[:], wgA.t[:, hf * 512:(hf + 1) * 512]), (gdB.t[:], wgB.t[:, hf * 512:(hf + 1) * 512])], [gdA, gdB, wgA, wgB], [ps])
                                    S.tt(ofin.t[:, hf * 512:(hf + 1) * 512], o3f[:, hf * 512:(hf + 1) * 512], ps.t[:, :], ALU.mult, [o3, ps], [ofin])
                                for cc in range(8):
                                    S.tr(PSB.t[:, cc * 128:(cc + 1) * 128], ofin.t[:, cc * 128:(cc + 1) * 128], identb.t[:], [ofin, identb], [PSB])
                                S.cp(ofT.t[:, :, q4 * 128:(q4 + 1) * 128], PSB.t[:, :].rearrange("p (a b) -> p a b", b=128), [PSB], [ofT])
                            for dc in range(KD):
                                g1, g2_ = S.rot('dgl', gl), S.rot('dgl', gl)
                                S.dma('sp', g1.t[:], gate_d[s][2048 + dc * 128:2048 + (dc + 1) * 128, t0:t0 + 512], [B_gate[s]], [g1])
                                S.dma('sp', g2_.t[:], ga_d[s][dc * 128:(dc + 1) * 128, t0:t0 + 512], [B_ga[s]], [g2_])
                                ps = S.rot('dps', PS[0:4])
                                S.mm(ps.t[:, :], [(worw.t[:, cc, dc * 128:(dc + 1) * 128], ofT.t[:, cc, :]) for cc in range(8)], [worw, ofT], [ps])
                                t_ = S.rot('dtm', tm)
                                S.tt(t_.t[:], ps.t[:, :], g1.t[:], ALU.mult, [ps, g1], [t_])
                                S.tt(miT.t[:, dc, :], t_.t[:], g2_.t[:], ALU.add, [t_, g2_], [miT])
                            for dc in range(KD):
                                xi = S.rot('dgl', gl)
                                S.dma('sp', xi.t[:], xin_d[dc * 128:(dc + 1) * 128, t0:t0 + 512], [xin_B], [xi])
                                ps = S.rot('dps', PS[0:4])
                                S.mm(ps.t[:, :], [(wout.t[:, k, dc * 128:(dc + 1) * 128], miT.t[:, k, :]) for k in range(KD)], [wout, miT], [ps])
                                xo_ = S.rot('dxo', xo)
                                S.stt(xo_.t[:], ps.t[:, :], gt1(dc), xi.t[:], ALU.mult, ALU.add, [ps, modT, xi], [xo_])
                                S.dma('sp', x1_d[s][dc * 128:(dc + 1) * 128, t0:t0 + 512], xo_.t[:], [xo_], [B_x1[s]])

                    _ck(7)
                    with ExitStack() as ph:
                        S.barrier()
                        xt = tile(ph, "f_xt", [128, KD, 512])
                        hT = tile(ph, "f_hT", [128, KD, 512], BF16)
                        xh = tile(ph, "f_xh", [128, KD, 2])
                        hTh = tile(ph, "f_hTh", [128, KD, 2], BF16)
                        sq = [tile(ph, f"f_sq{i}", [128, 512], BF16) for i in range(3)]
                        rstd = tile(ph, "f_rstd", [128, 512])
                        rstdh = tile(ph, "f_rstdh", [128, 2])
                        tmp = [tile(ph, f"f_tmp{i}", [128, 512]) for i in range(3)]
                        wsl = [tile(ph, f"f_w{i}", [128, KD, 512], BF16) for i in range(3)]
                        wds = [tile(ph, f"f_wd{i}", [128, 11, 512], BF16) for i in range(2)]
                        uT = tile(ph, "f_uT", [128, 44, 512], BF16)
                        acc = [tile(ph, f"f_acc{i}", [128, 512]) for i in range(2)]
                        hal = [tile(ph, f"f_hal{i}", [128, 2]) for i in range(2)]
                        xo = [tile(ph, f"f_xo{i}", [128, 512]) for i in range(2)]
                        last = (l == L - 1)
                        for blk in range(NB):
                            t0 = blk * 512
                            norm_block((xt, hT, sq, rstd, tmp), x1_d[s], B_x1[s], t0, 512, A2, sh2)
                            tl, tr_ = max(t0 - 1, 0), min(t0 + 512, S_ - 1)
                            S.dma('sp', xh.t[:, :, 0:1], x1_d[s][:, tl:tl + 1].rearrange("(k p) t -> p k t", p=128), [B_x1[s]], [xh], allow_slow_non_contiguous=True)
                            S.dma('sp', xh.t[:, :, 1:2], x1_d[s][:, tr_:tr_ + 1].rearrange("(k p) t -> p k t", p=128), [B_x1[s]], [xh], allow_slow_non_contiguous=True)
                            pss = PS[6]
                            for k in range(KD):
                                sqk = S.rot('sq', sq)
                                S.act(sqk.t[:, 0:2], xh.t[:, k, :], AF.Square, [xh], [sqk])
                                S.mm(pss.t[:, 0:2], [(onesb.t[:], sqk.t[:, 0:2])], [onesb, sqk], [pss], start=(k == 0), stop=(k == KD - 1))
                            S.rsqrt(rstdh.t[:], pss.t[:, 0:2], 1.0 / D, epsc.t[:, 0:1], [pss, epsc], [rstdh])
                            for k in range(KD):
                                tk = S.rot('ntmp', tmp)
                                S.tt(tk.t[:, 0:2], xh.t[:, k, :], rstdh.t[:], ALU.mult, [xh, rstdh], [tk])
                                S.act(hTh.t[:, k, :], tk.t[:, 0:2], AF.Identity, [tk, modT, der], [hTh], scale=A2(k), bias=sh2(k))
                            if blk == 0:
                                S.op('dve', lambda: nc.vector.memset(hTh.t[:, :, 0:1], 0.0), [], [hTh])
                            if blk == NB - 1:
                                S.op('dve', lambda: nc.vector.memset(hTh.t[:, :, 1:2], 0.0), [], [hTh])
                            for sj in range(11):
                                wa, wb = S.rot('fw', wsl), S.rot('fw', wsl)
                                S.dma('pool', wa.t[:], WB[("w_up", l)][0][sj], [WB[("w_up", l)][1][sj]], [wa])
                                S.dma('pool', wb.t[:], WB[("w_up", l)][0][11 + sj], [WB[("w_up", l)][1][11 + sj]], [wb])
                                for a in range(4):
                                    j = sj * 4 + a
                                    psa = S.rot('fpa', PS[0:2])
                                    psh = PS[5]
                                    psb_ = S.rot('fpb', PS[2:4])
                                    S.mm(psa.t[:, :], [(wa.t[:, k, a * 128:(a + 1) * 128], hT.t[:, k, :]) for k in range(KD)], [wa, hT], [psa])
                                    S.mm(psh.t[:, 0:2], [(wa.t[:, k, a * 128:(a + 1) * 128], hTh.t[:, k, :]) for k in range(KD)], [wa, hTh], [psh])
                                    S.mm(psb_.t[:, :], [(wb.t[:, k, a * 128:(a + 1) * 128], hT.t[:, k, :]) for k in range(KD)], [wb, hT], [psb_])
                                    ac = S.rot('facc', acc)
                                    hl = S.rot('fhal', hal)
                                    S.cp(hl.t[:], psh.t[:, 0:2], [psh], [hl], e='act')
                                    S.act(ac.t[:], psa.t[:, :], AF.Identity, [psa, cols], [ac], scale=col("cw1", j), bias=col("cb", j))
                                    S.stt(ac.t[:, 1:512], psa.t[:, 0:511], col("cw0", j), ac.t[:, 1:512], ALU.mult, ALU.add, [psa, cols, ac], [ac])
                                    S.stt(ac.t[:, 0:511], psa.t[:, 1:512], col("cw2", j), ac.t[:, 0:511], ALU.mult, ALU.add, [psa, cols, ac], [ac])
                                    S.stt(ac.t[:, 0:1], hl.t[:, 0:1], col("cw0", j), ac.t[:, 0:1], ALU.mult, ALU.add, [hl, cols, ac], [ac])
                                    S.stt(ac.t[:, 511:512], hl.t[:, 1:2], col("cw2", j), ac.t[:, 511:512], ALU.mult, ALU.add, [hl, cols, ac], [ac])
                                    S.act(ac.t[:], ac.t[:], AF.Silu, [ac], [ac])
                                    S.tt(uT.t[:, j, :], ac.t[:], psb_.t[:, :], ALU.mult, [ac, psb_], [uT])
                            for dg in range(4):
                                pso = PS[0:4]
                                for rs in range(4):
                                    wd_ = S.rot('fwd', wds)
                                    S.dma('pool', wd_.t[:], WB[("w_down", l)][0][dg * 4 + rs], [WB[("w_down", l)][1][dg * 4 + rs]], [wd_])
                                    for a in range(11):
                                        for d4 in range(4):
                                            S.mm(pso[d4].t[:, :], [(wd_.t[:, a, d4 * 128:(d4 + 1) * 128], uT.t[:, rs * 11 + a, :])], [wd_, uT], [pso[d4]],
                                                 start=(rs == 0 and a == 0), stop=(rs == 3 and a == 10))
                                for d4 in range(4):
                                    dc = dg * 4 + d4
                                    xo_ = S.rot('fxo', xo)
                                    S.stt(xo_.t[:], pso[d4].t[:, :], gt2(dc), xt.t[:, dc, :], ALU.mult, ALU.add, [pso[d4], modT, xt], [xo_])
                                    S.dma('sp', x2_d[s][dc * 128:(dc + 1) * 128, t0:t0 + 512], xo_.t[:], [xo_], [B_x2[s]])
                        if last:
                            for blk in range(NB):
                                t0 = blk * 512
                                S.dma('sp', xt.t[:], x2_d[s][:, t0:t0 + 512].rearrange("(k p) t -> p k t", p=128), [B_x2[s]], [xt])
                                pss = S.rot('nps', [PS[5], PS[6]])
                                for k in range(KD):
                                    sqk = S.rot('sq', sq)
                                    S.act(sqk.t[:], xt.t[:, k, :], AF.Square, [xt], [sqk])
                                    S.mm(pss.t[:, :], [(onesb.t[:], sqk.t[:])], [onesb, sqk], [pss], start=(k == 0), stop=(k == KD - 1))
                                S.rsqrt(rstd.t[:], pss.t[:, :], 1.0 / D, epsc.t[:, 0:1], [pss, epsc], [rstd])
                                for k in range(KD):
                                    xo_ = S.rot('fxo', xo)
                                    S.stt(xo_.t[:], xt.t[:, k, :], col("final_norm_g", k), rstd.t[:], ALU.mult, ALU.mult, [xt, cols, rstd], [xo_])
                                    S.dma('sp', yT[s, k * 128:(k + 1) * 128, t0:t0 + 512], xo_.t[:], [xo_], [B_y])
                    cur_d[s], cur_B[s] = x2_d[s], B_x2[s]
          except _Stop:
            break
        S.finish()
    return nc


def _fm(v):
    v = np.asarray(v, np.float32).reshape(-1)
    n = (v.size + 127) // 128
    p = np.zeros(n * 128, np.float32)
    p[:v.size] = v
    return np.ascontiguousarray(p.reshape(n, 128).T)


def _host_consts(S_):
    cst = np.zeros((128, NCST), np.float32)
    cst[:, 0:128] = np.eye(128)
    bd = np.zeros((128, 128), np.float32)
    bd[:64, :64] = 1
    bd[64:, 64:] = 1
    cst[:, 128:256] = bd
    p = np.arange(128)[:, None]
    f = np.arange(128)[None, :]
    cst[:, 256:384] = (p < f)
    cst[:, 384:512] = (p <= f)
    cst[:, 512:640] = (p > f)
    cst[:, 640:768] = (p >= f)
    cst[:64, 768] = 1
    cst[64:, 769] = 1
    sw = np.zeros((64, 64), np.float32)
    for m in range(32):
        sw[m + 32, m] = -1.0
        sw[m, m + 32] = 1.0
    cst[:64, 770:834] = sw
    inv = (1.0 / (np.float32(10000.0) ** (np.arange(0, 64, 2, dtype=np.float32) / np.float32(64)))).astype(np.float32)
    ang = np.arange(S_, dtype=np.float32)[:, None] * inv[None, :]
    cos = np.cos(ang).astype(np.float32).T
    sin = np.sin(ang).astype(np.float32).T
    return cst, np.ascontiguousarray(np.concatenate([cos, cos], 0)), np.ascontiguousarray(np.concatenate([sin, sin], 0))


def _pack_cols(I, L):
    out = np.zeros((L, 128, NCOL), np.float32)
    for l in range(L):
        def put(name, v):
            a = _fm(v)
            out[l, :, CO[name]:CO[name] + a.shape[1]] = a
        put("norm_mix_g", I["norm_mix_g"][l])
        put("norm_ffn_g", I["norm_ffn_g"][l])
        put("final_norm_g", I["final_norm_g"])
        put("ada_b", I["ada_b"][l])
        put("q_norm_g", I["q_norm_g"][l])
        put("kv_norm_g", I["kv_norm_g"][l])
        put("mu0", I["rwkv_mu"][l, 0])
        put("mu1", I["rwkv_mu"][l, 1])
        for d in range(2):
            put(f"w0_{d}", I["rwkv_w0"][l, d])
            put(f"a0_{d}", I["rwkv_a0"][l, d])
            put(f"r_k_{d}", I["rwkv_r_k"][l, d])
        put("k_k", I["rwkv_k_k"][l])
        put("k_a", I["rwkv_k_a"][l])
        for i in range(3):
            put(f"cw{i}", I["conv_w"][l, i])
        put("cb", I["conv_b"][l])
    return out


_NC_CACHE = {}


def run_groups(I, xs, cs, S_, NSEQ, L, n_cores):
    key = (S_, NSEQ, L)
    if key not in _NC_CACHE:
        _NC_CACHE[key] = build(S_, NSEQ, L)
    nc = _NC_CACHE[key]
    cst, rc, rs = _host_consts(S_)
    f32 = lambda a: np.ascontiguousarray(np.asarray(a, np.float32))
    shared = {
        "ada_w": f32(I["ada_w"]), "w_in": f32(I["w_in"]), "w_uq": f32(I["w_uq"]), "w_ukv": f32(I["w_ukv"]),
        "w_o_att": f32(I["w_o_att"]), "wdu": f32(I["rwkv_w_decay_up"]).reshape(L, 128, 1024),
        "wiu": f32(I["rwkv_w_iclr_up"]).reshape(L, 128, 1024), "wgu": f32(I["rwkv_w_gate_up"]),
        "w_o_rwkv": f32(I["w_o_rwkv"]), "w_out": f32(I["w_out"]), "w_up": f32(I["w_ffn_up"]), "w_down": f32(I["w_ffn_down"]),
        "gn_g": f32(I["rwkv_gn_g"]).reshape(L, 1, 1024), "gn_b": f32(I["rwkv_gn_b"]).reshape(L, 1, 1024),
        "cols": _pack_cols(I, L), "cst": cst, "ropec": rc, "ropes": rs,
    }
    in_maps = []
    for x, c in zip(xs, cs):
        m = dict(shared)
        m["xT"] = np.ascontiguousarray(np.transpose(x, (0, 2, 1)))
        m["c_fm"] = np.ascontiguousarray(np.transpose(np.asarray(c, np.float32).reshape(NSEQ, KD, 128), (2, 1, 0)))
        in_maps.append(m)
    res = run_bass_kernel_spmd(nc, in_maps, core_ids=list(range(n_cores)))
    return [np.ascontiguousarray(np.transpose(r["yT"], (0, 2, 1))) for r in res.results]


def kernel(**I):
    xp, xs_ = np.asarray(I["x_prompt"], np.float32), np.asarray(I["x_sample"], np.float32)
    cp, cs_ = np.asarray(I["c_prompt"], np.float32), np.asarray(I["c_sample"], np.float32)
    S_ = xp.shape[1]
    L = I["ada_w"].shape[0]
    allx = np.concatenate([xp, xs_], 0)
    allc = np.concatenate([cp, cs_], 0)
    n = allx.shape[0]
    NSEQ = 2
    assign = [[0, 1], [2, 3], [4, 5], [6, 7], [8, 8], [9, 9], [10, 10], [11, 11]]
    xs = [allx[a] for a in assign]
    cs = [allc[a] for a in assign]
    outs = run_groups(I, xs, cs, S_, NSEQ, L, 8)
    y = np.zeros_like(allx)
    for a, o in zip(assign, outs):
        for j, si in enumerate(a):
            y[si] = o[j]
    return (y[:xp.shape[0]], y[xp.shape[0]:])
```
